# Optimizing a Trainium2 kernel written in Bass

```python
import math
import jax, jax.numpy as jnp
from jax import lax
import numpy as np

D_MODEL = 1024
BATCH = 4
SEQ = 8192
DEPTH = 2

CONV_DIM = D_MODEL
CONV_KERNEL = 31
SSM_EXPAND = 2
SSM_DIM = SSM_EXPAND * D_MODEL
SSM_HEAD_DIM = 64
SSM_HEADS = SSM_DIM // SSM_HEAD_DIM
SSM_GROUPS = 4
SSM_STATE = 128
SSM_CONV = 4
SSM_CHUNK = 128
SSM_BC = 2 * SSM_GROUPS * SSM_STATE
FFN_DIM = 2816
FFN_CONV = 3
DN_ALPHA = (2 * DEPTH) ** 0.25
DN_BETA = (8 * DEPTH) ** -0.25
LN_EPS = 1e-5
RMS_EPS = 1e-5
IN_SIZES = (2 * CONV_DIM, SSM_DIM, SSM_DIM + SSM_BC, SSM_HEADS, D_MODEL, D_MODEL)
IN_DIM = sum(IN_SIZES)

kernel_name = "hybrid_conformer_ssd_deepnorm"


def layer_norm(x, g, b):
    xf = x.astype(jnp.float32)
    mu = jnp.mean(xf, axis=-1, keepdims=True)
    var = jnp.mean(jnp.square(xf - mu), axis=-1, keepdims=True)
    return ((xf - mu) * lax.rsqrt(var + LN_EPS) * g.astype(jnp.float32)
            + b.astype(jnp.float32)).astype(x.dtype)


def causal_dwconv(x, w, b):
    k = w.shape[0]
    y = lax.conv_general_dilated(
        x, w[:, None, :].astype(x.dtype), window_strides=(1,),
        padding=[(k - 1, 0)], dimension_numbers=("NWC", "WIO", "NWC"),
        feature_group_count=x.shape[-1])
    return y + b.astype(x.dtype)


def split_in(u):
    idx = [int(v) for v in np.cumsum(IN_SIZES)[:-1]]
    return jnp.split(u, idx, axis=-1)


def conformer_branch(u_glu, dw_w, dw_b, ln_g, ln_b, w_out):
    a, g = jnp.split(u_glu, 2, axis=-1)
    v = a * jax.nn.sigmoid(g)
    v = causal_dwconv(v, dw_w, dw_b)
    v = jax.nn.silu(layer_norm(v, ln_g, ln_b))
    return v @ w_out


def ssd_chunked(x, dt, a_head, bm, cm):
    b, t, h, p = x.shape
    g, n = bm.shape[2], bm.shape[3]
    r = h // g
    L = SSM_CHUNK
    c = t // L
    xs = (x * dt[..., None]).reshape(b, c, L, g, r, p)
    a = (dt * a_head).reshape(b, c, L, g, r).transpose(0, 1, 3, 4, 2)
    bc = bm.reshape(b, c, L, g, n)
    cc = cm.reshape(b, c, L, g, n)
    a_cs = jnp.cumsum(a, axis=-1)
    seg = a_cs[..., :, None] - a_cs[..., None, :]
    causal = jnp.tril(jnp.ones((L, L), dtype=bool))
    decay = jnp.where(causal, jnp.exp(jnp.where(causal, seg, 0.0)), 0.0)
    cb = jnp.einsum("bclgn,bcsgn->bcgls", cc, bc)
    y_diag = jnp.einsum("bcgrls,bcsgrp->bclgrp", cb[:, :, :, None] * decay, xs)
    decay_states = jnp.exp(a_cs[..., -1:] - a_cs)
    states = jnp.einsum("bclgn,bcgrl,bclgrp->bcgrpn", bc, decay_states, xs)
    chunk_decay = jnp.exp(a_cs[..., -1])

    def step(carry, inp):
        s_c, d_c = inp
        return d_c[..., None, None] * carry + s_c, carry

    init = jnp.zeros((b, g, r, p, n), states.dtype)
    _, prev = lax.scan(step, init, (jnp.moveaxis(states, 1, 0),
                                    jnp.moveaxis(chunk_decay, 1, 0)))
    prev = jnp.moveaxis(prev, 0, 1)
    y_off = jnp.einsum("bclgn,bcgrpn,bcgrl->bclgrp", cc, prev, jnp.exp(a_cs))
    return (y_diag + y_off).reshape(b, t, h, p)


def ssd_branch(z, xbc, dt_raw, conv_w, conv_b, dt_bias, a_log, d_skip, norm_w, w_out):
    bsz, t, _ = z.shape
    xbc = jax.nn.silu(causal_dwconv(xbc, conv_w, conv_b))
    gn = SSM_GROUPS * SSM_STATE
    xs = xbc[..., :SSM_DIM]
    bm = xbc[..., SSM_DIM:SSM_DIM + gn].reshape(bsz, t, SSM_GROUPS, SSM_STATE)
    cm = xbc[..., SSM_DIM + gn:].reshape(bsz, t, SSM_GROUPS, SSM_STATE)
    dt = jax.nn.softplus(dt_raw.astype(jnp.float32) + dt_bias.astype(jnp.float32))
    a_head = -jnp.exp(a_log.astype(jnp.float32))
    xh = xs.reshape(bsz, t, SSM_HEADS, SSM_HEAD_DIM).astype(jnp.float32)
    y = ssd_chunked(xh, dt, a_head, bm.astype(jnp.float32), cm.astype(jnp.float32))
    y = y + xh * d_skip.astype(jnp.float32)[:, None]
    yg = (y.reshape(bsz, t, SSM_DIM) * jax.nn.silu(z.astype(jnp.float32)))
    yg = yg.reshape(bsz, t, SSM_GROUPS, SSM_DIM // SSM_GROUPS)
    yg = yg * lax.rsqrt(jnp.mean(jnp.square(yg), axis=-1, keepdims=True) + RMS_EPS)
    yn = (yg.reshape(bsz, t, SSM_DIM) * norm_w.astype(jnp.float32)).astype(z.dtype)
    return yn @ w_out


def conv_ffn(h, w_up, dw_w, dw_b, w_down):
    u = causal_dwconv(h @ w_up, dw_w, dw_b)
    gate, val = jnp.split(u, 2, axis=-1)
    return (jax.nn.silu(gate) * val) @ w_down


def setup_inputs(seed: int = 0) -> dict:
    key = jax.random.key(seed)
    ks = jax.random.split(key, 32)
    f32 = jnp.float32
    nrm = lambda k, shape, s: (jax.random.normal(k, shape, f32) * s).astype(f32)
    gain = lambda k, shape: 1.0 + 0.05 * jax.random.normal(k, shape, f32)
    small = lambda k, shape: 0.02 * jax.random.normal(k, shape, f32)
    dt0 = jnp.exp(jax.random.uniform(ks[12], (DEPTH, SSM_HEADS), f32)
                  * (math.log(0.1) - math.log(0.001)) + math.log(0.001))
    return {
        "x": jax.random.normal(ks[0], (BATCH, SEQ, D_MODEL), f32),
        "ln_in_g": gain(ks[1], (D_MODEL,)),
        "ln_in_b": small(ks[2], (D_MODEL,)),
        "w_in": nrm(ks[3], (DEPTH, D_MODEL, IN_DIM), D_MODEL ** -0.5),
        "conv_dw_w": nrm(ks[4], (DEPTH, CONV_KERNEL, CONV_DIM), CONV_KERNEL ** -0.5),
        "conv_dw_b": small(ks[5], (DEPTH, CONV_DIM)),
        "conv_ln_g": gain(ks[6], (DEPTH, CONV_DIM)),
        "conv_ln_b": small(ks[7], (DEPTH, CONV_DIM)),
        "w_conv_out": nrm(ks[8], (DEPTH, CONV_DIM, D_MODEL), DN_BETA * CONV_DIM ** -0.5),
        "ssm_conv_w": nrm(ks[9], (DEPTH, SSM_CONV, SSM_DIM + SSM_BC), SSM_CONV ** -0.5),
        "ssm_conv_b": small(ks[10], (DEPTH, SSM_DIM + SSM_BC)),
        "ssm_dt_bias": dt0 + jnp.log(-jnp.expm1(-dt0)),
        "ssm_a_log": jnp.log(jax.random.uniform(ks[13], (DEPTH, SSM_HEADS), f32, 1.0, 16.0)),
        "ssm_d": gain(ks[14], (DEPTH, SSM_HEADS)),
        "ssm_norm_w": gain(ks[15], (DEPTH, SSM_DIM)),
        "w_ssm_out": nrm(ks[16], (DEPTH, SSM_DIM, D_MODEL), DN_BETA * SSM_DIM ** -0.5),
        "w_o": nrm(ks[17], (DEPTH, D_MODEL, D_MODEL), DN_BETA * D_MODEL ** -0.5),
        "ln1_g": gain(ks[18], (DEPTH, D_MODEL)),
        "ln1_b": small(ks[19], (DEPTH, D_MODEL)),
        "w_ffn_up": nrm(ks[20], (DEPTH, D_MODEL, 2 * FFN_DIM), DN_BETA * D_MODEL ** -0.5),
        "ffn_dw_w": nrm(ks[21], (DEPTH, FFN_CONV, 2 * FFN_DIM), FFN_CONV ** -0.5),
        "ffn_dw_b": small(ks[22], (DEPTH, 2 * FFN_DIM)),
        "w_ffn_down": nrm(ks[23], (DEPTH, FFN_DIM, D_MODEL), DN_BETA * FFN_DIM ** -0.5),
        "ln2_g": gain(ks[24], (DEPTH, D_MODEL)),
        "ln2_b": small(ks[25], (DEPTH, D_MODEL)),
    }


def reference(x, ln_in_g, ln_in_b, w_in, conv_dw_w, conv_dw_b, conv_ln_g, conv_ln_b,
              w_conv_out, ssm_conv_w, ssm_conv_b, ssm_dt_bias, ssm_a_log, ssm_d,
              ssm_norm_w, w_ssm_out, w_o, ln1_g, ln1_b, w_ffn_up, ffn_dw_w, ffn_dw_b,
              w_ffn_down, ln2_g, ln2_b):
    h = layer_norm(x, ln_in_g, ln_in_b)
    for l in range(DEPTH):
        u = h @ w_in[l]
        u_glu, z, xbc, dt_raw, gate_a, gate_b = split_in(u)
        y_a = conformer_branch(u_glu, conv_dw_w[l], conv_dw_b[l], conv_ln_g[l],
                               conv_ln_b[l], w_conv_out[l])
        y_b = ssd_branch(z, xbc, dt_raw, ssm_conv_w[l], ssm_conv_b[l], ssm_dt_bias[l],
                         ssm_a_log[l], ssm_d[l], ssm_norm_w[l], w_ssm_out[l])
        mix = (jax.nn.sigmoid(gate_a) * y_a + jax.nn.sigmoid(gate_b) * y_b) @ w_o[l]
        h = layer_norm(DN_ALPHA * h + mix, ln1_g[l], ln1_b[l])
        ffn = conv_ffn(h, w_ffn_up[l], ffn_dw_w[l], ffn_dw_b[l], w_ffn_down[l])
        h = layer_norm(DN_ALPHA * h + ffn, ln2_g[l], ln2_b[l])
    return h
```

```python
import contextlib
import numpy as np
import concourse.bass as bass
import concourse.mybir as mybir
from concourse.bass_utils import run_bass_kernel_spmd

F32 = mybir.dt.float32
BF16 = mybir.dt.bfloat16
ALU = mybir.AluOpType
AF = mybir.ActivationFunctionType

CELL = 512
D = 1024
IN_DIM = 9248
TT = 512
DEPTH = 2
ALPHA = float((2 * DEPTH) ** 0.25)
LN_EPS = 1e-5
RMS_EPS = 1e-5


def _dtsize(dt):
    s = str(dt)
    if '64' in s:
        return 8
    if '32' in s:
        return 4
    if '16' in s:
        return 2
    return 1


class Sched:
    ENGS = ('pe', 'act', 'dve', 'pool', 'sp')

    def __init__(self, nc, same_engine_sync=True):
        self.nc = nc
        self.same_engine_sync = same_engine_sync
        self.streams = {e: [] for e in self.ENGS}
        self.count = {e: 0 for e in self.ENGS}
        self.dma_count = {}
        self.cells = {}
        self.waited = {e: {} for e in self.ENGS}
        self.n_ops = 0
        self.tag = ''
        self.tags = {e: [] for e in self.ENGS}

    def _keys(self, r):
        if isinstance(r, tuple):
            return [r]
        ap = r
        name = ap.tensor.name
        sp_ = str(ap.space).upper()
        if 'DRAM' in sp_ or 'PSUM' in sp_:
            return [(name,)]
        pairs = ap.ap
        pstep = pairs[0][0]
        sz = _dtsize(ap.dtype)
        off = ap.offset % pstep if pstep > 0 else ap.offset
        hull = 0
        for (st, cn) in pairs[1:]:
            hull += abs(st) * (cn - 1)
        lo = off * sz
        hi = (off + hull + 1) * sz
        return [(name, c) for c in range(lo // CELL, (hi - 1) // CELL + 1)]

    def _deps(self, reads, writes):
        deps = {}
        rk = []
        for r in reads:
            rk += self._keys(r)
        wk = []
        for w in writes:
            wk += self._keys(w)
        cells = self.cells
        for k in rk:
            c = cells.get(k)
            if c is not None and c[0] is not None:
                kk, vv = c[0]
                if deps.get(kk, 0) < vv:
                    deps[kk] = vv
        for k in wk:
            c = cells.get(k)
            if c is not None:
                if c[0] is not None:
                    kk, vv = c[0]
                    if deps.get(kk, 0) < vv:
                        deps[kk] = vv
                for kk, vv in c[1].items():
                    if deps.get(kk, 0) < vv:
                        deps[kk] = vv
        return deps, rk, wk

    def _commit(self, rk, wk, tok):
        cells = self.cells
        for k in rk:
            c = cells.get(k)
            if c is None:
                c = [None, {}]
                cells[k] = c
            if c[1].get(tok[0], 0) < tok[1]:
                c[1][tok[0]] = tok[1]
        for k in wk:
            cells[k] = [tok, {}]

    def _filter(self, eng, deps):
        waits = []
        wd = self.waited[eng]
        for k, v in deps.items():
            if k == eng and (eng == 'pe' or not self.same_engine_sync):
                continue
            if wd.get(k, 0) >= v:
                continue
            wd[k] = v
            waits.append((k, v))
        return waits

    def op(self, eng, fn, reads=(), writes=()):
        deps, rk, wk = self._deps(reads, writes)
        waits = self._filter(eng, deps)
        self.count[eng] += 1
        tok = (eng, self.count[eng])
        self.tags[eng].append(self.tag)
        self.streams[eng].append((waits, fn, tok))
        self._commit(rk, wk, tok)
        self.n_ops += 1
        return tok

    def dma(self, queue, fn, sem, reads=(), writes=()):
        deps, rk, wk = self._deps(reads, writes)
        waits = self._filter(queue, deps)
        self.dma_count[sem] = self.dma_count.get(sem, 0) + 1
        tok = (sem, 16 * self.dma_count[sem])
        self.streams[queue].append((waits, fn, tok))
        self._commit(rk, wk, tok)
        self.n_ops += 1
        return tok

    def wait_all(self, eng, toks):
        deps = {}
        for k, v in toks:
            deps[k] = max(deps.get(k, 0), v)
        waits = self._filter(eng, deps)
        if waits:
            self.streams[eng].append((waits, None, None))

    def emit(self):
        nc = self.nc
        semnames = list(self.ENGS) + list(self.dma_count.keys())
        with contextlib.ExitStack() as st:
            sems = {}
            for n in semnames:
                sems[n] = st.enter_context(nc.semaphore("s_" + n))
            block = st.enter_context(nc.Block())
            engmap = {'pe': block.tensor, 'act': block.scalar, 'dve': block.vector,
                      'pool': block.gpsimd, 'sp': block.sync}

            def make(ename):
                stream = self.streams[ename]

                def body(e):
                    for waits, fn, tok in stream:
                        for (k, v) in waits:
                            e.wait_ge(sems[k], v)
                        if fn is None:
                            continue
                        ins = fn(e)
                        if tok[0] == ename:
                            ins.then_inc(sems[tok[0]], 1)
                        else:
                            ins.then_inc(sems[tok[0]], 16)
                return body
            for ename in self.ENGS:
                if self.streams[ename]:
                    engmap[ename](make(ename))


CW, CB, CG, CBE = 0, 248, 256, 264
SW, SB, NW = 272, 368, 392
FW, FB = 408, 540
L1G, L1B, L2G, L2B = 584, 592, 600, 608
LIG, LIB = 616, 624
NPP = 632

def _slot_table():
    s = []
    A = lambda name, k0, nk, c0, ncol: s.append((name, k0, nk, c0, ncol))
    A('w_in', 0, 8, 0, 512)
    A('w_in', 0, 8, 1024, 512)
    for c in range(4):
        A('diag', c, 8, 0, 496)
    A('w_in', 0, 8, 512, 512)
    A('w_in', 0, 8, 1536, 512)
    for c in range(4, 8):
        A('diag', c, 8, 0, 496)
    A('w_in', 0, 8, 7168, 32)
    A('w_in', 0, 8, 6144, 512)
    A('w_in', 0, 8, 6656, 512)
    for g in range(4):
        A('w_in', 0, 8, 4096 + 512 * g, 512)
        A('w_in', 0, 8, 2048 + 512 * g, 512)
    for ob in range(2):
        A('w_in', 0, 8, 7200 + 512 * ob, 512)
        A('w_conv_out', 0, 8, 512 * ob, 512)
        A('w_ssm_out', 0, 8, 512 * ob, 512)
        A('w_ssm_out', 8, 8, 512 * ob, 512)
        A('w_in', 0, 8, 8224 + 512 * ob, 512)
    for ob in range(2):
        A('w_o', 0, 8, 512 * ob, 512)
    for b in range(6):
        ncol = 512 if b < 5 else 256
        A('w_ffn_up', 0, 8, 512 * b, ncol)
        A('w_ffn_up', 0, 8, 2816 + 512 * b, ncol)
    for ob in range(2):
        A('w_ffn_down', 0, 8, 512 * ob, 512)
        A('w_ffn_down', 8, 8, 512 * ob, 512)
        A('w_ffn_down', 16, 6, 512 * ob, 512)
    return s


SLOTS = _slot_table()
NSL = len(SLOTS)
NRING = 4
NPREP = 4


def build_program(T, layers, apply_ln_in, conv_pool_chunks=(), same_engine_sync=False, dbg_stop=0, dbg_var=0, use_pool=False):
    PL = 'pool' if use_pool else 'dve'
    NT = T // TT
    NL = len(layers)
    nc = bass.Bass("TRN2", target_bir_lowering=False)
    S = Sched(nc, same_engine_sync=same_engine_sync)

    def din(name, shape):
        return nc.dram_tensor(name, shape, F32, kind="ExternalInput").ap()
    x_d = din("x", [T, D])
    wdr = {
        'w_in': din("w_in", [2, D, IN_DIM]),
        'w_conv_out': din("w_conv_out", [2, D, D]),
        'w_ssm_out': din("w_ssm_out", [2, 2048, D]),
        'w_o': din("w_o", [2, D, D]),
        'w_ffn_up': din("w_ffn_up", [2, D, 5632]),
        'w_ffn_down': din("w_ffn_down", [2, 2816, D]),
    }
    pp_d = din("pp", [128, 2, NPP])
    rowp_d = din("rowp", [2, 3, 32])
    cst_d = din("cst", [128, 5, 128])
    y_d = nc.dram_tensor("y", [T, D], F32, kind="ExternalOutput").ap()
    wscr = nc.dram_tensor("wscr", [2, NSL, 128, 4096], BF16, kind="Internal").ap()
    dscr = nc.dram_tensor("dscr", [2, 8, 128, 3968], BF16, kind="Internal").ap()

    with contextlib.ExitStack() as st:
        def sb(name, shape, dt=F32):
            return st.enter_context(nc.sbuf_tensor("sb_" + name, shape, dt))

        def psb(name):
            return st.enter_context(nc.psum_tensor(name, [128, 512], F32))
        cst = sb("cst", [128, 5, 128])
        IDENT, UINCL, ONES, USTRICT, MASKT = (cst[:, i, :] for i in range(5))
        identb = sb("identb", [128, 128], BF16)
        onesN = sb("onesN", [128, 128])
        pp = sb("pp", [128, 2, NPP])
        rowc = sb("rowc", [128, 2, 3, 32])
        epsb = sb("epsb", [128, 2])
        ST = sb("ST", [128, 2, 4, 512])
        STb = sb("STb", [128, 2, 4, 512], BF16)
        vhalo = sb("vhalo", [128, 2, 8, 30], BF16)
        xbch = sb("xbch", [128, 2, 24, 3])
        ffnh = sb("ffnh", [128, 2, 44, 2])
        hT = sb("hT", [128, 8, 512])
        hTb = sb("hTb", [128, 8, 512], BF16)
        wring = sb("wring", [128, NRING, 4096], BF16)
        arena = sb("arena", [128, 9216])
        sT = sb("sT", [128, 8, 512], BF16)
        BT = sb("BT", [128, 4, 512], BF16)
        CT = sb("CT", [128, 4, 512], BF16)
        Btok = sb("Btok", [128, 4, 512], BF16)
        dts = sb("dts", [128, 8, 128])
        xs_tok = sb("xs_tok", [128, 4, 512])
        zs = sb("zs", [128, 2, 4, 512], BF16)
        cbm = sb("cbm", [128, 2, 128])
        tA = sb("tA", [128, 2, 512])
        xsT2 = sb("xsT2", [128, 2, 512])
        t2b = sb("t2b", [128, 2, 512])
        tB = sb("tB", [128, 2, 512])
        tC = sb("tC", [128, 2, 512])
        xpre = sb("xpre", [128, 2, 516])
        ynb = sb("ynb", [128, 2, 512], BF16)
        xsd = sb("xsd", [128, 2, 512], BF16)
        xsw = sb("xsw", [128, 2, 512], BF16)
        sml = sb("sml", [128, 8])
        banks = [psb("ps%d" % i) for i in range(8)]

        vT = arena[:, 0:4 * 542].bitcast(BF16).rearrange("p (c t) -> p c t", c=8)
        convT = arena[:, 2176:2176 + 4096].rearrange("p (c t) -> p c t", c=8)
        stat = arena[:, 8192:9216].rearrange("p (c t) -> p c t", c=2)
        ynT = arena[:, 0:4096].bitcast(BF16).rearrange("p (c t) -> p c t", c=16)
        mixTb = arena[:, 4096:6144].bitcast(BF16).rearrange("p (c t) -> p c t", c=8)
        mixa = arena[:, 6144:8192].rearrange("p (c t) -> p c t", c=4)
        Apr2 = [arena[:, 6144 + 1024 * i:6144 + 1024 * (i + 1)] for i in range(2)]
        MT2 = [arena[:, 8192 + 512 * i:8192 + 512 * (i + 1)].bitcast(BF16).rearrange("p (h l) -> p h l", h=8)
               for i in range(2)]
        actT = arena[:, 0:5632].bitcast(BF16).rearrange("p (c t) -> p c t", c=22)
        upre = arena[:, 5632:5632 + 4 * 516].rearrange("p (c t) -> p c t", c=4)
        xio = arena[:, 4096:8192].rearrange("p (j c) -> p j c", j=4)

        class Cyc:
            def __init__(self, items):
                self.items = items
                self.i = 0
                self.held = set()

            def next(self):
                for _ in range(len(self.items) + 1):
                    k = self.i % len(self.items)
                    self.i += 1
                    if k not in self.held:
                        return self.items[k]
                raise RuntimeError("all held")

            def hold_next(self):
                for _ in range(len(self.items) + 1):
                    k = self.i % len(self.items)
                    self.i += 1
                    if k not in self.held:
                        self.held.add(k)
                        return k, self.items[k]
                raise RuntimeError("all held")

            def release(self, k):
                self.held.discard(k)
        P = Cyc(banks[:6])
        PS_MEAN, PS_EX2 = banks[6], banks[7]
        TA = Cyc([tA[:, i, :] for i in range(2)])
        XS = Cyc([xsT2[:, i, :] for i in range(2)])
        TB = Cyc([tB[:, i, :] for i in range(2)])
        TC = Cyc([tC[:, i, :] for i in range(2)])
        XP = Cyc([xpre[:, i, :] for i in range(2)])
        UP = Cyc([upre[:, i, :] for i in range(4)])
        YN = Cyc([ynb[:, i, :] for i in range(2)])
        XSD = Cyc([xsd[:, i, :] for i in range(2)])
        XSW = Cyc([xsw[:, i, :] for i in range(2)])

        def mm(out, lhsT, rhs, start, stop):
            S.op('pe', lambda e: e.matmul(out, lhsT=lhsT, rhs=rhs, start=start, stop=stop),
                 reads=[lhsT, rhs], writes=[out])

        def tr(out, in_, ident):
            S.op('pe', lambda e: e.transpose(out, in_, ident), reads=[in_, ident], writes=[out])

        def act(out, in_, func, bias=None, scale=None, accum=None, eng='act'):
            kw = {}
            rd = [in_]
            wr = [out]
            if bias is not None:
                kw['bias'] = bias
                if not isinstance(bias, float):
                    rd.append(bias)
            if scale is not None:
                kw['scale'] = scale
                if not isinstance(scale, float):
                    rd.append(scale)
            if accum is not None:
                kw['accum_out'] = accum
                wr.append(accum)
            S.op('act', lambda e: e.activation(out=out, in_=in_, func=func, **kw), reads=rd, writes=wr)

        def tt(eng, out, in0, in1, op):
            S.op(eng, lambda e: e.tensor_tensor(out=out, in0=in0, in1=in1, op=op), reads=[in0, in1], writes=[out])

        def ts(eng, out, in0, s1, s2, op0, op1=None):
            rd = [in0] + [s for s in (s1, s2) if s is not None and not isinstance(s, float)]
            if op1 is None:
                S.op(eng, lambda e: e.tensor_scalar(out=out, in0=in0, scalar1=s1, scalar2=None, op0=op0), reads=rd, writes=[out])
            else:
                S.op(eng, lambda e: e.tensor_scalar(out=out, in0=in0, scalar1=s1, scalar2=s2, op0=op0, op1=op1), reads=rd, writes=[out])

        def stt(eng, out, in0, scalar, in1, op0, op1):
            rd = [in0, in1] + ([] if isinstance(scalar, float) else [scalar])
            S.op(eng, lambda e: e.scalar_tensor_tensor(out=out, in0=in0, scalar=scalar, in1=in1, op0=op0, op1=op1),
                 reads=rd, writes=[out])

        def cp(eng, out, in_):
            if eng == 'act':
                S.op('act', lambda e: e.copy(out=out, in_=in_), reads=[in_], writes=[out])
            else:
                S.op(eng, lambda e: e.tensor_copy(out=out, in_=in_), reads=[in_], writes=[out])

        def memset(eng, ap, v):
            S.op(eng, lambda e: e.memset(ap, v), writes=[ap])

        n_prep = 0
        for l in (layers if dbg_stop not in (11, 13) else []):
            for s, (wn, k0, nk, c0, ncol) in enumerate(SLOTS):
                if wn == 'diag':
                    continue
                src = wdr[wn][l, k0 * 128:(k0 + nk) * 128, c0:c0 + ncol].rearrange("(k p) c -> p k c", p=128)
                dst = wscr[l, s].rearrange("p (k c) -> p k c", k=8)[:, 0:nk, 0:ncol]
                psem = 'prep%d' % (n_prep % NPREP)
                if n_prep >= NPREP:
                    S.wait_all('pool', [(psem, 16 * (n_prep // NPREP))])
                S.dma('pool', lambda e, src=src, dst=dst: e.dma_start(out=dst, in_=src), psem,
                      reads=[], writes=[('wscr', l, s)])
                n_prep += 1

        if n_prep:
            S.wait_all('pool', [('prep%d' % i, 16 * ((n_prep - 1 - i) // NPREP + 1)) for i in range(min(NPREP, n_prep))])
        uses = [(l, s) for t in range(NT) for l in layers for s in range(NSL)]
        wstate = {'issued': 0, 'n': 0}

        def wget():
            n = wstate['n']
            while wstate['issued'] < min(len(uses), max(n + NRING - 1, 2)):
                m = wstate['issued']
                l, s = uses[m]
                wn_, k0_, nk_, _, ncol_ = SLOTS[s]
                if wn_ == 'diag':
                    dst = wring[:, m % NRING, 0:3968]
                    src = dscr[l, k0_]
                    rkey = ('dscr', l, k0_)
                else:
                    dst = wring[:, m % NRING, :].rearrange("p (k c) -> p k c", k=8)[:, 0:nk_, 0:ncol_]
                    src = wscr[l, s].rearrange("p (k c) -> p k c", k=8)[:, 0:nk_, 0:ncol_]
                    rkey = ('wscr', l, s)
                S.dma('sp', lambda e, src=src, dst=dst: e.dma_start(out=dst, in_=src), 'wld%d' % (m % NRING),
                      reads=[rkey], writes=[wring[:, m % NRING, :]])
                wstate['issued'] += 1
            wstate['n'] += 1
            if SLOTS[uses[n][1]][0] == 'diag':
                return wring[:, n % NRING, 0:3968].rearrange("p (k m) -> p k m", k=31)
            return wring[:, n % NRING, :].rearrange("p (k c) -> p k c", k=8)

        S.dma('sp', lambda e: e.dma_start(out=cst[:], in_=cst_d), 'ld0', reads=[cst_d], writes=[cst[:]])
        S.dma('sp', lambda e: e.dma_start(out=pp[:], in_=pp_d), 'ld1', reads=[pp_d], writes=[pp[:]])
        S.dma('sp', lambda e: e.dma_start(out=rowc[:].rearrange("p a b c -> p (a b c)"),
                                          in_=rowp_d.rearrange("a b c -> (a b c)").partition_broadcast(128)),
              'ld2', reads=[rowp_d], writes=[rowc[:]])
        cp('dve', identb[:], IDENT)
        ts('dve', onesN[:], ONES, 1.0 / 1024.0, None, ALU.mult)
        memset('dve', epsb[:, 0:1], LN_EPS)
        memset('dve', epsb[:, 1:2], RMS_EPS)
        memset('dve', ST[:], 0.0)
        memset('dve', STb[:], 0.0)
        memset('dve', vhalo[:], 0.0)
        memset('dve', xbch[:], 0.0)
        memset('dve', ffnh[:], 0.0)
        for l in layers:
            act(rowc[:, l, 1, :], rowc[:, l, 1, :], AF.Exp)
            ts('dve', rowc[:, l, 1, :], rowc[:, l, 1, :], -1.0, None, ALU.mult)

        nbuilt = 0
        for l in layers:
            for c in range(8):
                stg = wring[:, nbuilt % NRING, 0:3968]
                tt('dve', stg.rearrange("p (k m) -> p k m", k=31),
                   IDENT.unsqueeze(1).to_broadcast([128, 31, 128]),
                   pp[:, l, CW + c * 31:CW + c * 31 + 31].unsqueeze(2).to_broadcast([128, 31, 128]), ALU.mult)
                S.dma('sp', lambda e, stg=stg, l=l, c=c: e.dma_start(out=dscr[l, c], in_=stg), 'dst%d' % (nbuilt % NRING),
                      reads=[stg], writes=[('dscr', l, c)])
                nbuilt += 1

        def ln_stats_begin():
            pass

        def ln_finish_stats():
            mean_b = stat[:, 0, :]
            rstd_b = stat[:, 1, :]
            cp('act', mean_b, PS_MEAN[:])
            t = TC.next()
            tt('dve', t, mean_b, mean_b, ALU.mult)
            tt('dve', t, PS_EX2[:], t, ALU.subtract)
            act(t, t, AF.Sqrt, bias=epsb[:, 0:1], scale=1.0)
            S.op('dve', lambda e: e.reciprocal(out=rstd_b, in_=t), reads=[t], writes=[rstd_b])
            return mean_b, rstd_b

        def ln_hT(l, gcol, bcol):
            for c in range(8):
                sq = TA.next()
                act(sq, hT[:, c, :], AF.Square)
                mm(PS_MEAN[:], onesN[:], hT[:, c, :], c == 0, c == 7)
                mm(PS_EX2[:], onesN[:], sq, c == 0, c == 7)
            mean_b, rstd_b = ln_finish_stats()
            for c in range(8):
                t = TC.next()
                tt('dve', t, hT[:, c, :], mean_b, ALU.subtract)
                tt('dve', t, t, rstd_b, ALU.mult)
                act(hT[:, c, :], t, AF.Identity, bias=pp[:, l, bcol + c:bcol + c + 1], scale=pp[:, l, gcol + c:gcol + c + 1])
                cp('act', hTb[:, c, :], hT[:, c, :])

        def conv_from_psum(ps, pool_cyc, halo, wcol, bcol, K, l, out_acc):
            xp = pool_cyc.next()
            H = K - 1
            cp('act', xp[:, H:H + 512], ps)
            cp('act', xp[:, 0:H], halo)
            cp('act', halo, xp[:, 512:512 + H])
            ts('dve', out_acc, xp[:, H:H + 512], pp[:, l, wcol + H:wcol + H + 1], pp[:, l, bcol:bcol + 1], ALU.mult, ALU.add)
            for k in range(H):
                stt('dve', out_acc, xp[:, k:k + 512], pp[:, l, wcol + k:wcol + k + 1], out_acc, ALU.mult, ALU.add)

        def layer_tile(li, l):
            PPl = pp[:, l, :]
            if dbg_stop == 10:
                return
            S.tag = 'p1_conformer'
            cp('act', vT[:, :, 0:30], vhalo[:, li, :, :])
            for cb in range(2):
                wa = wget()
                wg = wget()
                for ci in range(4):
                    c = 4 * cb + ci
                    pa = P.next()
                    for kc in range(8):
                        mm(pa[:], wa[:, kc, ci * 128:(ci + 1) * 128], hTb[:, kc, :], kc == 0, kc == 7)
                    pg = P.next()
                    for kc in range(8):
                        mm(pg[:], wg[:, kc, ci * 128:(ci + 1) * 128], hTb[:, kc, :], kc == 0, kc == 7)
                    sig = TA.next()
                    act(sig, pg[:], AF.Sigmoid)
                    tt('dve', vT[:, c, 30:542], pa[:], sig, ALU.mult)
                    cp('act', vhalo[:, li, c, :], vT[:, c, 512:542])
                for ci in range(4):
                    c = 4 * cb + ci
                    wd_ = wget()
                    pc = P.next()
                    for k in range(31):
                        mm(pc[:], wd_[:, k, :], vT[:, c, k:k + 512], k == 0, k == 30)
                    act(convT[:, c, :], pc[:], AF.Identity, bias=PPl[:, CB + c:CB + c + 1], scale=1.0)
                    sq = TA.next()
                    act(sq, convT[:, c, :], AF.Square)
                    mm(PS_MEAN[:], onesN[:], convT[:, c, :], c == 0, c == 7)
                    mm(PS_EX2[:], onesN[:], sq, c == 0, c == 7)
            if dbg_stop == 3:
                return
            S.tag = 'p1_ln'
            mean_b, rstd_b = ln_finish_stats()
            for c in range(8):
                t = TC.next()
                tt('dve', t, convT[:, c, :], mean_b, ALU.subtract)
                tt('dve', t, t, rstd_b, ALU.mult)
                act(sT[:, c, :], t, AF.Silu, bias=PPl[:, CBE + c:CBE + c + 1], scale=PPl[:, CG + c:CG + c + 1])

            if dbg_stop == 1:
                return
            S.tag = 'p2a_dtBC'
            dtt = dts[:, 0, :].rearrange("p (j h) -> p j h", j=4)
            aa = dts[:, 1, :].rearrange("p (j h) -> p j h", j=4)
            acs = dts[:, 2, :].rearrange("p (j h) -> p j h", j=4)
            eacs = dts[:, 3, :].rearrange("p (j h) -> p j h", j=4)
            cdec = dts[:, 4, :].rearrange("p (j h) -> p j h", j=4)
            w2 = dts[:, 5, :].rearrange("p (j h) -> p j h", j=4)
            dtmp = dts[:, 6, :].rearrange("p (j h) -> p j h", j=4)
            wdt = wget()
            pdt = P.next()
            for j in range(4):
                for kc in range(8):
                    mm(pdt[:, j * 32:(j + 1) * 32], hTb[:, kc, j * 128:(j + 1) * 128], wdt[:, kc, 0:32], kc == 0, kc == 7)
            tt('dve', dtmp, pdt[:, 0:128].rearrange("p (j h) -> p j h", j=4),
               rowc[:, l, 0, :].unsqueeze(1).to_broadcast([128, 4, 32]), ALU.add)
            act(dtmp, dtmp, AF.Exp)
            act(dtt, dtmp, AF.Ln, bias=1.0, scale=1.0)
            tt('dve', aa, dtt, rowc[:, l, 1, :].unsqueeze(1).to_broadcast([128, 4, 32]), ALU.mult)
            if dbg_stop == 41:
                return
            pcs = P.next()
            mm(pcs[:, 0:128], UINCL, dts[:, 1, :], True, True)
            mm(pcs[:, 128:256], ONES, dts[:, 1, :], True, True)
            pcs_cs = pcs[:, 0:128].rearrange("p (j h) -> p j h", j=4)
            pcs_tot = pcs[:, 128:256].rearrange("p (j h) -> p j h", j=4)
            if dbg_stop == 420:
                return
            cp('act', acs, pcs_cs)
            if dbg_stop == 421:
                return
            act(eacs, pcs_cs, AF.Exp)
            act(cdec, pcs_tot, AF.Exp)
            if dbg_stop == 422:
                return
            cp('act', dtmp, pcs_tot)
            tt('dve', dtmp, dtmp, acs, ALU.subtract)
            if dbg_stop == 423:
                return
            act(dtmp, dtmp, AF.Exp)
            if dbg_stop == 424:
                return
            tt('dve', w2, dtt, dtmp, ALU.mult)

            def xbc_chunk(wblk, ci, q, dest, dest_fp32_tmp=False):
                pX = P.next()
                for kc in range(8):
                    mm(pX[:], wblk[:, kc, ci * 128:(ci + 1) * 128], hTb[:, kc, :], kc == 0, kc == 7)
                acc = TB.next()
                conv_from_psum(pX[:], XP, xbch[:, li, q, :], SW + 4 * q, SB + q, 4, l, acc)
                act(dest, acc, AF.Silu)

            if dbg_stop == 42:
                return
            def b_transpose(g):
                ptb = P.next()
                ptb16 = ptb[:].bitcast(BF16)
                for j in range(4):
                    tr(ptb16[:, j * 128:(j + 1) * 128], BT[:, g, j * 128:(j + 1) * 128], identb[:])
                cp('act', Btok[:, :, g * 128:(g + 1) * 128], ptb16[:, 0:512].rearrange("p (j n) -> p j n", j=4))
            wB = wget()
            for g in range(4):
                xbc_chunk(wB, g, 16 + g, BT[:, g, :])
                if g > 0:
                    b_transpose(g - 1)
            wC = wget()
            for g in range(4):
                xbc_chunk(wC, g, 20 + g, CT[:, g, :])
                if g == 0:
                    b_transpose(3)

            if dbg_stop == 4:
                return
            def xs_transpose(xsT, ci):
                ptx = P.next()
                for j in range(4):
                    tr(ptx[:, j * 128:(j + 1) * 128], xsT[:, j * 128:(j + 1) * 128], IDENT)
                cp('act', xs_tok[:, :, ci * 128:(ci + 1) * 128], ptx[:].rearrange("p (j n) -> p j n", j=4))

            def xs_z(g):
                S.tag = 'p3_xs_z'
                wx = wget()
                wz = wget()
                pend = None
                for ci in range(4):
                    xsT = XS.next()
                    xbc_chunk(wx, ci, 4 * g + ci, xsT)
                    if pend is not None:
                        xs_transpose(*pend)
                    pend = (xsT, ci)
                for j in range(4):
                    pz = P.next()
                    for kc in range(8):
                        mm(pz[:], hTb[:, kc, j * 128:(j + 1) * 128], wz[:, kc, :], kc == 0, kc == 7)
                    act(zs[:, g % 2, j, :], pz[:], AF.Silu)
                    if j == 0:
                        xs_transpose(*pend)

            def S1(g, j, k):
                S.tag = 'p3_s1'
                hs = slice(8 * g, 8 * g + 8)
                jb = slice(j * 128, (j + 1) * 128)
                xs3 = xs_tok[:, j, :].rearrange("p (h d) -> p h d", h=8)
                tt('dve', xsd[:, k, :].rearrange("p (h d) -> p h d", h=8), xs3,
                   dtt[:, j, hs].unsqueeze(2).to_broadcast([128, 8, 64]), ALU.mult)
                tt(PL, xsw[:, k, :].rearrange("p (h d) -> p h d", h=8), xs3,
                   w2[:, j, hs].unsqueeze(2).to_broadcast([128, 8, 64]), ALU.mult)
                tt(PL, t2b[:, k, :].rearrange("p (h d) -> p h d", h=8), xs3,
                   rowc[:, l, 2, hs].unsqueeze(2).to_broadcast([128, 8, 64]), ALU.mult)
                pcb = P.next()
                mm(pcb[:, 0:128], BT[:, g, jb], CT[:, g, jb], True, True)
                tt('dve', cbm[:, k, :], pcb[:, 0:128], MASKT, ALU.mult)
                tt('dve', Apr2[k].rearrange("p (h l) -> p h l", h=8),
                   UINCL.unsqueeze(1).to_broadcast([128, 8, 128]),
                   aa[:, j, hs].unsqueeze(2).to_broadcast([128, 8, 128]), ALU.mult)
                for half in range(2):
                    pseg = P.next()
                    mm(pseg[:], USTRICT, Apr2[k][:, half * 512:(half + 1) * 512], True, True)
                    E = TA.next()
                    act(E, pseg[:], AF.Exp)
                    tt('dve', MT2[k][:, 4 * half:4 * half + 4, :], E.rearrange("p (h l) -> p h l", h=4),
                       cbm[:, k, :].unsqueeze(1).to_broadcast([128, 4, 128]), ALU.mult)

            def S2(g, j, k):
                S.tag = 'p3_s2'
                hs = slice(8 * g, 8 * g + 8)
                jb = slice(j * 128, (j + 1) * 128)
                xd = xsd[:, k, :]
                xw = xsw[:, k, :]
                pyd = P.next()
                for hh in range(8):
                    mm(pyd[:, hh * 64:(hh + 1) * 64], MT2[k][:, hh, :], xd[:, hh * 64:(hh + 1) * 64], True, True)
                pyo = P.next()
                mm(pyo[:], CT[:, g, jb], STb[:, li, g, :], True, True)
                pst = P.next()
                mm(pst[:], Btok[:, j, g * 128:(g + 1) * 128], xw, True, True)
                t1 = TC.next()
                tt('dve', t1.rearrange("p (h d) -> p h d", h=8), pyo[:].rearrange("p (h d) -> p h d", h=8),
                   eacs[:, j, hs].unsqueeze(2).to_broadcast([128, 8, 64]), ALU.mult)
                Sg = ST[:, li, g, :]
                tt('dve', Sg.rearrange("p (h d) -> p h d", h=8), Sg.rearrange("p (h d) -> p h d", h=8),
                   cdec[:, j, hs].unsqueeze(2).to_broadcast([128, 8, 64]), ALU.mult)
                tt('dve', Sg, Sg, pst[:], ALU.add)
                cp('act', STb[:, li, g, :], Sg)
                tt('dve', t1, t1, pyd[:], ALU.add)
                tt('dve', t1, t1, t2b[:, k, :], ALU.add)
                tt('dve', t1, t1, zs[:, g % 2, j, :], ALU.mult)
                act(t2b[:, k, :], t1, AF.Square, accum=sml[:, 0:1])
                act(sml[:, 1:2], sml[:, 0:1], AF.Sqrt, bias=epsb[:, 1:2], scale=1.0 / 512.0)
                S.op('dve', lambda e: e.reciprocal(out=sml[:, 2:3], in_=sml[:, 1:2]), reads=[sml[:, 1:2]], writes=[sml[:, 2:3]])
                yn = YN.next()
                act(yn, t1, AF.Copy, scale=sml[:, 2:3])
                pty = P.next()
                pty16 = pty[:].bitcast(BF16)
                for ci in range(4):
                    tr(pty16[:, ci * 128:(ci + 1) * 128], yn[:, ci * 128:(ci + 1) * 128], identb[:])
                for ci in range(4):
                    act(ynT[:, 4 * g + ci, jb], pty16[:, ci * 128:(ci + 1) * 128], AF.Copy,
                        scale=PPl[:, NW + 4 * g + ci:NW + 4 * g + ci + 1])

            iters = [(g, j) for g in range(4) for j in range(4)]
            xs_z(0)
            S1(0, 0, 0)
            for i, (g, j) in enumerate(iters):
                if i + 1 < len(iters):
                    g2, j2 = iters[i + 1]
                    if j2 == 0:
                        xs_z(g2)
                    S1(g2, j2, (i + 1) % 2)
                S2(g, j, i % 2)

            if dbg_stop == 6:
                return
            S.tag = 'p4_out'
            for ob in range(2):
                wga = wget()
                wco = wget()
                for ci in range(4):
                    pga = P.next()
                    for kc in range(8):
                        mm(pga[:], wga[:, kc, ci * 128:(ci + 1) * 128], hTb[:, kc, :], kc == 0, kc == 7)
                    sg = TA.next()
                    act(sg, pga[:], AF.Sigmoid)
                    pya = P.next()
                    for kc in range(8):
                        mm(pya[:], wco[:, kc, ci * 128:(ci + 1) * 128], sT[:, kc, :], kc == 0, kc == 7)
                    tt('dve', mixa[:, ci, :], pya[:], sg, ALU.mult)
                wsa = wget()
                held = [P.hold_next() for _ in range(4)]
                for ci in range(4):
                    for kc in range(8):
                        mm(held[ci][1][:], wsa[:, kc, ci * 128:(ci + 1) * 128], ynT[:, kc, :], kc == 0, False)
                wsb = wget()
                for ci in range(4):
                    for kc in range(8):
                        mm(held[ci][1][:], wsb[:, kc, ci * 128:(ci + 1) * 128], ynT[:, 8 + kc, :], False, kc == 7)
                wgb = wget()
                for ci in range(4):
                    pgb = P.next()
                    for kc in range(8):
                        mm(pgb[:], wgb[:, kc, ci * 128:(ci + 1) * 128], hTb[:, kc, :], kc == 0, kc == 7)
                    sg = TA.next()
                    act(sg, pgb[:], AF.Sigmoid)
                    t = TC.next()
                    tt('dve', t, held[ci][1][:], sg, ALU.mult)
                    tt('dve', mixTb[:, 4 * ob + ci, :], t, mixa[:, ci, :], ALU.add)
                    P.release(held[ci][0])
            for ob in range(2):
                wwo = wget()
                for ci in range(4):
                    oc = 4 * ob + ci
                    po = P.next()
                    for kc in range(8):
                        mm(po[:], wwo[:, kc, ci * 128:(ci + 1) * 128], mixTb[:, kc, :], kc == 0, kc == 7)
                    stt('dve', hT[:, oc, :], hT[:, oc, :], ALPHA, po[:], ALU.mult, ALU.add)
            S.tag = 'ln1'
            ln_hT(l, L1G, L1B)

            if dbg_stop == 7:
                return
            S.tag = 'p5_ffn'
            for b in range(6):
                wgt = wget()
                wvl = wget()
                nci = 4 if b < 5 else 2
                for ci in range(nci):
                    i = 4 * b + ci
                    pg = P.next()
                    for kc in range(8):
                        mm(pg[:], wgt[:, kc, ci * 128:(ci + 1) * 128], hTb[:, kc, :], kc == 0, kc == 7)
                    pv = P.next()
                    for kc in range(8):
                        mm(pv[:], wvl[:, kc, ci * 128:(ci + 1) * 128], hTb[:, kc, :], kc == 0, kc == 7)
                    ag = TB.next()
                    conv_from_psum(pg[:], UP, ffnh[:, li, i, :], FW + 3 * i, FB + i, 3, l, ag)
                    av = TB.next()
                    conv_from_psum(pv[:], UP, ffnh[:, li, 22 + i, :], FW + 3 * (22 + i), FB + 22 + i, 3, l, av)
                    sg = TA.next()
                    act(sg, ag, AF.Silu)
                    tt('dve', actT[:, i, :], sg, av, ALU.mult)
            for ob in range(2):
                held = [P.hold_next() for _ in range(4)]
                for ks in range(3):
                    wd = wget()
                    nk = 8 if ks < 2 else 6
                    for ci in range(4):
                        for kk in range(nk):
                            mm(held[ci][1][:], wd[:, kk, ci * 128:(ci + 1) * 128], actT[:, 8 * ks + kk, :],
                               ks == 0 and kk == 0, ks == 2 and kk == nk - 1)
                for ci in range(4):
                    oc = 4 * ob + ci
                    stt('dve', hT[:, oc, :], hT[:, oc, :], ALPHA, held[ci][1][:], ALU.mult, ALU.add)
                    P.release(held[ci][0])
            S.tag = 'ln2'
            ln_hT(l, L2G, L2B)

        out_toks = []
        for t in range(NT):
            S.tag = 'io_in'
            S.dma('sp', lambda e, t=t: e.dma_start(out=xio, in_=x_d[t * TT:(t + 1) * TT, :].rearrange("(j p) c -> p j c", p=128)),
                  'xld', reads=[x_d], writes=[xio])
            for c in range(8):
                ptx = P.next()
                for j in range(4):
                    tr(ptx[:, j * 128:(j + 1) * 128], xio[:, j, c * 128:(c + 1) * 128], IDENT)
                cp('act', hT[:, c, :], ptx[:])
            if apply_ln_in and dbg_stop not in (11, 12):
                ln_hT(layers[0], LIG, LIB)
            else:
                for c in range(8):
                    cp('act', hTb[:, c, :], hT[:, c, :])
            for li, l in enumerate(layers):
                if dbg_stop in (11, 12, 13):
                    continue
                layer_tile(li, l)
            S.tag = 'io_out'
            for j in range(4):
                for m in range(2):
                    pto = P.next()
                    for cc in range(4):
                        c = 4 * m + cc
                        tr(pto[:, cc * 128:(cc + 1) * 128], hT[:, c, j * 128:(j + 1) * 128], IDENT)
                    cp('act', xio[:, j, m * 512:(m + 1) * 512], pto[:])
            tok = S.dma('sp', lambda e, t=t: e.dma_start(out=y_d[t * TT:(t + 1) * TT, :].rearrange("(j p) c -> p j c", p=128), in_=xio),
                        'yst', reads=[xio], writes=[('y', t)])
            out_toks.append(tok)
        S.wait_all('sp', out_toks[-1:])
        S.emit()
    return nc, S


def _pack_params(inp):
    pp = np.zeros((128, 2, NPP), np.float32)

    def fm(v, nch):
        return np.ascontiguousarray(v.reshape(nch, 128).T)
    for l in range(2):
        w = inp['conv_dw_w'][l]
        pp[:, l, CW:CW + 248] = w.T.reshape(8, 128, 31).transpose(1, 0, 2).reshape(128, 248)
        pp[:, l, CB:CB + 8] = fm(inp['conv_dw_b'][l], 8)
        pp[:, l, CG:CG + 8] = fm(inp['conv_ln_g'][l], 8)
        pp[:, l, CBE:CBE + 8] = fm(inp['conv_ln_b'][l], 8)
        w = inp['ssm_conv_w'][l]
        pp[:, l, SW:SW + 96] = w.T.reshape(24, 128, 4).transpose(1, 0, 2).reshape(128, 96)
        pp[:, l, SB:SB + 24] = fm(inp['ssm_conv_b'][l], 24)
        pp[:, l, NW:NW + 16] = fm(inp['ssm_norm_w'][l], 16)
        w = inp['ffn_dw_w'][l]
        pp[:, l, FW:FW + 132] = w.T.reshape(44, 128, 3).transpose(1, 0, 2).reshape(128, 132)
        pp[:, l, FB:FB + 44] = fm(inp['ffn_dw_b'][l], 44)
        pp[:, l, L1G:L1G + 8] = fm(inp['ln1_g'][l], 8)
        pp[:, l, L1B:L1B + 8] = fm(inp['ln1_b'][l], 8)
        pp[:, l, L2G:L2G + 8] = fm(inp['ln2_g'][l], 8)
        pp[:, l, L2B:L2B + 8] = fm(inp['ln2_b'][l], 8)
        pp[:, l, LIG:LIG + 8] = fm(inp['ln_in_g'], 8)
        pp[:, l, LIB:LIB + 8] = fm(inp['ln_in_b'], 8)
    rowp = np.zeros((2, 3, 32), np.float32)
    for l in range(2):
        rowp[l, 0] = inp['ssm_dt_bias'][l]
        rowp[l, 1] = inp['ssm_a_log'][l]
        rowp[l, 2] = inp['ssm_d'][l]
    cst = np.zeros((128, 5, 128), np.float32)
    cst[:, 0, :] = np.eye(128)
    cst[:, 1, :] = np.triu(np.ones((128, 128)))
    cst[:, 2, :] = 1.0
    cst[:, 3, :] = np.tril(np.ones((128, 128)), -1)
    cst[:, 4, :] = np.triu(np.ones((128, 128)))
    return pp, rowp, cst


_CACHE = {}


def _get_prog(T, layers, apply_ln_in):
    key = (T, tuple(layers), apply_ln_in)
    if key not in _CACHE:
        _CACHE[key] = build_program(T, list(layers), apply_ln_in)
    return _CACHE[key][0]


def run_layers(xs, inp, layers, apply_ln_in, n_cores):
    T = xs[0].shape[0]
    nc = _get_prog(T, layers, apply_ln_in)
    pp, rowp, cst = _pack_params(inp)
    wts = {k: np.ascontiguousarray(inp[k], dtype=np.float32) for k in
           ('w_in', 'w_conv_out', 'w_ssm_out', 'w_o', 'w_ffn_up', 'w_ffn_down')}
    in_maps = []
    for c in range(n_cores):
        m = dict(wts)
        m['x'] = np.ascontiguousarray(xs[c % len(xs)], dtype=np.float32)
        m['pp'] = pp
        m['rowp'] = rowp
        m['cst'] = cst
        in_maps.append(m)
    res = run_bass_kernel_spmd(nc, in_maps, core_ids=list(range(n_cores)))
    return [np.asarray(res.results[c]['y']) for c in range(len(xs))]


FUSED = True


def kernel(**inputs):
    inp = {k: np.asarray(v) for k, v in inputs.items()}
    x = inp['x'].astype(np.float32)
    B = x.shape[0]
    xs = [x[b] for b in range(B)]
    if FUSED:
        ys = run_layers(xs, inp, (0, 1), True, 4)
    else:
        h = run_layers(xs, inp, (0,), True, 8)
        ys = run_layers(h, inp, (1,), False, 8)
    return np.stack(ys, 0).astype(np.float32)
```

```python
import contextlib
import numpy as np
import concourse.bass as bass
import concourse.mybir as mybir
from concourse.bass_utils import run_bass_kernel_spmd

F32 = mybir.dt.float32
BF16 = mybir.dt.bfloat16
ALU = mybir.AluOpType
AF = mybir.ActivationFunctionType

CELL = 512
D = 1024
IN_DIM = 9248
TT = 512
DEPTH = 2
ALPHA = float((2 * DEPTH) ** 0.25)
LN_EPS = 1e-5
RMS_EPS = 1e-5


def _dtsize(dt):
    s = str(dt)
    if '64' in s:
        return 8
    if '32' in s:
        return 4
    if '16' in s:
        return 2
    return 1


class Sched:
    ENGS = ('pe', 'act', 'dve', 'pool', 'sp')

    def __init__(self, nc, same_engine_sync=True):
        self.nc = nc
        self.same_engine_sync = same_engine_sync
        self.streams = {e: [] for e in self.ENGS}
        self.count = {e: 0 for e in self.ENGS}
        self.dma_count = {}
        self.cells = {}
        self.waited = {e: {} for e in self.ENGS}
        self.n_ops = 0
        self.tag = ''
        self.tags = {e: [] for e in self.ENGS}

    def _keys(self, r):
        if isinstance(r, tuple):
            return [r]
        ap = r
        name = ap.tensor.name
        sp_ = str(ap.space).upper()
        if 'DRAM' in sp_ or 'PSUM' in sp_:
            return [(name,)]
        pairs = ap.ap
        pstep = pairs[0][0]
        sz = _dtsize(ap.dtype)
        off = ap.offset % pstep if pstep > 0 else ap.offset
        hull = 0
        for (st, cn) in pairs[1:]:
            hull += abs(st) * (cn - 1)
        lo = off * sz
        hi = (off + hull + 1) * sz
        return [(name, c) for c in range(lo // CELL, (hi - 1) // CELL + 1)]

    def _deps(self, reads, writes):
        deps = {}
        rk = []
        for r in reads:
            rk += self._keys(r)
        wk = []
        for w in writes:
            wk += self._keys(w)
        cells = self.cells
        for k in rk:
            c = cells.get(k)
            if c is not None and c[0] is not None:
                kk, vv = c[0]
                if deps.get(kk, 0) < vv:
                    deps[kk] = vv
        for k in wk:
            c = cells.get(k)
            if c is not None:
                if c[0] is not None:
                    kk, vv = c[0]
                    if deps.get(kk, 0) < vv:
                        deps[kk] = vv
                for kk, vv in c[1].items():
                    if deps.get(kk, 0) < vv:
                        deps[kk] = vv
        return deps, rk, wk

    def _commit(self, rk, wk, tok):
        cells = self.cells
        for k in rk:
            c = cells.get(k)
            if c is None:
                c = [None, {}]
                cells[k] = c
            if c[1].get(tok[0], 0) < tok[1]:
                c[1][tok[0]] = tok[1]
        for k in wk:
            cells[k] = [tok, {}]

    def _filter(self, eng, deps, force_self=False):
        waits = []
        wd = self.waited[eng]
        for k, v in deps.items():
            if k == eng and (eng == 'pe' or not (self.same_engine_sync or force_self)):
                continue
            if wd.get(k, 0) >= v:
                continue
            wd[k] = v
            waits.append((k, v))
        return waits

    def op(self, eng, fn, reads=(), writes=(), force_self=False):
        deps, rk, wk = self._deps(reads, writes)
        waits = self._filter(eng, deps, force_self)
        self.count[eng] += 1
        tok = (eng, self.count[eng])
        self.tags[eng].append(self.tag)
        self.streams[eng].append((waits, fn, tok))
        self._commit(rk, wk, tok)
        self.n_ops += 1
        return tok

    def dma(self, queue, fn, sem, reads=(), writes=()):
        deps, rk, wk = self._deps(reads, writes)
        waits = self._filter(queue, deps)
        self.dma_count[sem] = self.dma_count.get(sem, 0) + 1
        tok = (sem, 16 * self.dma_count[sem])
        self.streams[queue].append((waits, fn, tok))
        self._commit(rk, wk, tok)
        self.n_ops += 1
        return tok

    def wait_all(self, eng, toks):
        deps = {}
        for k, v in toks:
            deps[k] = max(deps.get(k, 0), v)
        waits = self._filter(eng, deps)
        if waits:
            self.streams[eng].append((waits, None, None))

    def emit(self):
        nc = self.nc
        semnames = list(self.ENGS) + list(self.dma_count.keys())
        with contextlib.ExitStack() as st:
            sems = {}
            for n in semnames:
                sems[n] = st.enter_context(nc.semaphore("s_" + n))
            block = st.enter_context(nc.Block())
            engmap = {'pe': block.tensor, 'act': block.scalar, 'dve': block.vector,
                      'pool': block.gpsimd, 'sp': block.sync}

            def make(ename):
                stream = self.streams[ename]

                def body(e):
                    for waits, fn, tok in stream:
                        for (k, v) in waits:
                            e.wait_ge(sems[k], v)
                        if fn is None:
                            continue
                        ins = fn(e)
                        if tok[0] == ename:
                            ins.then_inc(sems[tok[0]], 1)
                        else:
                            ins.then_inc(sems[tok[0]], 16)
                return body
            for ename in self.ENGS:
                if self.streams[ename]:
                    engmap[ename](make(ename))


CW, CB, CG, CBE = 0, 248, 256, 264
SW, SB, NW = 272, 368, 392
FW, FB = 408, 540
L1G, L1B, L2G, L2B = 584, 592, 600, 608
LIG, LIB = 616, 624
NPP = 632

def _slot_table():
    s = []
    A = lambda name, k0, nk, c0, ncol: s.append((name, k0, nk, c0, ncol))
    A('w_in', 0, 8, 0, 512)
    A('w_in', 0, 8, 1024, 512)
    for c in range(4):
        A('diag', c, 8, 0, 496)
    A('w_in', 0, 8, 512, 512)
    A('w_in', 0, 8, 1536, 512)
    for c in range(4, 8):
        A('diag', c, 8, 0, 496)
    A('w_in', 0, 8, 7168, 32)
    A('w_in', 0, 8, 6144, 512)
    A('w_in', 0, 8, 6656, 512)
    for g in range(4):
        A('w_in', 0, 8, 4096 + 512 * g, 512)
        A('w_in', 0, 8, 2048 + 512 * g, 512)
    for ob in range(2):
        A('w_in', 0, 8, 7200 + 512 * ob, 512)
        A('w_conv_out', 0, 8, 512 * ob, 512)
        A('w_ssm_out', 0, 8, 512 * ob, 512)
        A('w_ssm_out', 8, 8, 512 * ob, 512)
        A('w_in', 0, 8, 8224 + 512 * ob, 512)
    for ob in range(2):
        A('w_o', 0, 8, 512 * ob, 512)
    for b in range(6):
        ncol = 512 if b < 5 else 256
        A('w_ffn_up', 0, 8, 512 * b, ncol)
        A('w_ffn_up', 0, 8, 2816 + 512 * b, ncol)
    for ob in range(2):
        A('w_ffn_down', 0, 8, 512 * ob, 512)
        A('w_ffn_down', 8, 8, 512 * ob, 512)
        A('w_ffn_down', 16, 6, 512 * ob, 512)
    return s


SLOTS = _slot_table()
NSL = len(SLOTS)
NRING = 4
NPREP = 4


def build_program(T, layers, apply_ln_in, conv_pool_chunks=(), same_engine_sync=False, dbg_stop=0, dbg_var=0, use_pool=False):
    PL = 'pool' if use_pool else 'dve'
    NT = T // TT
    NL = len(layers)
    nc = bass.Bass("TRN2", target_bir_lowering=False)
    S = Sched(nc, same_engine_sync=same_engine_sync)

    def din(name, shape):
        return nc.dram_tensor(name, shape, F32, kind="ExternalInput").ap()
    x_d = din("x", [T, D])
    wdr = {
        'w_in': din("w_in", [2, D, IN_DIM]),
        'w_conv_out': din("w_conv_out", [2, D, D]),
        'w_ssm_out': din("w_ssm_out", [2, 2048, D]),
        'w_o': din("w_o", [2, D, D]),
        'w_ffn_up': din("w_ffn_up", [2, D, 5632]),
        'w_ffn_down': din("w_ffn_down", [2, 2816, D]),
    }
    pp_d = din("pp", [128, 2, NPP])
    rowp_d = din("rowp", [2, 3, 32])
    cst_d = din("cst", [128, 5, 128])
    y_d = nc.dram_tensor("y", [T, D], F32, kind="ExternalOutput").ap()
    wscr = nc.dram_tensor("wscr", [2, NSL, 128, 4096], BF16, kind="Internal").ap()
    dscr = nc.dram_tensor("dscr", [2, 8, 128, 3968], BF16, kind="Internal").ap()

    with contextlib.ExitStack() as st:
        def sb(name, shape, dt=F32):
            return st.enter_context(nc.sbuf_tensor("sb_" + name, shape, dt))

        def psb(name):
            return st.enter_context(nc.psum_tensor(name, [128, 512], F32))
        cst = sb("cst", [128, 5, 128])
        IDENT, UINCL, ONES, USTRICT, MASKT = (cst[:, i, :] for i in range(5))
        identb = sb("identb", [128, 128], BF16)
        onesN = sb("onesN", [128, 128])
        pp = sb("pp", [128, 2, NPP])
        rowc = sb("rowc", [128, 2, 3, 32])
        epsb = sb("epsb", [128, 2])
        ST = sb("ST", [128, 2, 4, 512])
        STb = sb("STb", [128, 2, 4, 512], BF16)
        vhalo = sb("vhalo", [128, 2, 8, 30], BF16)
        xbch = sb("xbch", [128, 2, 24, 3])
        ffnh = sb("ffnh", [128, 2, 44, 2])
        hT = sb("hT", [128, 8, 512])
        hTb = sb("hTb", [128, 8, 512], BF16)
        wring = sb("wring", [128, NRING, 4096], BF16)
        arena = sb("arena", [128, 9216])
        sT = sb("sT", [128, 8, 512], BF16)
        BT = sb("BT", [128, 4, 512], BF16)
        CT = sb("CT", [128, 4, 512], BF16)
        Btok = sb("Btok", [128, 4, 512], BF16)
        dts = sb("dts", [128, 8, 128])
        xs_tok = sb("xs_tok", [128, 4, 512])
        zs = sb("zs", [128, 2, 4, 512], BF16)
        cbm = sb("cbm", [128, 2, 128], BF16)
        tA = sb("tA", [128, 2, 512])
        xsT2 = sb("xsT2", [128, 2, 512])
        t2b = sb("t2b", [128, 2, 512])
        tB = sb("tB", [128, 4, 512])
        lnb = sb("lnb", [128, 4, 512], BF16)
        onesNb = sb("onesNb", [128, 128], BF16)
        tC = sb("tC", [128, 2, 512])
        xpre = sb("xpre", [128, 2, 516])
        ynb = sb("ynb", [128, 2, 512], BF16)
        xsd = sb("xsd", [128, 2, 512], BF16)
        xsw = sb("xsw", [128, 2, 512], BF16)
        sml = sb("sml", [128, 8])
        banks = [psb("ps%d" % i) for i in range(8)]

        vT = arena[:, 0:4 * 542].bitcast(BF16).rearrange("p (c t) -> p c t", c=8)
        convT = arena[:, 2176:2176 + 4096].rearrange("p (c t) -> p c t", c=8)
        stat = arena[:, 8192:9216].rearrange("p (c t) -> p c t", c=2)
        ynT = arena[:, 0:4096].bitcast(BF16).rearrange("p (c t) -> p c t", c=16)
        mixTb = arena[:, 4096:6144].bitcast(BF16).rearrange("p (c t) -> p c t", c=8)
        mixa = arena[:, 6144:8192].rearrange("p (c t) -> p c t", c=4)
        Apr2 = [arena[:, 6144 + 1024 * i:6144 + 1024 * (i + 1)] for i in range(2)]
        MT2 = [arena[:, 8192 + 512 * i:8192 + 512 * (i + 1)].bitcast(BF16).rearrange("p (h l) -> p h l", h=8)
               for i in range(2)]
        actT = arena[:, 0:5632].bitcast(BF16).rearrange("p (c t) -> p c t", c=22)
        upre = arena[:, 5632:5632 + 4 * 516].rearrange("p (c t) -> p c t", c=4)
        xio = arena[:, 4096:8192].rearrange("p (j c) -> p j c", j=4)

        class Cyc:
            def __init__(self, items):
                self.items = items
                self.i = 0
                self.held = set()

            def next(self):
                for _ in range(len(self.items) + 1):
                    k = self.i % len(self.items)
                    self.i += 1
                    if k not in self.held:
                        return self.items[k]
                raise RuntimeError("all held")

            def hold_next(self):
                for _ in range(len(self.items) + 1):
                    k = self.i % len(self.items)
                    self.i += 1
                    if k not in self.held:
                        self.held.add(k)
                        return k, self.items[k]
                raise RuntimeError("all held")

            def release(self, k):
                self.held.discard(k)
        P = Cyc(banks[:6])
        PS_MEAN, PS_EX2 = banks[6], banks[7]
        TA = Cyc([tA[:, i, :] for i in range(2)])
        XS = Cyc([xsT2[:, i, :] for i in range(2)])
        TB = Cyc([tB[:, i, :] for i in range(4)])
        LNB = Cyc([lnb[:, i, :] for i in range(4)])
        TC = Cyc([tC[:, i, :] for i in range(2)])
        XP = Cyc([xpre[:, i, :] for i in range(2)])
        UP = Cyc([upre[:, i, :] for i in range(4)])
        YN = Cyc([ynb[:, i, :] for i in range(2)])
        XSD = Cyc([xsd[:, i, :] for i in range(2)])
        XSW = Cyc([xsw[:, i, :] for i in range(2)])

        def mm(out, lhsT, rhs, start, stop):
            S.op('pe', lambda e: e.matmul(out, lhsT=lhsT, rhs=rhs, start=start, stop=stop),
                 reads=[lhsT, rhs], writes=[out])

        def tr(out, in_, ident):
            S.op('pe', lambda e: e.transpose(out, in_, ident), reads=[in_, ident], writes=[out])

        def act(out, in_, func, bias=None, scale=None, accum=None, eng='act', force_self=False):
            kw = {}
            rd = [in_]
            wr = [out]
            if bias is not None:
                kw['bias'] = bias
                if not isinstance(bias, float):
                    rd.append(bias)
            if scale is not None:
                kw['scale'] = scale
                if not isinstance(scale, float):
                    rd.append(scale)
            if accum is not None:
                kw['accum_out'] = accum
                wr.append(accum)
            S.op('act', lambda e: e.activation(out=out, in_=in_, func=func, **kw), reads=rd, writes=wr, force_self=force_self)

        def tt(eng, out, in0, in1, op):
            S.op(eng, lambda e: e.tensor_tensor(out=out, in0=in0, in1=in1, op=op), reads=[in0, in1], writes=[out])

        def ts(eng, out, in0, s1, s2, op0, op1=None):
            rd = [in0] + [s for s in (s1, s2) if s is not None and not isinstance(s, float)]
            if op1 is None:
                S.op(eng, lambda e: e.tensor_scalar(out=out, in0=in0, scalar1=s1, scalar2=None, op0=op0), reads=rd, writes=[out])
            else:
                S.op(eng, lambda e: e.tensor_scalar(out=out, in0=in0, scalar1=s1, scalar2=s2, op0=op0, op1=op1), reads=rd, writes=[out])

        def stt(eng, out, in0, scalar, in1, op0, op1):
            rd = [in0, in1] + ([] if isinstance(scalar, float) else [scalar])
            S.op(eng, lambda e: e.scalar_tensor_tensor(out=out, in0=in0, scalar=scalar, in1=in1, op0=op0, op1=op1),
                 reads=rd, writes=[out])

        def cp(eng, out, in_, force_self=False):
            if eng == 'act':
                S.op('act', lambda e: e.copy(out=out, in_=in_), reads=[in_], writes=[out], force_self=force_self)
            else:
                S.op(eng, lambda e: e.tensor_copy(out=out, in_=in_), reads=[in_], writes=[out], force_self=force_self)

        def memset(eng, ap, v):
            S.op(eng, lambda e: e.memset(ap, v), writes=[ap])

        n_prep = 0
        for l in (layers if dbg_stop not in (11, 13) else []):
            for s, (wn, k0, nk, c0, ncol) in enumerate(SLOTS):
                if wn == 'diag':
                    continue
                src = wdr[wn][l, k0 * 128:(k0 + nk) * 128, c0:c0 + ncol].rearrange("(k p) c -> p k c", p=128)
                dst = wscr[l, s].rearrange("p (k c) -> p k c", k=8)[:, 0:nk, 0:ncol]
                psem = 'prep%d' % (n_prep % NPREP)
                if n_prep >= NPREP:
                    S.wait_all('pool', [(psem, 16 * (n_prep // NPREP))])
                S.dma('pool', lambda e, src=src, dst=dst: e.dma_start(out=dst, in_=src), psem,
                      reads=[], writes=[('wscr', l, s)])
                n_prep += 1

        if n_prep:
            S.wait_all('pool', [('prep%d' % i, 16 * ((n_prep - 1 - i) // NPREP + 1)) for i in range(min(NPREP, n_prep))])
        uses = [(l, s) for t in range(NT) for l in layers for s in range(NSL)]
        wstate = {'issued': 0, 'n': 0}

        def wget():
            n = wstate['n']
            while wstate['issued'] < min(len(uses), max(n + NRING - 1, 2)):
                m = wstate['issued']
                l, s = uses[m]
                wn_, k0_, nk_, _, ncol_ = SLOTS[s]
                if wn_ == 'diag':
                    dst = wring[:, m % NRING, 0:3968]
                    src = dscr[l, k0_]
                    rkey = ('dscr', l, k0_)
                else:
                    dst = wring[:, m % NRING, :].rearrange("p (k c) -> p k c", k=8)[:, 0:nk_, 0:ncol_]
                    src = wscr[l, s].rearrange("p (k c) -> p k c", k=8)[:, 0:nk_, 0:ncol_]
                    rkey = ('wscr', l, s)
                S.dma('sp', lambda e, src=src, dst=dst: e.dma_start(out=dst, in_=src), 'wld%d' % (m % NRING),
                      reads=[rkey], writes=[wring[:, m % NRING, :]])
                wstate['issued'] += 1
            wstate['n'] += 1
            if SLOTS[uses[n][1]][0] == 'diag':
                return wring[:, n % NRING, 0:3968].rearrange("p (k m) -> p k m", k=31)
            return wring[:, n % NRING, :].rearrange("p (k c) -> p k c", k=8)

        S.dma('sp', lambda e: e.dma_start(out=cst[:], in_=cst_d), 'ld0', reads=[cst_d], writes=[cst[:]])
        S.dma('sp', lambda e: e.dma_start(out=pp[:], in_=pp_d), 'ld1', reads=[pp_d], writes=[pp[:]])
        S.dma('sp', lambda e: e.dma_start(out=rowc[:].rearrange("p a b c -> p (a b c)"),
                                          in_=rowp_d.rearrange("a b c -> (a b c)").partition_broadcast(128)),
              'ld2', reads=[rowp_d], writes=[rowc[:]])
        cp('dve', identb[:], IDENT)
        ts('dve', onesN[:], ONES, 1.0 / 1024.0, None, ALU.mult)
        ts('dve', onesNb[:], ONES, 1.0 / 1024.0, None, ALU.mult)
        memset('dve', epsb[:, 0:1], LN_EPS)
        memset('dve', epsb[:, 1:2], RMS_EPS)
        memset('dve', ST[:], 0.0)
        memset('dve', STb[:], 0.0)
        memset('dve', vhalo[:], 0.0)
        memset('dve', xbch[:], 0.0)
        memset('dve', ffnh[:], 0.0)
        for l in layers:
            act(rowc[:, l, 1, :], rowc[:, l, 1, :], AF.Exp)
            ts('dve', rowc[:, l, 1, :], rowc[:, l, 1, :], -1.0, None, ALU.mult)

        nbuilt = 0
        for l in layers:
            for c in range(8):
                stg = wring[:, nbuilt % NRING, 0:3968]
                tt('dve', stg.rearrange("p (k m) -> p k m", k=31),
                   IDENT.unsqueeze(1).to_broadcast([128, 31, 128]),
                   pp[:, l, CW + c * 31:CW + c * 31 + 31].unsqueeze(2).to_broadcast([128, 31, 128]), ALU.mult)
                S.dma('sp', lambda e, stg=stg, l=l, c=c: e.dma_start(out=dscr[l, c], in_=stg), 'dst%d' % (nbuilt % NRING),
                      reads=[stg], writes=[('dscr', l, c)])
                nbuilt += 1

        def ln_stats_begin():
            pass

        def ln_finish_stats():
            mean_b = stat[:, 0, :]
            rstd_b = stat[:, 1, :]
            cp('act', mean_b, PS_MEAN[:])
            t = TC.next()
            tt('dve', t, mean_b, mean_b, ALU.mult)
            tt('dve', t, PS_EX2[:], t, ALU.subtract)
            act(t, t, AF.Sqrt, bias=epsb[:, 0:1], scale=1.0)
            S.op('dve', lambda e: e.reciprocal(out=rstd_b, in_=t), reads=[t], writes=[rstd_b])
            return mean_b, rstd_b

        def ln_hT(l, gcol, bcol):
            for c in range(8):
                sq = LNB.next()
                act(sq, hT[:, c, :], AF.Square)
                rb = LNB.next()
                cp('act', rb, hT[:, c, :])
                mm(PS_MEAN[:], onesNb[:], rb, c == 0, c == 7)
                mm(PS_EX2[:], onesNb[:], sq, c == 0, c == 7)
            mean_b, rstd_b = ln_finish_stats()
            for c in range(8):
                t = TC.next()
                tt('dve', t, hT[:, c, :], mean_b, ALU.subtract)
                tt('dve', t, t, rstd_b, ALU.mult)
                act(hT[:, c, :], t, AF.Identity, bias=pp[:, l, bcol + c:bcol + c + 1], scale=pp[:, l, gcol + c:gcol + c + 1])
                cp('act', hTb[:, c, :], hT[:, c, :])

        def conv_from_psum(ps, pool_cyc, halo, wcol, bcol, K, l, out_acc, act_tap=False):
            xp = pool_cyc.next()
            H = K - 1
            cp('act', xp[:, H:H + 512], ps)
            cp('act', xp[:, 0:H], halo)
            cp('act', halo, xp[:, 512:512 + H])
            if act_tap:
                act(out_acc, ps, AF.Identity, bias=pp[:, l, bcol:bcol + 1], scale=pp[:, l, wcol + H:wcol + H + 1])
            else:
                ts('dve', out_acc, xp[:, H:H + 512], pp[:, l, wcol + H:wcol + H + 1], pp[:, l, bcol:bcol + 1], ALU.mult, ALU.add)
            for k in range(H):
                stt('dve', out_acc, xp[:, k:k + 512], pp[:, l, wcol + k:wcol + k + 1], out_acc, ALU.mult, ALU.add)

        def layer_tile(li, l):
            PPl = pp[:, l, :]
            if dbg_stop == 10:
                return
            S.tag = 'p1_conformer'
            cp('act', vT[:, :, 0:30], vhalo[:, li, :, :])
            for cb in range(2):
                wa = wget()
                wg = wget()
                for ci in range(4):
                    c = 4 * cb + ci
                    pa = P.next()
                    for kc in range(8):
                        mm(pa[:], wa[:, kc, ci * 128:(ci + 1) * 128], hTb[:, kc, :], kc == 0, kc == 7)
                    pg = P.next()
                    for kc in range(8):
                        mm(pg[:], wg[:, kc, ci * 128:(ci + 1) * 128], hTb[:, kc, :], kc == 0, kc == 7)
                    sig = TA.next()
                    act(sig, pg[:], AF.Sigmoid)
                    tt('dve', vT[:, c, 30:542], pa[:], sig, ALU.mult)
                    cp('act', vhalo[:, li, c, :], vT[:, c, 512:542])
                for ci in range(4):
                    c = 4 * cb + ci
                    wd_ = wget()
                    pc = P.next()
                    for k in range(31):
                        mm(pc[:], wd_[:, k, :], vT[:, c, k:k + 512], k == 0, k == 30)
                    act(convT[:, c, :], pc[:], AF.Identity, bias=PPl[:, CB + c:CB + c + 1], scale=1.0)
                    rb = LNB.next()
                    act(rb, pc[:], AF.Identity, bias=PPl[:, CB + c:CB + c + 1], scale=1.0)
                    sq = LNB.next()
                    act(sq, convT[:, c, :], AF.Square)
                    mm(PS_MEAN[:], onesNb[:], rb, c == 0, c == 7)
                    mm(PS_EX2[:], onesNb[:], sq, c == 0, c == 7)
            if dbg_stop == 3:
                return
            S.tag = 'p1_ln'
            mean_b, rstd_b = ln_finish_stats()
            for c in range(8):
                t = TC.next()
                tt('dve', t, convT[:, c, :], mean_b, ALU.subtract)
                tt('dve', t, t, rstd_b, ALU.mult)
                act(sT[:, c, :], t, AF.Silu, bias=PPl[:, CBE + c:CBE + c + 1], scale=PPl[:, CG + c:CG + c + 1])

            if dbg_stop == 1:
                return
            S.tag = 'p2a_dtBC'
            dtt = dts[:, 0, :].rearrange("p (j h) -> p j h", j=4)
            aa = dts[:, 1, :].rearrange("p (j h) -> p j h", j=4)
            acs = dts[:, 2, :].rearrange("p (j h) -> p j h", j=4)
            eacs = dts[:, 3, :].rearrange("p (j h) -> p j h", j=4)
            cdec = dts[:, 4, :].rearrange("p (j h) -> p j h", j=4)
            w2 = dts[:, 5, :].rearrange("p (j h) -> p j h", j=4)
            dtmp = dts[:, 6, :].rearrange("p (j h) -> p j h", j=4)
            wdt = wget()
            pdt = P.next()
            for j in range(4):
                for kc in range(8):
                    mm(pdt[:, j * 32:(j + 1) * 32], hTb[:, kc, j * 128:(j + 1) * 128], wdt[:, kc, 0:32], kc == 0, kc == 7)
            tt('dve', dtmp, pdt[:, 0:128].rearrange("p (j h) -> p j h", j=4),
               rowc[:, l, 0, :].unsqueeze(1).to_broadcast([128, 4, 32]), ALU.add)
            act(dtmp, dtmp, AF.Exp, force_self=True)
            act(dtt, dtmp, AF.Ln, bias=1.0, scale=1.0, force_self=True)
            tt('dve', aa, dtt, rowc[:, l, 1, :].unsqueeze(1).to_broadcast([128, 4, 32]), ALU.mult)
            if dbg_stop == 41:
                return
            pcs = P.next()
            mm(pcs[:, 0:128], UINCL, dts[:, 1, :], True, True)
            mm(pcs[:, 128:256], ONES, dts[:, 1, :], True, True)
            pcs_cs = pcs[:, 0:128].rearrange("p (j h) -> p j h", j=4)
            pcs_tot = pcs[:, 128:256].rearrange("p (j h) -> p j h", j=4)
            if dbg_stop == 420:
                return
            cp('act', acs, pcs_cs, force_self=True)
            if dbg_stop == 421:
                return
            act(eacs, pcs_cs, AF.Exp, force_self=True)
            act(cdec, pcs_tot, AF.Exp, force_self=True)
            if dbg_stop == 422:
                return
            cp('act', dtmp, pcs_tot, force_self=True)
            S.op('dve', lambda e: e.tensor_tensor(out=dtmp, in0=dtmp, in1=acs, op=ALU.subtract), reads=[dtmp, acs], writes=[dtmp], force_self=True)
            if dbg_stop == 423:
                return
            act(dtmp, dtmp, AF.Exp, force_self=True)
            if dbg_stop == 424:
                return
            tt('dve', w2, dtt, dtmp, ALU.mult)

            def xbc_chunk(wblk, ci, q, dest, dest_fp32_tmp=False):
                pX = P.next()
                for kc in range(8):
                    mm(pX[:], wblk[:, kc, ci * 128:(ci + 1) * 128], hTb[:, kc, :], kc == 0, kc == 7)
                acc = TB.next()
                conv_from_psum(pX[:], XP, xbch[:, li, q, :], SW + 4 * q, SB + q, 4, l, acc)
                act(dest, acc, AF.Silu)

            if dbg_stop == 42:
                return
            def b_transpose(g):
                ptb = P.next()
                ptb16 = ptb[:].bitcast(BF16)
                for j in range(4):
                    tr(ptb16[:, j * 128:(j + 1) * 128], BT[:, g, j * 128:(j + 1) * 128], identb[:])
                cp('act', Btok[:, :, g * 128:(g + 1) * 128], ptb16[:, 0:512].rearrange("p (j n) -> p j n", j=4))
            wB = wget()
            for g in range(4):
                xbc_chunk(wB, g, 16 + g, BT[:, g, :])
                if g > 0:
                    b_transpose(g - 1)
            wC = wget()
            for g in range(4):
                xbc_chunk(wC, g, 20 + g, CT[:, g, :])
                if g == 0:
                    b_transpose(3)

            if dbg_stop == 4:
                return
            def xs_transpose(xsT, ci):
                ptx = P.next()
                for j in range(4):
                    tr(ptx[:, j * 128:(j + 1) * 128], xsT[:, j * 128:(j + 1) * 128], IDENT)
                cp('act', xs_tok[:, :, ci * 128:(ci + 1) * 128], ptx[:].rearrange("p (j n) -> p j n", j=4))

            def xs_z(g):
                S.tag = 'p3_xs_z'
                wx = wget()
                wz = wget()
                pend = None
                for ci in range(4):
                    xsT = XS.next()
                    xbc_chunk(wx, ci, 4 * g + ci, xsT)
                    if pend is not None:
                        xs_transpose(*pend)
                    pend = (xsT, ci)
                for j in range(4):
                    pz = P.next()
                    for kc in range(8):
                        mm(pz[:], hTb[:, kc, j * 128:(j + 1) * 128], wz[:, kc, :], kc == 0, kc == 7)
                    act(zs[:, g % 2, j, :], pz[:], AF.Silu)
                    if j == 0:
                        xs_transpose(*pend)

            def S1a(g, j, k):
                S.tag = 'p3_s1'
                hs = slice(8 * g, 8 * g + 8)
                jb = slice(j * 128, (j + 1) * 128)
                xs3 = xs_tok[:, j, :].rearrange("p (h d) -> p h d", h=8)
                tt('dve', xsd[:, k, :].rearrange("p (h d) -> p h d", h=8), xs3,
                   dtt[:, j, hs].unsqueeze(2).to_broadcast([128, 8, 64]), ALU.mult)
                tt('dve', xsw[:, k, :].rearrange("p (h d) -> p h d", h=8), xs3,
                   w2[:, j, hs].unsqueeze(2).to_broadcast([128, 8, 64]), ALU.mult)
                tt('dve', t2b[:, k, :].rearrange("p (h d) -> p h d", h=8), xs3,
                   rowc[:, l, 2, hs].unsqueeze(2).to_broadcast([128, 8, 64]), ALU.mult)
                pcb = P.next()
                mm(pcb[:, 0:128], BT[:, g, jb], CT[:, g, jb], True, True)
                tt('dve', cbm[:, k, :], pcb[:, 0:128], MASKT, ALU.mult)
                tt('dve', Apr2[k].rearrange("p (h l) -> p h l", h=8),
                   UINCL.unsqueeze(1).to_broadcast([128, 8, 128]),
                   aa[:, j, hs].unsqueeze(2).to_broadcast([128, 8, 128]), ALU.mult)
                Es = []
                for half in range(2):
                    pseg = P.next()
                    mm(pseg[:], USTRICT, Apr2[k][:, half * 512:(half + 1) * 512], True, True)
                    E = LNB.next()
                    act(E, pseg[:], AF.Exp)
                    Es.append(E)
                return Es

            def S1b(g, j, k, Es):
                S.tag = 'p3_s1'
                for half in range(2):
                    tt('dve', MT2[k][:, 4 * half:4 * half + 4, :], Es[half].rearrange("p (h l) -> p h l", h=4),
                       cbm[:, k, :].unsqueeze(1).to_broadcast([128, 4, 128]), ALU.mult)

            def S2a(g, j, k):
                S.tag = 'p3_s2'
                hs = slice(8 * g, 8 * g + 8)
                jb = slice(j * 128, (j + 1) * 128)
                xd = xsd[:, k, :]
                xw = xsw[:, k, :]
                pyd = P.next()
                for hh in range(8):
                    mm(pyd[:, hh * 64:(hh + 1) * 64], MT2[k][:, hh, :], xd[:, hh * 64:(hh + 1) * 64], True, True)
                pyo = P.next()
                mm(pyo[:], CT[:, g, jb], STb[:, li, g, :], True, True)
                pst = P.next()
                mm(pst[:], Btok[:, j, g * 128:(g + 1) * 128], xw, True, True)
                t1 = TC.next()
                tt('dve', t1.rearrange("p (h d) -> p h d", h=8), pyo[:].rearrange("p (h d) -> p h d", h=8),
                   eacs[:, j, hs].unsqueeze(2).to_broadcast([128, 8, 64]), ALU.mult)
                Sg = ST[:, li, g, :]
                tt('dve', Sg.rearrange("p (h d) -> p h d", h=8), Sg.rearrange("p (h d) -> p h d", h=8),
                   cdec[:, j, hs].unsqueeze(2).to_broadcast([128, 8, 64]), ALU.mult)
                tt('dve', Sg, Sg, pst[:], ALU.add)
                cp('act', STb[:, li, g, :], Sg)
                tt('dve', t1, t1, pyd[:], ALU.add)
                tt('dve', t1, t1, t2b[:, k, :], ALU.add)
                tt('dve', t1, t1, zs[:, g % 2, j, :], ALU.mult)
                act(t2b[:, k, :], t1, AF.Square, accum=sml[:, 0:1])
                act(sml[:, 1:2], sml[:, 0:1], AF.Ln, bias=epsb[:, 1:2], scale=1.0 / 512.0, force_self=True)
                act(sml[:, 2:3], sml[:, 1:2], AF.Exp, scale=-0.5, force_self=True)
                return t1

            def S2b(g, j, k, t1):
                S.tag = 'p3_s2'
                jb = slice(j * 128, (j + 1) * 128)
                yn = YN.next()
                act(yn, t1, AF.Copy, scale=sml[:, 2:3], force_self=True)
                pty = P.next()
                pty16 = pty[:].bitcast(BF16)
                for ci in range(4):
                    tr(pty16[:, ci * 128:(ci + 1) * 128], yn[:, ci * 128:(ci + 1) * 128], identb[:])
                for ci in range(4):
                    act(ynT[:, 4 * g + ci, jb], pty16[:, ci * 128:(ci + 1) * 128], AF.Copy,
                        scale=PPl[:, NW + 4 * g + ci:NW + 4 * g + ci + 1])

            iters = [(g, j) for g in range(4) for j in range(4)]
            xs_z(0)
            Es = S1a(0, 0, 0)
            S1b(0, 0, 0, Es)
            for i, (g, j) in enumerate(iters):
                nxt = None
                if i + 1 < len(iters):
                    g2, j2 = iters[i + 1]
                    if j2 == 0:
                        xs_z(g2)
                    nxt = (g2, j2, (i + 1) % 2, S1a(g2, j2, (i + 1) % 2))
                t1 = S2a(g, j, i % 2)
                if nxt is not None:
                    S1b(*nxt)
                S2b(g, j, i % 2, t1)

            if dbg_stop == 6:
                return
            S.tag = 'p4_out'
            for ob in range(2):
                wga = wget()
                wco = wget()
                for ci in range(4):
                    pga = P.next()
                    for kc in range(8):
                        mm(pga[:], wga[:, kc, ci * 128:(ci + 1) * 128], hTb[:, kc, :], kc == 0, kc == 7)
                    sg = TA.next()
                    act(sg, pga[:], AF.Sigmoid)
                    pya = P.next()
                    for kc in range(8):
                        mm(pya[:], wco[:, kc, ci * 128:(ci + 1) * 128], sT[:, kc, :], kc == 0, kc == 7)
                    tt('dve', mixa[:, ci, :], pya[:], sg, ALU.mult)
                wsa = wget()
                held = [P.hold_next() for _ in range(4)]
                for ci in range(4):
                    for kc in range(8):
                        mm(held[ci][1][:], wsa[:, kc, ci * 128:(ci + 1) * 128], ynT[:, kc, :], kc == 0, False)
                wsb = wget()
                for ci in range(4):
                    for kc in range(8):
                        mm(held[ci][1][:], wsb[:, kc, ci * 128:(ci + 1) * 128], ynT[:, 8 + kc, :], False, kc == 7)
                wgb = wget()
                for ci in range(4):
                    pgb = P.next()
                    for kc in range(8):
                        mm(pgb[:], wgb[:, kc, ci * 128:(ci + 1) * 128], hTb[:, kc, :], kc == 0, kc == 7)
                    sg = TA.next()
                    act(sg, pgb[:], AF.Sigmoid)
                    t = TC.next()
                    tt('dve', t, held[ci][1][:], sg, ALU.mult)
                    tt('dve', mixTb[:, 4 * ob + ci, :], t, mixa[:, ci, :], ALU.add)
                    P.release(held[ci][0])
            for ob in range(2):
                wwo = wget()
                for ci in range(4):
                    oc = 4 * ob + ci
                    po = P.next()
                    for kc in range(8):
                        mm(po[:], wwo[:, kc, ci * 128:(ci + 1) * 128], mixTb[:, kc, :], kc == 0, kc == 7)
                    stt('dve', hT[:, oc, :], hT[:, oc, :], ALPHA, po[:], ALU.mult, ALU.add)
            S.tag = 'ln1'
            ln_hT(l, L1G, L1B)

            if dbg_stop == 7:
                return
            S.tag = 'p5_ffn'
            for b in range(6):
                wgt = wget()
                wvl = wget()
                nci = 4 if b < 5 else 2
                for ci in range(nci):
                    i = 4 * b + ci
                    pg = P.next()
                    for kc in range(8):
                        mm(pg[:], wgt[:, kc, ci * 128:(ci + 1) * 128], hTb[:, kc, :], kc == 0, kc == 7)
                    pv = P.next()
                    for kc in range(8):
                        mm(pv[:], wvl[:, kc, ci * 128:(ci + 1) * 128], hTb[:, kc, :], kc == 0, kc == 7)
                    ag = TB.next()
                    conv_from_psum(pg[:], UP, ffnh[:, li, i, :], FW + 3 * i, FB + i, 3, l, ag)
                    av = TB.next()
                    conv_from_psum(pv[:], UP, ffnh[:, li, 22 + i, :], FW + 3 * (22 + i), FB + 22 + i, 3, l, av)
                    sg = TA.next()
                    act(sg, ag, AF.Silu)
                    tt('dve', actT[:, i, :], sg, av, ALU.mult)
            for ob in range(2):
                held = [P.hold_next() for _ in range(4)]
                for ks in range(3):
                    wd = wget()
                    nk = 8 if ks < 2 else 6
                    for ci in range(4):
                        for kk in range(nk):
                            mm(held[ci][1][:], wd[:, kk, ci * 128:(ci + 1) * 128], actT[:, 8 * ks + kk, :],
                               ks == 0 and kk == 0, ks == 2 and kk == nk - 1)
                for ci in range(4):
                    oc = 4 * ob + ci
                    stt('dve', hT[:, oc, :], hT[:, oc, :], ALPHA, held[ci][1][:], ALU.mult, ALU.add)
                    P.release(held[ci][0])
            S.tag = 'ln2'
            ln_hT(l, L2G, L2B)

        out_toks = []
        for t in range(NT):
            S.tag = 'io_in'
            S.dma('sp', lambda e, t=t: e.dma_start(out=xio, in_=x_d[t * TT:(t + 1) * TT, :].rearrange("(j p) c -> p j c", p=128)),
                  'xld', reads=[x_d], writes=[xio])
            for c in range(8):
                ptx = P.next()
                for j in range(4):
                    tr(ptx[:, j * 128:(j + 1) * 128], xio[:, j, c * 128:(c + 1) * 128], IDENT)
                cp('act', hT[:, c, :], ptx[:])
            if apply_ln_in and dbg_stop not in (11, 12):
                ln_hT(layers[0], LIG, LIB)
            else:
                for c in range(8):
                    cp('act', hTb[:, c, :], hT[:, c, :])
            for li, l in enumerate(layers):
                if dbg_stop in (11, 12, 13):
                    continue
                layer_tile(li, l)
            S.tag = 'io_out'
            for j in range(4):
                for m in range(2):
                    pto = P.next()
                    for cc in range(4):
                        c = 4 * m + cc
                        tr(pto[:, cc * 128:(cc + 1) * 128], hT[:, c, j * 128:(j + 1) * 128], IDENT)
                    cp('act', xio[:, j, m * 512:(m + 1) * 512], pto[:])
            tok = S.dma('sp', lambda e, t=t: e.dma_start(out=y_d[t * TT:(t + 1) * TT, :].rearrange("(j p) c -> p j c", p=128), in_=xio),
                        'yst', reads=[xio], writes=[('y', t)])
            out_toks.append(tok)
        S.wait_all('sp', out_toks[-1:])
        S.emit()
    return nc, S


def _pack_params(inp):
    pp = np.zeros((128, 2, NPP), np.float32)

    def fm(v, nch):
        return np.ascontiguousarray(v.reshape(nch, 128).T)
    for l in range(2):
        w = inp['conv_dw_w'][l]
        pp[:, l, CW:CW + 248] = w.T.reshape(8, 128, 31).transpose(1, 0, 2).reshape(128, 248)
        pp[:, l, CB:CB + 8] = fm(inp['conv_dw_b'][l], 8)
        pp[:, l, CG:CG + 8] = fm(inp['conv_ln_g'][l], 8)
        pp[:, l, CBE:CBE + 8] = fm(inp['conv_ln_b'][l], 8)
        w = inp['ssm_conv_w'][l]
        pp[:, l, SW:SW + 96] = w.T.reshape(24, 128, 4).transpose(1, 0, 2).reshape(128, 96)
        pp[:, l, SB:SB + 24] = fm(inp['ssm_conv_b'][l], 24)
        pp[:, l, NW:NW + 16] = fm(inp['ssm_norm_w'][l], 16)
        w = inp['ffn_dw_w'][l]
        pp[:, l, FW:FW + 132] = w.T.reshape(44, 128, 3).transpose(1, 0, 2).reshape(128, 132)
        pp[:, l, FB:FB + 44] = fm(inp['ffn_dw_b'][l], 44)
        pp[:, l, L1G:L1G + 8] = fm(inp['ln1_g'][l], 8)
        pp[:, l, L1B:L1B + 8] = fm(inp['ln1_b'][l], 8)
        pp[:, l, L2G:L2G + 8] = fm(inp['ln2_g'][l], 8)
        pp[:, l, L2B:L2B + 8] = fm(inp['ln2_b'][l], 8)
        pp[:, l, LIG:LIG + 8] = fm(inp['ln_in_g'], 8)
        pp[:, l, LIB:LIB + 8] = fm(inp['ln_in_b'], 8)
    rowp = np.zeros((2, 3, 32), np.float32)
    for l in range(2):
        rowp[l, 0] = inp['ssm_dt_bias'][l]
        rowp[l, 1] = inp['ssm_a_log'][l]
        rowp[l, 2] = inp['ssm_d'][l]
    cst = np.zeros((128, 5, 128), np.float32)
    cst[:, 0, :] = np.eye(128)
    cst[:, 1, :] = np.triu(np.ones((128, 128)))
    cst[:, 2, :] = 1.0
    cst[:, 3, :] = np.tril(np.ones((128, 128)), -1)
    cst[:, 4, :] = np.triu(np.ones((128, 128)))
    return pp, rowp, cst


_CACHE = {}


def _get_prog(T, layers, apply_ln_in):
    key = (T, tuple(layers), apply_ln_in)
    if key not in _CACHE:
        _CACHE[key] = build_program(T, list(layers), apply_ln_in)
    return _CACHE[key][0]


def run_layers(xs, inp, layers, apply_ln_in, n_cores):
    T = xs[0].shape[0]
    nc = _get_prog(T, layers, apply_ln_in)
    pp, rowp, cst = _pack_params(inp)
    wts = {k: np.ascontiguousarray(inp[k], dtype=np.float32) for k in
           ('w_in', 'w_conv_out', 'w_ssm_out', 'w_o', 'w_ffn_up', 'w_ffn_down')}
    in_maps = []
    for c in range(n_cores):
        m = dict(wts)
        m['x'] = np.ascontiguousarray(xs[c % len(xs)], dtype=np.float32)
        m['pp'] = pp
        m['rowp'] = rowp
        m['cst'] = cst
        in_maps.append(m)
    res = run_bass_kernel_spmd(nc, in_maps, core_ids=list(range(n_cores)))
    return [np.asarray(res.results[c]['y']) for c in range(len(xs))]


FUSED = True


def kernel(**inputs):
    inp = {k: np.asarray(v) for k, v in inputs.items()}
    x = inp['x'].astype(np.float32)
    B = x.shape[0]
    xs = [x[b] for b in range(B)]
    if FUSED:
        ys = run_layers(xs, inp, (0, 1), True, 4)
    else:
        h = run_layers(xs, inp, (0,), True, 8)
        ys = run_layers(h, inp, (1,), False, 8)
    return np.stack(ys, 0).astype(np.float32)
```

```python
import contextlib
import numpy as np
import concourse.bass as bass
import concourse.mybir as mybir
from concourse.bass_utils import run_bass_kernel_spmd

F32 = mybir.dt.float32
BF16 = mybir.dt.bfloat16
ALU = mybir.AluOpType
AF = mybir.ActivationFunctionType

CELL = 512
D = 1024
IN_DIM = 9248
TT = 512
DEPTH = 2
ALPHA = float((2 * DEPTH) ** 0.25)
LN_EPS = 1e-5
RMS_EPS = 1e-5


def _dtsize(dt):
    s = str(dt)
    if '64' in s:
        return 8
    if '32' in s:
        return 4
    if '16' in s:
        return 2
    return 1


class Sched:
    ENGS = ('pe', 'act', 'dve', 'pool', 'sp')

    def __init__(self, nc, same_engine_sync=True):
        self.nc = nc
        self.same_engine_sync = same_engine_sync
        self.streams = {e: [] for e in self.ENGS}
        self.count = {e: 0 for e in self.ENGS}
        self.dma_count = {}
        self.dma_inc = {}
        self.cells = {}
        self.waited = {e: {} for e in self.ENGS}
        self.n_ops = 0
        self.tag = ''
        self.tags = {e: [] for e in self.ENGS}

    def _keys(self, r):
        if isinstance(r, tuple):
            return [r]
        ap = r
        name = ap.tensor.name
        sp_ = str(ap.space).upper()
        if 'DRAM' in sp_ or 'PSUM' in sp_:
            return [(name,)]
        pairs = ap.ap
        pstep = pairs[0][0]
        sz = _dtsize(ap.dtype)
        off = ap.offset % pstep if pstep > 0 else ap.offset
        hull = 0
        for (st, cn) in pairs[1:]:
            hull += abs(st) * (cn - 1)
        lo = off * sz
        hi = (off + hull + 1) * sz
        return [(name, c) for c in range(lo // CELL, (hi - 1) // CELL + 1)]

    def _deps(self, reads, writes):
        deps = {}
        rk = []
        for r in reads:
            rk += self._keys(r)
        wk = []
        for w in writes:
            wk += self._keys(w)
        cells = self.cells
        for k in rk:
            c = cells.get(k)
            if c is not None and c[0] is not None:
                kk, vv = c[0]
                if deps.get(kk, 0) < vv:
                    deps[kk] = vv
        for k in wk:
            c = cells.get(k)
            if c is not None:
                if c[0] is not None:
                    kk, vv = c[0]
                    if deps.get(kk, 0) < vv:
                        deps[kk] = vv
                for kk, vv in c[1].items():
                    if deps.get(kk, 0) < vv:
                        deps[kk] = vv
        return deps, rk, wk

    def _commit(self, rk, wk, tok):
        cells = self.cells
        for k in rk:
            c = cells.get(k)
            if c is None:
                c = [None, {}]
                cells[k] = c
            if c[1].get(tok[0], 0) < tok[1]:
                c[1][tok[0]] = tok[1]
        for k in wk:
            cells[k] = [tok, {}]

    def _filter(self, eng, deps, force_self=False):
        waits = []
        wd = self.waited[eng]
        for k, v in deps.items():
            if k == eng and (eng == 'pe' or not (self.same_engine_sync or force_self)):
                continue
            if wd.get(k, 0) >= v:
                continue
            wd[k] = v
            waits.append((k, v))
        return waits

    def op(self, eng, fn, reads=(), writes=(), force_self=False):
        deps, rk, wk = self._deps(reads, writes)
        waits = self._filter(eng, deps, force_self)
        self.count[eng] += 1
        tok = (eng, self.count[eng])
        self.tags[eng].append(self.tag)
        self.streams[eng].append((waits, fn, tok))
        self._commit(rk, wk, tok)
        self.n_ops += 1
        return tok

    def dma(self, queue, fn, sem, reads=(), writes=(), inc=16):
        deps, rk, wk = self._deps(reads, writes)
        waits = self._filter(queue, deps)
        self.dma_count[sem] = self.dma_count.get(sem, 0) + 1
        self.dma_inc[sem] = inc
        tok = (sem, inc * self.dma_count[sem])
        self.streams[queue].append((waits, fn, tok))
        self._commit(rk, wk, tok)
        self.n_ops += 1
        return tok

    def wait_all(self, eng, toks):
        deps = {}
        for k, v in toks:
            deps[k] = max(deps.get(k, 0), v)
        waits = self._filter(eng, deps)
        if waits:
            self.streams[eng].append((waits, None, None))

    def emit(self):
        nc = self.nc
        semnames = list(self.ENGS) + list(self.dma_count.keys())
        with contextlib.ExitStack() as st:
            sems = {}
            for n in semnames:
                sems[n] = st.enter_context(nc.semaphore("s_" + n))
            block = st.enter_context(nc.Block())
            engmap = {'pe': block.tensor, 'act': block.scalar, 'dve': block.vector,
                      'pool': block.gpsimd, 'sp': block.sync}

            def make(ename):
                stream = self.streams[ename]

                def body(e):
                    for waits, fn, tok in stream:
                        for (k, v) in waits:
                            e.wait_ge(sems[k], v)
                        if fn is None:
                            continue
                        ins = fn(e)
                        if tok[0] == ename:
                            ins.then_inc(sems[tok[0]], 1)
                        else:
                            ins.then_inc(sems[tok[0]], self.dma_inc[tok[0]])
                return body
            for ename in self.ENGS:
                if self.streams[ename]:
                    engmap[ename](make(ename))


CW, CB, CG, CBE = 0, 248, 256, 264
SW, SB, NW = 272, 368, 392
FW, FB = 408, 540
L1G, L1B, L2G, L2B = 584, 592, 600, 608
LIG, LIB = 616, 624
NPP = 632

def _slot_table():
    s = []
    A = lambda name, k0, nk, c0, ncol: s.append((name, k0, nk, c0, ncol))
    A('w_in', 0, 8, 0, 512)
    A('w_in', 0, 8, 1024, 512)
    for c in range(4):
        A('diag', c, 8, 0, 496)
    A('w_in', 0, 8, 512, 512)
    A('w_in', 0, 8, 1536, 512)
    for c in range(4, 8):
        A('diag', c, 8, 0, 496)
    A('w_in', 0, 8, 7168, 32)
    A('w_in', 0, 8, 6144, 512)
    A('w_in', 0, 8, 6656, 512)
    for g in range(4):
        A('w_in', 0, 8, 4096 + 512 * g, 512)
        A('w_in', 0, 8, 2048 + 512 * g, 512)
    for ob in range(2):
        A('w_in', 0, 8, 7200 + 512 * ob, 512)
        A('w_conv_out', 0, 8, 512 * ob, 512)
        A('w_ssm_out', 0, 8, 512 * ob, 512)
        A('w_ssm_out', 8, 8, 512 * ob, 512)
        A('w_in', 0, 8, 8224 + 512 * ob, 512)
    for ob in range(2):
        A('w_o', 0, 8, 512 * ob, 512)
    for b in range(6):
        ncol = 512 if b < 5 else 256
        A('w_ffn_up', 0, 8, 512 * b, ncol)
        A('w_ffn_up', 0, 8, 2816 + 512 * b, ncol)
    for ob in range(2):
        A('w_ffn_down', 0, 8, 512 * ob, 512)
        A('w_ffn_down', 8, 8, 512 * ob, 512)
        A('w_ffn_down', 16, 6, 512 * ob, 512)
    return s


SLOTS = _slot_table()
NSL = len(SLOTS)
NRING = 4
NPREP = 4


def build_program(T, layers, apply_ln_in, conv_pool_chunks=(), same_engine_sync=False, dbg_stop=0, dbg_var=0, use_pool=False, pipe=False):
    PL = 'pool' if use_pool else 'dve'
    NT = T // TT
    NLW = 1 if pipe else 2
    NL = len(layers)
    nc = bass.Bass("TRN2", target_bir_lowering=False, num_devices=8) if pipe else bass.Bass("TRN2", target_bir_lowering=False)
    S = Sched(nc, same_engine_sync=same_engine_sync)

    def din(name, shape):
        return nc.dram_tensor(name, shape, F32, kind="ExternalInput").ap()
    x_d = din("x", [T, D])
    wdr = {
        'w_in': din("w_in", [NLW, D, IN_DIM]),
        'w_conv_out': din("w_conv_out", [NLW, D, D]),
        'w_ssm_out': din("w_ssm_out", [NLW, 2048, D]),
        'w_o': din("w_o", [NLW, D, D]),
        'w_ffn_up': din("w_ffn_up", [NLW, D, 5632]),
        'w_ffn_down': din("w_ffn_down", [NLW, 2816, D]),
    }
    pp_d = din("pp", [128, NLW, NPP])
    rowp_d = din("rowp", [NLW, 3, 32])
    cst_d = din("cst", [128, 5, 128])
    y_d = nc.dram_tensor("y", [T, D], F32, kind="ExternalOutput").ap()
    if pipe:
        sel_d = din("sel", [128, 4])
        send_d = [nc.dram_tensor("send%d" % i, [TT, D], F32, kind="Internal").ap() for i in range(2)]
        recv_d = [nc.dram_tensor("recv%d" % i, [2 * TT, D], F32, kind="Internal").ap() for i in range(2)]
    wscr = nc.dram_tensor("wscr", [NLW, NSL, 128, 4096], BF16, kind="Internal").ap()
    dscr = nc.dram_tensor("dscr", [NLW, 8, 128, 3968], BF16, kind="Internal").ap()

    with contextlib.ExitStack() as st:
        def sb(name, shape, dt=F32):
            return st.enter_context(nc.sbuf_tensor("sb_" + name, shape, dt))

        def psb(name):
            return st.enter_context(nc.psum_tensor(name, [128, 512], F32))
        cst = sb("cst", [128, 5, 128])
        IDENT, UINCL, ONES, USTRICT, MASKT = (cst[:, i, :] for i in range(5))
        identb = sb("identb", [128, 128], BF16)
        onesN = sb("onesN", [128, 128])
        pp = sb("pp", [128, NLW, NPP])
        rowc = sb("rowc", [128, NLW, 3, 32])
        epsb = sb("epsb", [128, 2])
        sel = sb("sel", [128, 4])
        ST = sb("ST", [128, NLW, 4, 512])
        STb = sb("STb", [128, NLW, 4, 512], BF16)
        vhalo = sb("vhalo", [128, NLW, 8, 30], BF16)
        xbch = sb("xbch", [128, NLW, 24, 3])
        ffnh = sb("ffnh", [128, NLW, 44, 2])
        hT = sb("hT", [128, 8, 512])
        hTb = sb("hTb", [128, 8, 512], BF16)
        wring = sb("wring", [128, NRING, 4096], BF16)
        arena = sb("arena", [128, 9216])
        sT = sb("sT", [128, 8, 512], BF16)
        BT = sb("BT", [128, 4, 512], BF16)
        CT = sb("CT", [128, 4, 512], BF16)
        Btok = sb("Btok", [128, 4, 512], BF16)
        dts = sb("dts", [128, 8, 128])
        xs_tok = sb("xs_tok", [128, 4, 512])
        zs = sb("zs", [128, 2, 4, 512], BF16)
        cbm = sb("cbm", [128, 2, 128], BF16)
        tA = sb("tA", [128, 2, 512])
        xsT2 = sb("xsT2", [128, 2, 512])
        t2b = sb("t2b", [128, 2, 512])
        tB = sb("tB", [128, 4, 512])
        lnb = sb("lnb", [128, 4, 512], BF16)
        onesNb = sb("onesNb", [128, 128], BF16)
        tC = sb("tC", [128, 2, 512])
        xpre = sb("xpre", [128, 2, 516])
        ynb = sb("ynb", [128, 2, 512], BF16)
        xsd = sb("xsd", [128, 2, 512], BF16)
        xsw = sb("xsw", [128, 2, 512], BF16)
        sml = sb("sml", [128, 8])
        banks = [psb("ps%d" % i) for i in range(8)]

        vT = arena[:, 0:4 * 542].bitcast(BF16).rearrange("p (c t) -> p c t", c=8)
        convT = arena[:, 2176:2176 + 4096].rearrange("p (c t) -> p c t", c=8)
        stat = arena[:, 8192:9216].rearrange("p (c t) -> p c t", c=2)
        ynT = arena[:, 0:4096].bitcast(BF16).rearrange("p (c t) -> p c t", c=16)
        mixTb = arena[:, 4096:6144].bitcast(BF16).rearrange("p (c t) -> p c t", c=8)
        mixa = arena[:, 6144:8192].rearrange("p (c t) -> p c t", c=4)
        Apr2 = [arena[:, 6144 + 1024 * i:6144 + 1024 * (i + 1)] for i in range(2)]
        MT2 = [arena[:, 8192 + 512 * i:8192 + 512 * (i + 1)].bitcast(BF16).rearrange("p (h l) -> p h l", h=8)
               for i in range(2)]
        actT = arena[:, 0:5632].bitcast(BF16).rearrange("p (c t) -> p c t", c=22)
        upre = arena[:, 5632:5632 + 4 * 516].rearrange("p (c t) -> p c t", c=4)
        xio = arena[:, 4096:8192].rearrange("p (j c) -> p j c", j=4)
        xio2 = arena[:, 0:4096].rearrange("p (j c) -> p j c", j=4)

        class Cyc:
            def __init__(self, items):
                self.items = items
                self.i = 0
                self.held = set()

            def next(self):
                for _ in range(len(self.items) + 1):
                    k = self.i % len(self.items)
                    self.i += 1
                    if k not in self.held:
                        return self.items[k]
                raise RuntimeError("all held")

            def hold_next(self):
                for _ in range(len(self.items) + 1):
                    k = self.i % len(self.items)
                    self.i += 1
                    if k not in self.held:
                        self.held.add(k)
                        return k, self.items[k]
                raise RuntimeError("all held")

            def release(self, k):
                self.held.discard(k)
        P = Cyc(banks[:6])
        PS_MEAN, PS_EX2 = banks[6], banks[7]
        TA = Cyc([tA[:, i, :] for i in range(2)])
        XS = Cyc([xsT2[:, i, :] for i in range(2)])
        TB = Cyc([tB[:, i, :] for i in range(4)])
        LNB = Cyc([lnb[:, i, :] for i in range(4)])
        TC = Cyc([tC[:, i, :] for i in range(2)])
        XP = Cyc([xpre[:, i, :] for i in range(2)])
        UP = Cyc([upre[:, i, :] for i in range(4)])
        YN = Cyc([ynb[:, i, :] for i in range(2)])
        XSD = Cyc([xsd[:, i, :] for i in range(2)])
        XSW = Cyc([xsw[:, i, :] for i in range(2)])

        def mm(out, lhsT, rhs, start, stop):
            S.op('pe', lambda e: e.matmul(out, lhsT=lhsT, rhs=rhs, start=start, stop=stop),
                 reads=[lhsT, rhs], writes=[out])

        def tr(out, in_, ident):
            S.op('pe', lambda e: e.transpose(out, in_, ident), reads=[in_, ident], writes=[out])

        def act(out, in_, func, bias=None, scale=None, accum=None, eng='act', force_self=False):
            kw = {}
            rd = [in_]
            wr = [out]
            if bias is not None:
                kw['bias'] = bias
                if not isinstance(bias, float):
                    rd.append(bias)
            if scale is not None:
                kw['scale'] = scale
                if not isinstance(scale, float):
                    rd.append(scale)
            if accum is not None:
                kw['accum_out'] = accum
                wr.append(accum)
            S.op('act', lambda e: e.activation(out=out, in_=in_, func=func, **kw), reads=rd, writes=wr, force_self=force_self)

        def tt(eng, out, in0, in1, op):
            S.op(eng, lambda e: e.tensor_tensor(out=out, in0=in0, in1=in1, op=op), reads=[in0, in1], writes=[out])

        def ts(eng, out, in0, s1, s2, op0, op1=None):
            rd = [in0] + [s for s in (s1, s2) if s is not None and not isinstance(s, float)]
            if op1 is None:
                S.op(eng, lambda e: e.tensor_scalar(out=out, in0=in0, scalar1=s1, scalar2=None, op0=op0), reads=rd, writes=[out])
            else:
                S.op(eng, lambda e: e.tensor_scalar(out=out, in0=in0, scalar1=s1, scalar2=s2, op0=op0, op1=op1), reads=rd, writes=[out])

        def stt(eng, out, in0, scalar, in1, op0, op1):
            rd = [in0, in1] + ([] if isinstance(scalar, float) else [scalar])
            S.op(eng, lambda e: e.scalar_tensor_tensor(out=out, in0=in0, scalar=scalar, in1=in1, op0=op0, op1=op1),
                 reads=rd, writes=[out])

        def cp(eng, out, in_, force_self=False):
            if eng == 'act':
                S.op('act', lambda e: e.copy(out=out, in_=in_), reads=[in_], writes=[out], force_self=force_self)
            else:
                S.op(eng, lambda e: e.tensor_copy(out=out, in_=in_), reads=[in_], writes=[out], force_self=force_self)

        def memset(eng, ap, v):
            S.op(eng, lambda e: e.memset(ap, v), writes=[ap])

        n_prep = 0
        for l in (layers if dbg_stop not in (11, 13) else []):
            for s, (wn, k0, nk, c0, ncol) in enumerate(SLOTS):
                if wn == 'diag':
                    continue
                src = wdr[wn][l, k0 * 128:(k0 + nk) * 128, c0:c0 + ncol].rearrange("(k p) c -> p k c", p=128)
                dst = wscr[l, s].rearrange("p (k c) -> p k c", k=8)[:, 0:nk, 0:ncol]
                psem = 'prep%d' % (n_prep % NPREP)
                if n_prep >= NPREP:
                    S.wait_all('pool', [(psem, 16 * (n_prep // NPREP))])
                S.dma('pool', lambda e, src=src, dst=dst: e.dma_start(out=dst, in_=src), psem,
                      reads=[], writes=[('wscr', l, s)])
                n_prep += 1

        if n_prep:
            S.wait_all('pool', [('prep%d' % i, 16 * ((n_prep - 1 - i) // NPREP + 1)) for i in range(min(NPREP, n_prep))])
        uses = [(l, s) for t in range(NT + (1 if pipe else 0)) for l in layers for s in range(NSL)]
        wstate = {'issued': 0, 'n': 0}

        def wget():
            n = wstate['n']
            while wstate['issued'] < min(len(uses), max(n + NRING - 1, 2)):
                m = wstate['issued']
                l, s = uses[m]
                wn_, k0_, nk_, _, ncol_ = SLOTS[s]
                if wn_ == 'diag':
                    dst = wring[:, m % NRING, 0:3968]
                    src = dscr[l, k0_]
                    rkey = ('dscr', l, k0_)
                else:
                    dst = wring[:, m % NRING, :].rearrange("p (k c) -> p k c", k=8)[:, 0:nk_, 0:ncol_]
                    src = wscr[l, s].rearrange("p (k c) -> p k c", k=8)[:, 0:nk_, 0:ncol_]
                    rkey = ('wscr', l, s)
                S.dma('sp', lambda e, src=src, dst=dst: e.dma_start(out=dst, in_=src), 'wld%d' % (m % NRING),
                      reads=[rkey], writes=[wring[:, m % NRING, :]])
                wstate['issued'] += 1
            wstate['n'] += 1
            if SLOTS[uses[n][1]][0] == 'diag':
                return wring[:, n % NRING, 0:3968].rearrange("p (k m) -> p k m", k=31)
            return wring[:, n % NRING, :].rearrange("p (k c) -> p k c", k=8)

        S.dma('sp', lambda e: e.dma_start(out=cst[:], in_=cst_d), 'ld0', reads=[cst_d], writes=[cst[:]])
        S.dma('sp', lambda e: e.dma_start(out=pp[:], in_=pp_d), 'ld1', reads=[pp_d], writes=[pp[:]])
        S.dma('sp', lambda e: e.dma_start(out=rowc[:].rearrange("p a b c -> p (a b c)"),
                                          in_=rowp_d.rearrange("a b c -> (a b c)").partition_broadcast(128)),
              'ld2', reads=[rowp_d], writes=[rowc[:]])
        if pipe:
            S.dma('sp', lambda e: e.dma_start(out=sel[:], in_=sel_d), 'ld3', reads=[sel_d], writes=[sel[:]])
        cp('dve', identb[:], IDENT)
        ts('dve', onesN[:], ONES, 1.0 / 1024.0, None, ALU.mult)
        ts('dve', onesNb[:], ONES, 1.0 / 1024.0, None, ALU.mult)
        memset('dve', epsb[:, 0:1], LN_EPS)
        memset('dve', epsb[:, 1:2], RMS_EPS)
        memset('dve', ST[:], 0.0)
        memset('dve', STb[:], 0.0)
        memset('dve', vhalo[:], 0.0)
        memset('dve', xbch[:], 0.0)
        memset('dve', ffnh[:], 0.0)
        for l in layers:
            act(rowc[:, l, 1, :], rowc[:, l, 1, :], AF.Exp)
            ts('dve', rowc[:, l, 1, :], rowc[:, l, 1, :], -1.0, None, ALU.mult)

        nbuilt = 0
        for l in layers:
            for c in range(8):
                stg = wring[:, nbuilt % NRING, 0:3968]
                tt('dve', stg.rearrange("p (k m) -> p k m", k=31),
                   IDENT.unsqueeze(1).to_broadcast([128, 31, 128]),
                   pp[:, l, CW + c * 31:CW + c * 31 + 31].unsqueeze(2).to_broadcast([128, 31, 128]), ALU.mult)
                S.dma('sp', lambda e, stg=stg, l=l, c=c: e.dma_start(out=dscr[l, c], in_=stg), 'dst%d' % (nbuilt % NRING),
                      reads=[stg], writes=[('dscr', l, c)])
                nbuilt += 1

        def ln_stats_begin():
            pass

        def ln_finish_stats(blend=False):
            mean_b = stat[:, 0, :]
            rstd_b = stat[:, 1, :]
            cp('act', mean_b, PS_MEAN[:])
            t = TC.next()
            tt('dve', t, mean_b, mean_b, ALU.mult)
            tt('dve', t, PS_EX2[:], t, ALU.subtract)
            act(t, t, AF.Sqrt, bias=epsb[:, 0:1], scale=1.0)
            S.op('dve', lambda e: e.reciprocal(out=rstd_b, in_=t), reads=[t], writes=[rstd_b])
            if blend:
                ts('dve', mean_b, mean_b, sel[:, 0:1], None, ALU.mult)
                ts('dve', rstd_b, rstd_b, sel[:, 0:1], sel[:, 1:2], ALU.mult, ALU.add)
            return mean_b, rstd_b

        def ln_hT(l, gcol, bcol, blend=False):
            for c in range(8):
                sq = LNB.next()
                act(sq, hT[:, c, :], AF.Square)
                rb = LNB.next()
                cp('act', rb, hT[:, c, :])
                mm(PS_MEAN[:], onesNb[:], rb, c == 0, c == 7)
                mm(PS_EX2[:], onesNb[:], sq, c == 0, c == 7)
            mean_b, rstd_b = ln_finish_stats(blend)
            for c in range(8):
                t = TC.next()
                tt('dve', t, hT[:, c, :], mean_b, ALU.subtract)
                tt('dve', t, t, rstd_b, ALU.mult)
                act(hT[:, c, :], t, AF.Identity, bias=pp[:, l, bcol + c:bcol + c + 1], scale=pp[:, l, gcol + c:gcol + c + 1])
                cp('act', hTb[:, c, :], hT[:, c, :])

        def conv_from_psum(ps, pool_cyc, halo, wcol, bcol, K, l, out_acc, act_tap=False):
            xp = pool_cyc.next()
            H = K - 1
            cp('act', xp[:, H:H + 512], ps)
            cp('act', xp[:, 0:H], halo)
            cp('act', halo, xp[:, 512:512 + H])
            if act_tap:
                act(out_acc, ps, AF.Identity, bias=pp[:, l, bcol:bcol + 1], scale=pp[:, l, wcol + H:wcol + H + 1])
            else:
                ts('dve', out_acc, xp[:, H:H + 512], pp[:, l, wcol + H:wcol + H + 1], pp[:, l, bcol:bcol + 1], ALU.mult, ALU.add)
            for k in range(H):
                stt('dve', out_acc, xp[:, k:k + 512], pp[:, l, wcol + k:wcol + k + 1], out_acc, ALU.mult, ALU.add)

        def layer_tile(li, l):
            PPl = pp[:, l, :]
            if dbg_stop == 10:
                return
            S.tag = 'p1_conformer'
            cp('act', vT[:, :, 0:30], vhalo[:, li, :, :])
            for cb in range(2):
                wa = wget()
                wg = wget()
                for ci in range(4):
                    c = 4 * cb + ci
                    pa = P.next()
                    for kc in range(8):
                        mm(pa[:], wa[:, kc, ci * 128:(ci + 1) * 128], hTb[:, kc, :], kc == 0, kc == 7)
                    pg = P.next()
                    for kc in range(8):
                        mm(pg[:], wg[:, kc, ci * 128:(ci + 1) * 128], hTb[:, kc, :], kc == 0, kc == 7)
                    sig = TA.next()
                    act(sig, pg[:], AF.Sigmoid)
                    tt('dve', vT[:, c, 30:542], pa[:], sig, ALU.mult)
                    cp('act', vhalo[:, li, c, :], vT[:, c, 512:542])
                for ci in range(4):
                    c = 4 * cb + ci
                    wd_ = wget()
                    pc = P.next()
                    for k in range(31):
                        mm(pc[:], wd_[:, k, :], vT[:, c, k:k + 512], k == 0, k == 30)
                    act(convT[:, c, :], pc[:], AF.Identity, bias=PPl[:, CB + c:CB + c + 1], scale=1.0)
                    rb = LNB.next()
                    act(rb, pc[:], AF.Identity, bias=PPl[:, CB + c:CB + c + 1], scale=1.0)
                    sq = LNB.next()
                    act(sq, convT[:, c, :], AF.Square)
                    mm(PS_MEAN[:], onesNb[:], rb, c == 0, c == 7)
                    mm(PS_EX2[:], onesNb[:], sq, c == 0, c == 7)
            if dbg_stop == 3:
                return
            S.tag = 'p1_ln'
            mean_b, rstd_b = ln_finish_stats()
            for c in range(8):
                t = TC.next()
                tt('dve', t, convT[:, c, :], mean_b, ALU.subtract)
                tt('dve', t, t, rstd_b, ALU.mult)
                act(sT[:, c, :], t, AF.Silu, bias=PPl[:, CBE + c:CBE + c + 1], scale=PPl[:, CG + c:CG + c + 1])

            if dbg_stop == 1:
                return
            S.tag = 'p2a_dtBC'
            dtt = dts[:, 0, :].rearrange("p (j h) -> p j h", j=4)
            aa = dts[:, 1, :].rearrange("p (j h) -> p j h", j=4)
            acs = dts[:, 2, :].rearrange("p (j h) -> p j h", j=4)
            eacs = dts[:, 3, :].rearrange("p (j h) -> p j h", j=4)
            cdec = dts[:, 4, :].rearrange("p (j h) -> p j h", j=4)
            w2 = dts[:, 5, :].rearrange("p (j h) -> p j h", j=4)
            dtmp = dts[:, 6, :].rearrange("p (j h) -> p j h", j=4)
            wdt = wget()
            pdt = P.next()
            for j in range(4):
                for kc in range(8):
                    mm(pdt[:, j * 32:(j + 1) * 32], hTb[:, kc, j * 128:(j + 1) * 128], wdt[:, kc, 0:32], kc == 0, kc == 7)
            tt('dve', dtmp, pdt[:, 0:128].rearrange("p (j h) -> p j h", j=4),
               rowc[:, l, 0, :].unsqueeze(1).to_broadcast([128, 4, 32]), ALU.add)
            act(dtmp, dtmp, AF.Exp, force_self=True)
            act(dtt, dtmp, AF.Ln, bias=1.0, scale=1.0, force_self=True)
            tt('dve', aa, dtt, rowc[:, l, 1, :].unsqueeze(1).to_broadcast([128, 4, 32]), ALU.mult)
            if dbg_stop == 41:
                return
            pcs = P.next()
            mm(pcs[:, 0:128], UINCL, dts[:, 1, :], True, True)
            mm(pcs[:, 128:256], ONES, dts[:, 1, :], True, True)
            pcs_cs = pcs[:, 0:128].rearrange("p (j h) -> p j h", j=4)
            pcs_tot = pcs[:, 128:256].rearrange("p (j h) -> p j h", j=4)
            if dbg_stop == 420:
                return
            cp('act', acs, pcs_cs, force_self=True)
            if dbg_stop == 421:
                return
            act(eacs, pcs_cs, AF.Exp, force_self=True)
            act(cdec, pcs_tot, AF.Exp, force_self=True)
            if dbg_stop == 422:
                return
            cp('act', dtmp, pcs_tot, force_self=True)
            S.op('dve', lambda e: e.tensor_tensor(out=dtmp, in0=dtmp, in1=acs, op=ALU.subtract), reads=[dtmp, acs], writes=[dtmp], force_self=True)
            if dbg_stop == 423:
                return
            act(dtmp, dtmp, AF.Exp, force_self=True)
            if dbg_stop == 424:
                return
            tt('dve', w2, dtt, dtmp, ALU.mult)

            def xbc_chunk(wblk, ci, q, dest, dest_fp32_tmp=False):
                pX = P.next()
                for kc in range(8):
                    mm(pX[:], wblk[:, kc, ci * 128:(ci + 1) * 128], hTb[:, kc, :], kc == 0, kc == 7)
                acc = TB.next()
                conv_from_psum(pX[:], XP, xbch[:, li, q, :], SW + 4 * q, SB + q, 4, l, acc)
                act(dest, acc, AF.Silu)

            if dbg_stop == 42:
                return
            def b_transpose(g):
                ptb = P.next()
                ptb16 = ptb[:].bitcast(BF16)
                for j in range(4):
                    tr(ptb16[:, j * 128:(j + 1) * 128], BT[:, g, j * 128:(j + 1) * 128], identb[:])
                cp('act', Btok[:, :, g * 128:(g + 1) * 128], ptb16[:, 0:512].rearrange("p (j n) -> p j n", j=4))
            wB = wget()
            for g in range(4):
                xbc_chunk(wB, g, 16 + g, BT[:, g, :])
                if g > 0:
                    b_transpose(g - 1)
            wC = wget()
            for g in range(4):
                xbc_chunk(wC, g, 20 + g, CT[:, g, :])
                if g == 0:
                    b_transpose(3)

            if dbg_stop == 4:
                return
            def xs_transpose(xsT, ci):
                ptx = P.next()
                for j in range(4):
                    tr(ptx[:, j * 128:(j + 1) * 128], xsT[:, j * 128:(j + 1) * 128], IDENT)
                cp('act', xs_tok[:, :, ci * 128:(ci + 1) * 128], ptx[:].rearrange("p (j n) -> p j n", j=4))

            def xs_z(g):
                S.tag = 'p3_xs_z'
                wx = wget()
                wz = wget()
                pend = None
                for ci in range(4):
                    xsT = XS.next()
                    xbc_chunk(wx, ci, 4 * g + ci, xsT)
                    if pend is not None:
                        xs_transpose(*pend)
                    pend = (xsT, ci)
                for j in range(4):
                    pz = P.next()
                    for kc in range(8):
                        mm(pz[:], hTb[:, kc, j * 128:(j + 1) * 128], wz[:, kc, :], kc == 0, kc == 7)
                    act(zs[:, g % 2, j, :], pz[:], AF.Silu)
                    if j == 0:
                        xs_transpose(*pend)

            def S1a(g, j, k):
                S.tag = 'p3_s1'
                hs = slice(8 * g, 8 * g + 8)
                jb = slice(j * 128, (j + 1) * 128)
                xs3 = xs_tok[:, j, :].rearrange("p (h d) -> p h d", h=8)
                tt('dve', xsd[:, k, :].rearrange("p (h d) -> p h d", h=8), xs3,
                   dtt[:, j, hs].unsqueeze(2).to_broadcast([128, 8, 64]), ALU.mult)
                tt('dve', xsw[:, k, :].rearrange("p (h d) -> p h d", h=8), xs3,
                   w2[:, j, hs].unsqueeze(2).to_broadcast([128, 8, 64]), ALU.mult)
                tt('dve', t2b[:, k, :].rearrange("p (h d) -> p h d", h=8), xs3,
                   rowc[:, l, 2, hs].unsqueeze(2).to_broadcast([128, 8, 64]), ALU.mult)
                pcb = P.next()
                mm(pcb[:, 0:128], BT[:, g, jb], CT[:, g, jb], True, True)
                tt('dve', cbm[:, k, :], pcb[:, 0:128], MASKT, ALU.mult)
                tt('dve', Apr2[k].rearrange("p (h l) -> p h l", h=8),
                   UINCL.unsqueeze(1).to_broadcast([128, 8, 128]),
                   aa[:, j, hs].unsqueeze(2).to_broadcast([128, 8, 128]), ALU.mult)
                Es = []
                for half in range(2):
                    pseg = P.next()
                    mm(pseg[:], USTRICT, Apr2[k][:, half * 512:(half + 1) * 512], True, True)
                    E = LNB.next()
                    act(E, pseg[:], AF.Exp)
                    Es.append(E)
                return Es

            def S1b(g, j, k, Es):
                S.tag = 'p3_s1'
                for half in range(2):
                    tt('dve', MT2[k][:, 4 * half:4 * half + 4, :], Es[half].rearrange("p (h l) -> p h l", h=4),
                       cbm[:, k, :].unsqueeze(1).to_broadcast([128, 4, 128]), ALU.mult)

            def S2a(g, j, k):
                S.tag = 'p3_s2'
                hs = slice(8 * g, 8 * g + 8)
                jb = slice(j * 128, (j + 1) * 128)
                xd = xsd[:, k, :]
                xw = xsw[:, k, :]
                pyd = P.next()
                for hh in range(8):
                    mm(pyd[:, hh * 64:(hh + 1) * 64], MT2[k][:, hh, :], xd[:, hh * 64:(hh + 1) * 64], True, True)
                pyo = P.next()
                mm(pyo[:], CT[:, g, jb], STb[:, li, g, :], True, True)
                pst = P.next()
                mm(pst[:], Btok[:, j, g * 128:(g + 1) * 128], xw, True, True)
                t1 = TC.next()
                tt('dve', t1.rearrange("p (h d) -> p h d", h=8), pyo[:].rearrange("p (h d) -> p h d", h=8),
                   eacs[:, j, hs].unsqueeze(2).to_broadcast([128, 8, 64]), ALU.mult)
                Sg = ST[:, li, g, :]
                tt('dve', Sg.rearrange("p (h d) -> p h d", h=8), Sg.rearrange("p (h d) -> p h d", h=8),
                   cdec[:, j, hs].unsqueeze(2).to_broadcast([128, 8, 64]), ALU.mult)
                tt('dve', Sg, Sg, pst[:], ALU.add)
                cp('act', STb[:, li, g, :], Sg)
                tt('dve', t1, t1, pyd[:], ALU.add)
                tt('dve', t1, t1, t2b[:, k, :], ALU.add)
                tt('dve', t1, t1, zs[:, g % 2, j, :], ALU.mult)
                act(t2b[:, k, :], t1, AF.Square, accum=sml[:, 0:1])
                act(sml[:, 1:2], sml[:, 0:1], AF.Ln, bias=epsb[:, 1:2], scale=1.0 / 512.0, force_self=True)
                act(sml[:, 2:3], sml[:, 1:2], AF.Exp, scale=-0.5, force_self=True)
                return t1

            def S2b(g, j, k, t1):
                S.tag = 'p3_s2'
                jb = slice(j * 128, (j + 1) * 128)
                yn = YN.next()
                act(yn, t1, AF.Copy, scale=sml[:, 2:3], force_self=True)
                pty = P.next()
                pty16 = pty[:].bitcast(BF16)
                for ci in range(4):
                    tr(pty16[:, ci * 128:(ci + 1) * 128], yn[:, ci * 128:(ci + 1) * 128], identb[:])
                for ci in range(4):
                    act(ynT[:, 4 * g + ci, jb], pty16[:, ci * 128:(ci + 1) * 128], AF.Copy,
                        scale=PPl[:, NW + 4 * g + ci:NW + 4 * g + ci + 1])

            iters = [(g, j) for g in range(4) for j in range(4)]
            xs_z(0)
            Es = S1a(0, 0, 0)
            S1b(0, 0, 0, Es)
            for i, (g, j) in enumerate(iters):
                nxt = None
                if i + 1 < len(iters):
                    g2, j2 = iters[i + 1]
                    if j2 == 0:
                        xs_z(g2)
                    nxt = (g2, j2, (i + 1) % 2, S1a(g2, j2, (i + 1) % 2))
                t1 = S2a(g, j, i % 2)
                if nxt is not None:
                    S1b(*nxt)
                S2b(g, j, i % 2, t1)

            if dbg_stop == 6:
                return
            S.tag = 'p4_out'
            for ob in range(2):
                wga = wget()
                wco = wget()
                for ci in range(4):
                    pga = P.next()
                    for kc in range(8):
                        mm(pga[:], wga[:, kc, ci * 128:(ci + 1) * 128], hTb[:, kc, :], kc == 0, kc == 7)
                    sg = TA.next()
                    act(sg, pga[:], AF.Sigmoid)
                    pya = P.next()
                    for kc in range(8):
                        mm(pya[:], wco[:, kc, ci * 128:(ci + 1) * 128], sT[:, kc, :], kc == 0, kc == 7)
                    tt('dve', mixa[:, ci, :], pya[:], sg, ALU.mult)
                wsa = wget()
                held = [P.hold_next() for _ in range(4)]
                for ci in range(4):
                    for kc in range(8):
                        mm(held[ci][1][:], wsa[:, kc, ci * 128:(ci + 1) * 128], ynT[:, kc, :], kc == 0, False)
                wsb = wget()
                for ci in range(4):
                    for kc in range(8):
                        mm(held[ci][1][:], wsb[:, kc, ci * 128:(ci + 1) * 128], ynT[:, 8 + kc, :], False, kc == 7)
                wgb = wget()
                for ci in range(4):
                    pgb = P.next()
                    for kc in range(8):
                        mm(pgb[:], wgb[:, kc, ci * 128:(ci + 1) * 128], hTb[:, kc, :], kc == 0, kc == 7)
                    sg = TA.next()
                    act(sg, pgb[:], AF.Sigmoid)
                    t = TC.next()
                    tt('dve', t, held[ci][1][:], sg, ALU.mult)
                    tt('dve', mixTb[:, 4 * ob + ci, :], t, mixa[:, ci, :], ALU.add)
                    P.release(held[ci][0])
            for ob in range(2):
                wwo = wget()
                for ci in range(4):
                    oc = 4 * ob + ci
                    po = P.next()
                    for kc in range(8):
                        mm(po[:], wwo[:, kc, ci * 128:(ci + 1) * 128], mixTb[:, kc, :], kc == 0, kc == 7)
                    stt('dve', hT[:, oc, :], hT[:, oc, :], ALPHA, po[:], ALU.mult, ALU.add)
            S.tag = 'ln1'
            ln_hT(l, L1G, L1B)

            if dbg_stop == 7:
                return
            S.tag = 'p5_ffn'
            for b in range(6):
                wgt = wget()
                wvl = wget()
                nci = 4 if b < 5 else 2
                for ci in range(nci):
                    i = 4 * b + ci
                    pg = P.next()
                    for kc in range(8):
                        mm(pg[:], wgt[:, kc, ci * 128:(ci + 1) * 128], hTb[:, kc, :], kc == 0, kc == 7)
                    pv = P.next()
                    for kc in range(8):
                        mm(pv[:], wvl[:, kc, ci * 128:(ci + 1) * 128], hTb[:, kc, :], kc == 0, kc == 7)
                    ag = TB.next()
                    conv_from_psum(pg[:], UP, ffnh[:, li, i, :], FW + 3 * i, FB + i, 3, l, ag)
                    av = TB.next()
                    conv_from_psum(pv[:], UP, ffnh[:, li, 22 + i, :], FW + 3 * (22 + i), FB + 22 + i, 3, l, av)
                    sg = TA.next()
                    act(sg, ag, AF.Silu)
                    tt('dve', actT[:, i, :], sg, av, ALU.mult)
            for ob in range(2):
                held = [P.hold_next() for _ in range(4)]
                for ks in range(3):
                    wd = wget()
                    nk = 8 if ks < 2 else 6
                    for ci in range(4):
                        for kk in range(nk):
                            mm(held[ci][1][:], wd[:, kk, ci * 128:(ci + 1) * 128], actT[:, 8 * ks + kk, :],
                               ks == 0 and kk == 0, ks == 2 and kk == nk - 1)
                for ci in range(4):
                    oc = 4 * ob + ci
                    stt('dve', hT[:, oc, :], hT[:, oc, :], ALPHA, held[ci][1][:], ALU.mult, ALU.add)
                    P.release(held[ci][0])
            S.tag = 'ln2'
            ln_hT(l, L2G, L2B)

        out_toks = []
        groups = [[0, 1], [2, 3], [4, 5], [6, 7]]
        nsteps = NT + 1 if pipe else NT
        for t in range(nsteps):
            S.tag = 'io_in'
            tx = min(t, NT - 1)
            S.dma('sp', lambda e, tx=tx: e.dma_start(out=xio, in_=x_d[tx * TT:(tx + 1) * TT, :].rearrange("(j p) c -> p j c", p=128)),
                  'xld', reads=[x_d], writes=[xio])
            if pipe:
                ts('dve', xio, xio, sel[:, 0:1], None, ALU.mult)
                if t >= 1:
                    rv = recv_d[(t - 1) % 2]
                    S.dma('sp', lambda e, rv=rv: e.dma_start(out=xio2, in_=rv[0:TT, :].rearrange("(j p) c -> p j c", p=128)),
                          'rld', reads=[rv], writes=[xio2])
                    stt('dve', xio, xio2, sel[:, 1:2], xio, ALU.mult, ALU.add)
            for c in range(8):
                ptx = P.next()
                for j in range(4):
                    tr(ptx[:, j * 128:(j + 1) * 128], xio[:, j, c * 128:(c + 1) * 128], IDENT)
                cp('act', hT[:, c, :], ptx[:])
            if apply_ln_in and dbg_stop not in (11, 12):
                ln_hT(layers[0], LIG, LIB, blend=pipe)
            else:
                for c in range(8):
                    cp('act', hTb[:, c, :], hT[:, c, :])
            for li, l in enumerate(layers):
                if dbg_stop in (11, 12, 13):
                    continue
                layer_tile(li, l)
            if pipe and t == 0:
                for buf in (ST, STb, vhalo, xbch, ffnh):
                    flat = buf[:].rearrange("p a b c -> p (a b c)")
                    ts('dve', flat, flat, sel[:, 2:3], None, ALU.mult)
            S.tag = 'io_out'
            for j in range(4):
                for m in range(2):
                    pto = P.next()
                    for cc in range(4):
                        c = 4 * m + cc
                        tr(pto[:, cc * 128:(cc + 1) * 128], hT[:, c, j * 128:(j + 1) * 128], IDENT)
                    cp('act', xio[:, j, m * 512:(m + 1) * 512], pto[:])
            if pipe:
                if t < NT:
                    sd = send_d[t % 2]
                    rv = recv_d[t % 2]
                    S.dma('sp', lambda e, sd=sd: e.dma_start(out=sd.rearrange("(j p) c -> p j c", p=128), in_=xio),
                          'sst', reads=[xio], writes=[sd])
                    S.dma('pool', lambda e, sd=sd, rv=rv: e.collective_compute("AllGather", ALU.bypass, replica_groups=groups, ins=[sd], outs=[rv]),
                          'cc%d' % (t % 2), reads=[sd], writes=[rv], inc=1)
                if t >= 1:
                    ty = t - 1
                    tok = S.dma('sp', lambda e, ty=ty: e.dma_start(out=y_d[ty * TT:(ty + 1) * TT, :].rearrange("(j p) c -> p j c", p=128), in_=xio),
                                'yst', reads=[xio], writes=[('y', ty)])
                    out_toks.append(tok)
            else:
                tok = S.dma('sp', lambda e, t=t: e.dma_start(out=y_d[t * TT:(t + 1) * TT, :].rearrange("(j p) c -> p j c", p=128), in_=xio),
                            'yst', reads=[xio], writes=[('y', t)])
                out_toks.append(tok)
        S.wait_all('sp', out_toks[-1:])
        S.emit()
    return nc, S


def _pack_params(inp):
    pp = np.zeros((128, 2, NPP), np.float32)

    def fm(v, nch):
        return np.ascontiguousarray(v.reshape(nch, 128).T)
    for l in range(2):
        w = inp['conv_dw_w'][l]
        pp[:, l, CW:CW + 248] = w.T.reshape(8, 128, 31).transpose(1, 0, 2).reshape(128, 248)
        pp[:, l, CB:CB + 8] = fm(inp['conv_dw_b'][l], 8)
        pp[:, l, CG:CG + 8] = fm(inp['conv_ln_g'][l], 8)
        pp[:, l, CBE:CBE + 8] = fm(inp['conv_ln_b'][l], 8)
        w = inp['ssm_conv_w'][l]
        pp[:, l, SW:SW + 96] = w.T.reshape(24, 128, 4).transpose(1, 0, 2).reshape(128, 96)
        pp[:, l, SB:SB + 24] = fm(inp['ssm_conv_b'][l], 24)
        pp[:, l, NW:NW + 16] = fm(inp['ssm_norm_w'][l], 16)
        w = inp['ffn_dw_w'][l]
        pp[:, l, FW:FW + 132] = w.T.reshape(44, 128, 3).transpose(1, 0, 2).reshape(128, 132)
        pp[:, l, FB:FB + 44] = fm(inp['ffn_dw_b'][l], 44)
        pp[:, l, L1G:L1G + 8] = fm(inp['ln1_g'][l], 8)
        pp[:, l, L1B:L1B + 8] = fm(inp['ln1_b'][l], 8)
        pp[:, l, L2G:L2G + 8] = fm(inp['ln2_g'][l], 8)
        pp[:, l, L2B:L2B + 8] = fm(inp['ln2_b'][l], 8)
        pp[:, l, LIG:LIG + 8] = fm(inp['ln_in_g'], 8)
        pp[:, l, LIB:LIB + 8] = fm(inp['ln_in_b'], 8)
    rowp = np.zeros((2, 3, 32), np.float32)
    for l in range(2):
        rowp[l, 0] = inp['ssm_dt_bias'][l]
        rowp[l, 1] = inp['ssm_a_log'][l]
        rowp[l, 2] = inp['ssm_d'][l]
    cst = np.zeros((128, 5, 128), np.float32)
    cst[:, 0, :] = np.eye(128)
    cst[:, 1, :] = np.triu(np.ones((128, 128)))
    cst[:, 2, :] = 1.0
    cst[:, 3, :] = np.tril(np.ones((128, 128)), -1)
    cst[:, 4, :] = np.triu(np.ones((128, 128)))
    return pp, rowp, cst


_CACHE = {}


def _get_prog(T, layers, apply_ln_in):
    key = (T, tuple(layers), apply_ln_in)
    if key not in _CACHE:
        _CACHE[key] = build_program(T, list(layers), apply_ln_in)
    return _CACHE[key][0]


def run_layers(xs, inp, layers, apply_ln_in, n_cores):
    T = xs[0].shape[0]
    nc = _get_prog(T, layers, apply_ln_in)
    pp, rowp, cst = _pack_params(inp)
    wts = {k: np.ascontiguousarray(inp[k], dtype=np.float32) for k in
           ('w_in', 'w_conv_out', 'w_ssm_out', 'w_o', 'w_ffn_up', 'w_ffn_down')}
    in_maps = []
    for c in range(n_cores):
        m = dict(wts)
        m['x'] = np.ascontiguousarray(xs[c % len(xs)], dtype=np.float32)
        m['pp'] = pp
        m['rowp'] = rowp
        m['cst'] = cst
        in_maps.append(m)
    res = run_bass_kernel_spmd(nc, in_maps, core_ids=list(range(n_cores)))
    return [np.asarray(res.results[c]['y']) for c in range(len(xs))]


def run_pipe(xs, inp):
    T = xs[0].shape[0]
    key = (T, 'pipe')
    if key not in _CACHE:
        _CACHE[key] = build_program(T, [0], True, pipe=True)
    nc = _CACHE[key][0]
    pp, rowp, cst = _pack_params(inp)
    zeros = np.zeros((T, D), np.float32)
    in_maps = []
    for c in range(8):
        b, lc = c // 2, c % 2
        m = {k: np.ascontiguousarray(inp[k][lc:lc + 1], dtype=np.float32) for k in
             ('w_in', 'w_conv_out', 'w_ssm_out', 'w_o', 'w_ffn_up', 'w_ffn_down')}
        ppc = np.ascontiguousarray(pp[:, lc:lc + 1, :])
        sel = np.zeros((128, 4), np.float32)
        if lc == 0:
            m['x'] = np.ascontiguousarray(xs[b % len(xs)], dtype=np.float32)
            sel[:, 0] = 1.0
            sel[:, 2] = 1.0
        else:
            m['x'] = zeros
            sel[:, 1] = 1.0
            ppc[:, 0, LIG:LIG + 8] = 1.0
            ppc[:, 0, LIB:LIB + 8] = 0.0
        m['pp'] = ppc
        m['rowp'] = np.ascontiguousarray(rowp[lc:lc + 1])
        m['cst'] = cst
        m['sel'] = sel
        in_maps.append(m)
    res = run_bass_kernel_spmd(nc, in_maps, core_ids=list(range(8)))
    return [np.asarray(res.results[2 * b + 1]['y']) for b in range(len(xs))]


FUSED = True
PIPE = True


def kernel(**inputs):
    inp = {k: np.asarray(v) for k, v in inputs.items()}
    x = inp['x'].astype(np.float32)
    B = x.shape[0]
    xs = [x[b] for b in range(B)]
    if PIPE:
        ys = run_pipe(xs, inp)
    elif FUSED:
        ys = run_layers(xs, inp, (0, 1), True, 4)
    else:
        h = run_layers(xs, inp, (0,), True, 8)
        ys = run_layers(h, inp, (1,), False, 8)
    return np.stack(ys, 0).astype(np.float32)
```

```python
import contextlib
import numpy as np
import concourse.bass as bass
import concourse.mybir as mybir
from concourse.bass_utils import run_bass_kernel_spmd

F32 = mybir.dt.float32
BF16 = mybir.dt.bfloat16
ALU = mybir.AluOpType
AF = mybir.ActivationFunctionType

CELL = 512
D = 1024
IN_DIM = 9248
TT = 512
DEPTH = 2
ALPHA = float((2 * DEPTH) ** 0.25)
LN_EPS = 1e-5
RMS_EPS = 1e-5


def _dtsize(dt):
    s = str(dt)
    if '64' in s:
        return 8
    if '32' in s:
        return 4
    if '16' in s:
        return 2
    return 1


class Sched:
    ENGS = ('pe', 'act', 'dve', 'pool', 'sp')

    def __init__(self, nc, same_engine_sync=True):
        self.nc = nc
        self.same_engine_sync = same_engine_sync
        self.streams = {e: [] for e in self.ENGS}
        self.count = {e: 0 for e in self.ENGS}
        self.dma_count = {}
        self.dma_inc = {}
        self.cells = {}
        self.waited = {e: {} for e in self.ENGS}
        self.n_ops = 0
        self.tag = ''
        self.tags = {e: [] for e in self.ENGS}

    def _keys(self, r):
        if isinstance(r, tuple):
            return [r]
        ap = r
        name = ap.tensor.name
        sp_ = str(ap.space).upper()
        if 'DRAM' in sp_ or 'PSUM' in sp_:
            return [(name,)]
        pairs = ap.ap
        pstep = pairs[0][0]
        sz = _dtsize(ap.dtype)
        off = ap.offset % pstep if pstep > 0 else ap.offset
        hull = 0
        for (st, cn) in pairs[1:]:
            hull += abs(st) * (cn - 1)
        lo = off * sz
        hi = (off + hull + 1) * sz
        return [(name, c) for c in range(lo // CELL, (hi - 1) // CELL + 1)]

    def _deps(self, reads, writes):
        deps = {}
        rk = []
        for r in reads:
            rk += self._keys(r)
        wk = []
        for w in writes:
            wk += self._keys(w)
        cells = self.cells
        for k in rk:
            c = cells.get(k)
            if c is not None and c[0] is not None:
                kk, vv = c[0]
                if deps.get(kk, 0) < vv:
                    deps[kk] = vv
        for k in wk:
            c = cells.get(k)
            if c is not None:
                if c[0] is not None:
                    kk, vv = c[0]
                    if deps.get(kk, 0) < vv:
                        deps[kk] = vv
                for kk, vv in c[1].items():
                    if deps.get(kk, 0) < vv:
                        deps[kk] = vv
        return deps, rk, wk

    def _commit(self, rk, wk, tok):
        cells = self.cells
        for k in rk:
            c = cells.get(k)
            if c is None:
                c = [None, {}]
                cells[k] = c
            if c[1].get(tok[0], 0) < tok[1]:
                c[1][tok[0]] = tok[1]
        for k in wk:
            cells[k] = [tok, {}]

    def _filter(self, eng, deps, force_self=False):
        waits = []
        wd = self.waited[eng]
        for k, v in deps.items():
            if k == eng and (eng == 'pe' or not (self.same_engine_sync or force_self)):
                continue
            if wd.get(k, 0) >= v:
                continue
            wd[k] = v
            waits.append((k, v))
        return waits

    def op(self, eng, fn, reads=(), writes=(), force_self=False):
        deps, rk, wk = self._deps(reads, writes)
        waits = self._filter(eng, deps, force_self)
        self.count[eng] += 1
        tok = (eng, self.count[eng])
        self.tags[eng].append(self.tag)
        self.streams[eng].append((waits, fn, tok))
        self._commit(rk, wk, tok)
        self.n_ops += 1
        return tok

    def dma(self, queue, fn, sem, reads=(), writes=(), inc=16):
        deps, rk, wk = self._deps(reads, writes)
        waits = self._filter(queue, deps)
        self.dma_count[sem] = self.dma_count.get(sem, 0) + 1
        self.dma_inc[sem] = inc
        tok = (sem, inc * self.dma_count[sem])
        self.streams[queue].append((waits, fn, tok))
        self._commit(rk, wk, tok)
        self.n_ops += 1
        return tok

    def wait_all(self, eng, toks):
        deps = {}
        for k, v in toks:
            deps[k] = max(deps.get(k, 0), v)
        waits = self._filter(eng, deps)
        if waits:
            self.streams[eng].append((waits, None, None))

    def emit(self):
        nc = self.nc
        semnames = list(self.ENGS) + list(self.dma_count.keys())
        with contextlib.ExitStack() as st:
            sems = {}
            for n in semnames:
                sems[n] = st.enter_context(nc.semaphore("s_" + n))
            block = st.enter_context(nc.Block())
            engmap = {'pe': block.tensor, 'act': block.scalar, 'dve': block.vector,
                      'pool': block.gpsimd, 'sp': block.sync}

            def make(ename):
                stream = self.streams[ename]

                def body(e):
                    for waits, fn, tok in stream:
                        for (k, v) in waits:
                            e.wait_ge(sems[k], v)
                        if fn is None:
                            continue
                        ins = fn(e)
                        if tok[0] == ename:
                            ins.then_inc(sems[tok[0]], 1)
                        else:
                            ins.then_inc(sems[tok[0]], self.dma_inc[tok[0]])
                return body
            for ename in self.ENGS:
                if self.streams[ename]:
                    engmap[ename](make(ename))


CW, CB, CG, CBE = 0, 248, 256, 264
SW, SB, NW = 272, 368, 392
FW, FB = 408, 540
L1G, L1B, L2G, L2B = 584, 592, 600, 608
LIG, LIB = 616, 624
NPP = 632

def _slot_table():
    s = []
    A = lambda name, k0, nk, c0, ncol: s.append((name, k0, nk, c0, ncol))
    A('w_in', 0, 8, 0, 512)
    A('w_in', 0, 8, 1024, 512)
    A('w_in', 0, 8, 512, 512)
    A('w_in', 0, 8, 1536, 512)
    A('w_in', 0, 8, 7168, 32)
    A('w_in', 0, 8, 6144, 512)
    A('w_in', 0, 8, 6656, 512)
    def XZ(g):
        A('w_in', 0, 8, 4096 + 512 * g, 512)
        A('w_in', 0, 8, 2048 + 512 * g, 512)
    XZ(0)
    for c in (0, 1, 2):
        A('diag', c, 8, 0, 496)
    XZ(1)
    for c in (3, 4, 5, 6):
        A('diag', c, 8, 0, 496)
    XZ(2)
    A('diag', 7, 8, 0, 496)
    XZ(3)
    for ob in range(2):
        A('w_in', 0, 8, 7200 + 512 * ob, 512)
        A('w_conv_out', 0, 8, 512 * ob, 512)
        A('w_ssm_out', 0, 8, 512 * ob, 512)
        A('w_ssm_out', 8, 8, 512 * ob, 512)
        A('w_in', 0, 8, 8224 + 512 * ob, 512)
    for ob in range(2):
        A('w_o', 0, 8, 512 * ob, 512)
    for b in range(6):
        ncol = 512 if b < 5 else 256
        A('w_ffn_up', 0, 8, 512 * b, ncol)
        A('w_ffn_up', 0, 8, 2816 + 512 * b, ncol)
    for ob in range(2):
        A('w_ffn_down', 0, 8, 512 * ob, 512)
        A('w_ffn_down', 8, 8, 512 * ob, 512)
        A('w_ffn_down', 16, 6, 512 * ob, 512)
    return s


SLOTS = _slot_table()
NSL = len(SLOTS)
NRING = 4
NPREP = 4


def build_program(T, layers, apply_ln_in, conv_pool_chunks=(), same_engine_sync=False, dbg_stop=0, dbg_var=0, use_pool=False, pipe=False):
    PL = 'pool' if use_pool else 'dve'
    NT = T // TT
    NLW = 1 if (pipe or len(layers) == 1) else 2
    NL = len(layers)
    nc = bass.Bass("TRN2", target_bir_lowering=False, num_devices=8) if pipe else bass.Bass("TRN2", target_bir_lowering=False)
    S = Sched(nc, same_engine_sync=same_engine_sync)

    def din(name, shape):
        return nc.dram_tensor(name, shape, F32, kind="ExternalInput").ap()
    x_d = din("x", [T, D])
    wdr = {
        'w_in': din("w_in", [NLW, D, IN_DIM]),
        'w_conv_out': din("w_conv_out", [NLW, D, D]),
        'w_ssm_out': din("w_ssm_out", [NLW, 2048, D]),
        'w_o': din("w_o", [NLW, D, D]),
        'w_ffn_up': din("w_ffn_up", [NLW, D, 5632]),
        'w_ffn_down': din("w_ffn_down", [NLW, 2816, D]),
    }
    pp_d = din("pp", [128, NLW, NPP])
    rowp_d = din("rowp", [NLW, 3, 32])
    cst_d = din("cst", [128, 5, 128])
    y_d = nc.dram_tensor("y", [T, D], F32, kind="ExternalOutput").ap()
    if pipe:
        sel_d = din("sel", [128, 4])
        send_d = [nc.dram_tensor("send%d" % i, [TT, D], F32, kind="Internal").ap() for i in range(2)]
        recv_d = [nc.dram_tensor("recv%d" % i, [2 * TT, D], F32, kind="Internal").ap() for i in range(2)]
    wscr = nc.dram_tensor("wscr", [NLW, NSL, 128, 4096], BF16, kind="Internal").ap()
    dscr = nc.dram_tensor("dscr", [NLW, 8, 128, 3968], BF16, kind="Internal").ap()

    with contextlib.ExitStack() as st:
        def sb(name, shape, dt=F32):
            return st.enter_context(nc.sbuf_tensor("sb_" + name, shape, dt))

        def psb(name):
            return st.enter_context(nc.psum_tensor(name, [128, 512], F32))
        cst = sb("cst", [128, 5, 128])
        IDENT, UINCL, ONES, USTRICT, MASKT = (cst[:, i, :] for i in range(5))
        identb = sb("identb", [128, 128], BF16)
        onesN = sb("onesN", [128, 128])
        pp = sb("pp", [128, NLW, NPP])
        rowc = sb("rowc", [128, NLW, 3, 32])
        epsb = sb("epsb", [128, 2])
        sel = sb("sel", [128, 4])
        ST = sb("ST", [128, NL, 4, 512])
        STb = sb("STb", [128, NL, 4, 512], BF16)
        vhalo = sb("vhalo", [128, NL, 8, 30], BF16)
        xbch = sb("xbch", [128, NL, 24, 3])
        ffnh = sb("ffnh", [128, NL, 44, 2])
        hT = sb("hT", [128, 8, 512])
        hTb = sb("hTb", [128, 8, 512], BF16)
        wring = sb("wring", [128, NRING, 4096], BF16)
        arena = sb("arena", [128, 9472])
        ynT = sb("ynT", [128, 16, 512], BF16)
        stat = sb("stat", [128, 2, 512])
        sT = sb("sT", [128, 8, 512], BF16)
        BT = sb("BT", [128, 4, 512], BF16)
        CT = sb("CT", [128, 4, 512], BF16)
        Btok = sb("Btok", [128, 4, 512], BF16)
        dts = sb("dts", [128, 7, 128])
        xs_tok = sb("xs_tok", [128, 4, 512])
        zs = sb("zs", [128, 2, 4, 512], BF16)
        cbm = sb("cbm", [128, 2, 128], BF16)
        tA = sb("tA", [128, 2, 512])
        xsT2 = sb("xsT2", [128, 2, 512])
        t2b = sb("t2b", [128, 2, 512])
        tB = sb("tB", [128, 2, 512])
        lnb = sb("lnb", [128, 4, 512], BF16)
        onesNb = sb("onesNb", [128, 128], BF16)
        tC = sb("tC", [128, 2, 512])
        xpre = sb("xpre", [128, 2, 516])
        ynb = sb("ynb", [128, 2, 512], BF16)
        xsd = sb("xsd", [128, 2, 512], BF16)
        xsw = sb("xsw", [128, 2, 512], BF16)
        sml = sb("sml", [128, 8])
        banks = [psb("ps%d" % i) for i in range(8)]

        vT = arena[:, 0:4 * 542].bitcast(BF16).rearrange("p (c t) -> p c t", c=8)
        convT = arena[:, 2176:2176 + 4096].rearrange("p (c t) -> p c t", c=8)
        mixTb = arena[:, 4096:6144].bitcast(BF16).rearrange("p (c t) -> p c t", c=8)
        mixa = arena[:, 6144:8192].rearrange("p (c t) -> p c t", c=4)
        Apr2 = [arena[:, 6272 + 1024 * i:6272 + 1024 * (i + 1)] for i in range(2)]
        MT2 = [arena[:, 8320 + 512 * i:8320 + 512 * (i + 1)].bitcast(BF16).rearrange("p (h l) -> p h l", h=8)
               for i in range(2)]
        actT = arena[:, 0:5632].bitcast(BF16).rearrange("p (c t) -> p c t", c=22)
        upre = arena[:, 5632:5632 + 4 * 516].rearrange("p (c t) -> p c t", c=4)
        xio = arena[:, 4096:8192].rearrange("p (j c) -> p j c", j=4)
        xio2 = arena[:, 0:4096].rearrange("p (j c) -> p j c", j=4)

        class Cyc:
            def __init__(self, items):
                self.items = items
                self.i = 0
                self.held = set()

            def next(self):
                for _ in range(len(self.items) + 1):
                    k = self.i % len(self.items)
                    self.i += 1
                    if k not in self.held:
                        return self.items[k]
                raise RuntimeError("all held")

            def hold_next(self):
                for _ in range(len(self.items) + 1):
                    k = self.i % len(self.items)
                    self.i += 1
                    if k not in self.held:
                        self.held.add(k)
                        return k, self.items[k]
                raise RuntimeError("all held")

            def release(self, k):
                self.held.discard(k)
        P = Cyc(banks[:6])
        PS_MEAN, PS_EX2 = banks[6], banks[7]
        TA = Cyc([tA[:, i, :] for i in range(2)])
        XS = Cyc([xsT2[:, i, :] for i in range(2)])
        TB = Cyc([tB[:, i, :] for i in range(2)])
        LNB = Cyc([lnb[:, i, :] for i in range(4)])
        TC = Cyc([tC[:, i, :] for i in range(2)])
        XP = Cyc([xpre[:, i, :] for i in range(2)])
        UP = Cyc([upre[:, i, :] for i in range(4)])
        YN = Cyc([ynb[:, i, :] for i in range(2)])
        XSD = Cyc([xsd[:, i, :] for i in range(2)])
        XSW = Cyc([xsw[:, i, :] for i in range(2)])

        def mm(out, lhsT, rhs, start, stop):
            S.op('pe', lambda e: e.matmul(out, lhsT=lhsT, rhs=rhs, start=start, stop=stop),
                 reads=[lhsT, rhs], writes=[out])

        def tr(out, in_, ident):
            S.op('pe', lambda e: e.transpose(out, in_, ident), reads=[in_, ident], writes=[out])

        def act(out, in_, func, bias=None, scale=None, accum=None, eng='act', force_self=False):
            kw = {}
            rd = [in_]
            wr = [out]
            if bias is not None:
                kw['bias'] = bias
                if not isinstance(bias, float):
                    rd.append(bias)
            if scale is not None:
                kw['scale'] = scale
                if not isinstance(scale, float):
                    rd.append(scale)
            if accum is not None:
                kw['accum_out'] = accum
                wr.append(accum)
            S.op('act', lambda e: e.activation(out=out, in_=in_, func=func, **kw), reads=rd, writes=wr, force_self=force_self)

        def tt(eng, out, in0, in1, op):
            S.op(eng, lambda e: e.tensor_tensor(out=out, in0=in0, in1=in1, op=op), reads=[in0, in1], writes=[out])

        def ts(eng, out, in0, s1, s2, op0, op1=None):
            rd = [in0] + [s for s in (s1, s2) if s is not None and not isinstance(s, float)]
            if op1 is None:
                S.op(eng, lambda e: e.tensor_scalar(out=out, in0=in0, scalar1=s1, scalar2=None, op0=op0), reads=rd, writes=[out])
            else:
                S.op(eng, lambda e: e.tensor_scalar(out=out, in0=in0, scalar1=s1, scalar2=s2, op0=op0, op1=op1), reads=rd, writes=[out])

        def stt(eng, out, in0, scalar, in1, op0, op1):
            rd = [in0, in1] + ([] if isinstance(scalar, float) else [scalar])
            S.op(eng, lambda e: e.scalar_tensor_tensor(out=out, in0=in0, scalar=scalar, in1=in1, op0=op0, op1=op1),
                 reads=rd, writes=[out])

        def cp(eng, out, in_, force_self=False):
            if eng == 'act':
                S.op('act', lambda e: e.copy(out=out, in_=in_), reads=[in_], writes=[out], force_self=force_self)
            else:
                S.op(eng, lambda e: e.tensor_copy(out=out, in_=in_), reads=[in_], writes=[out], force_self=force_self)

        def memset(eng, ap, v):
            S.op(eng, lambda e: e.memset(ap, v), writes=[ap])

        n_prep = 0
        for l in (layers if dbg_stop not in (11, 13) else []):
            for s, (wn, k0, nk, c0, ncol) in enumerate(SLOTS):
                if wn == 'diag':
                    continue
                src = wdr[wn][l, k0 * 128:(k0 + nk) * 128, c0:c0 + ncol].rearrange("(k p) c -> p k c", p=128)
                dst = wscr[l, s].rearrange("p (k c) -> p k c", k=8)[:, 0:nk, 0:ncol]
                psem = 'prep%d' % (n_prep % NPREP)
                if n_prep >= NPREP:
                    S.wait_all('pool', [(psem, 16 * (n_prep // NPREP))])
                S.dma('pool', lambda e, src=src, dst=dst: e.dma_start(out=dst, in_=src), psem,
                      reads=[], writes=[('wscr', l, s)])
                n_prep += 1

        if n_prep:
            S.wait_all('pool', [('prep%d' % i, 16 * ((n_prep - 1 - i) // NPREP + 1)) for i in range(min(NPREP, n_prep))])
        uses = [(l, s) for t in range(NT + (1 if pipe else 0)) for l in layers for s in range(NSL)]
        wstate = {'issued': 0, 'n': 0}

        def wget():
            n = wstate['n']
            while wstate['issued'] < min(len(uses), max(n + NRING - 1, 2)):
                m = wstate['issued']
                l, s = uses[m]
                wn_, k0_, nk_, _, ncol_ = SLOTS[s]
                if wn_ == 'diag':
                    dst = wring[:, m % NRING, 0:3968]
                    src = dscr[l, k0_]
                    rkey = ('dscr', l, k0_)
                else:
                    dst = wring[:, m % NRING, :].rearrange("p (k c) -> p k c", k=8)[:, 0:nk_, 0:ncol_]
                    src = wscr[l, s].rearrange("p (k c) -> p k c", k=8)[:, 0:nk_, 0:ncol_]
                    rkey = ('wscr', l, s)
                S.dma('sp', lambda e, src=src, dst=dst: e.dma_start(out=dst, in_=src), 'wld%d' % (m % NRING),
                      reads=[rkey], writes=[wring[:, m % NRING, :]])
                wstate['issued'] += 1
            wstate['n'] += 1
            if SLOTS[uses[n][1]][0] == 'diag':
                return wring[:, n % NRING, 0:3968].rearrange("p (k m) -> p k m", k=31)
            return wring[:, n % NRING, :].rearrange("p (k c) -> p k c", k=8)

        S.dma('sp', lambda e: e.dma_start(out=cst[:], in_=cst_d), 'ld0', reads=[cst_d], writes=[cst[:]])
        S.dma('sp', lambda e: e.dma_start(out=pp[:], in_=pp_d), 'ld1', reads=[pp_d], writes=[pp[:]])
        S.dma('sp', lambda e: e.dma_start(out=rowc[:].rearrange("p a b c -> p (a b c)"),
                                          in_=rowp_d.rearrange("a b c -> (a b c)").partition_broadcast(128)),
              'ld2', reads=[rowp_d], writes=[rowc[:]])
        if pipe:
            S.dma('sp', lambda e: e.dma_start(out=sel[:], in_=sel_d), 'ld3', reads=[sel_d], writes=[sel[:]])
        cp('dve', identb[:], IDENT)
        ts('dve', onesN[:], ONES, 1.0 / 1024.0, None, ALU.mult)
        ts('dve', onesNb[:], ONES, 1.0 / 1024.0, None, ALU.mult)
        memset('dve', epsb[:, 0:1], LN_EPS)
        memset('dve', epsb[:, 1:2], RMS_EPS)
        memset('dve', ST[:], 0.0)
        memset('dve', STb[:], 0.0)
        memset('dve', vhalo[:], 0.0)
        memset('dve', xbch[:], 0.0)
        memset('dve', ffnh[:], 0.0)
        for l in layers:
            act(rowc[:, l, 1, :], rowc[:, l, 1, :], AF.Exp)
            ts('dve', rowc[:, l, 1, :], rowc[:, l, 1, :], -1.0, None, ALU.mult)

        nbuilt = 0
        for l in layers:
            for c in range(8):
                stg = wring[:, nbuilt % NRING, 0:3968]
                tt('dve', stg.rearrange("p (k m) -> p k m", k=31),
                   IDENT.unsqueeze(1).to_broadcast([128, 31, 128]),
                   pp[:, l, CW + c * 31:CW + c * 31 + 31].unsqueeze(2).to_broadcast([128, 31, 128]), ALU.mult)
                S.dma('sp', lambda e, stg=stg, l=l, c=c: e.dma_start(out=dscr[l, c], in_=stg), 'dst%d' % (nbuilt % NRING),
                      reads=[stg], writes=[('dscr', l, c)])
                nbuilt += 1

        def ln_stats_begin():
            pass

        def ln_finish_stats(blend=False):
            mean_b = stat[:, 0, :]
            rstd_b = stat[:, 1, :]
            cp('act', mean_b, PS_MEAN[:])
            t = TC.next()
            tt('dve', t, mean_b, mean_b, ALU.mult)
            tt('dve', t, PS_EX2[:], t, ALU.subtract)
            act(t, t, AF.Sqrt, bias=epsb[:, 0:1], scale=1.0)
            S.op('dve', lambda e: e.reciprocal(out=rstd_b, in_=t), reads=[t], writes=[rstd_b])
            if blend:
                ts('dve', mean_b, mean_b, sel[:, 0:1], None, ALU.mult)
                ts('dve', rstd_b, rstd_b, sel[:, 0:1], sel[:, 1:2], ALU.mult, ALU.add)
            return mean_b, rstd_b

        def ln_hT(l, gcol, bcol, blend=False):
            for c in range(8):
                sq = LNB.next()
                act(sq, hT[:, c, :], AF.Square)
                rb = LNB.next()
                cp('act', rb, hT[:, c, :])
                mm(PS_MEAN[:], onesNb[:], rb, c == 0, c == 7)
                mm(PS_EX2[:], onesNb[:], sq, c == 0, c == 7)
            mean_b, rstd_b = ln_finish_stats(blend)
            for c in range(8):
                t = TC.next()
                tt('dve', t, hT[:, c, :], mean_b, ALU.subtract)
                tt('dve', t, t, rstd_b, ALU.mult)
                act(hT[:, c, :], t, AF.Identity, bias=pp[:, l, bcol + c:bcol + c + 1], scale=pp[:, l, gcol + c:gcol + c + 1])
                cp('act', hTb[:, c, :], hT[:, c, :])

        def conv_from_psum(ps, pool_cyc, halo, wcol, bcol, K, l, out_acc, act_tap=False):
            xp = pool_cyc.next()
            H = K - 1
            cp('act', xp[:, H:H + 512], ps)
            cp('act', xp[:, 0:H], halo)
            cp('act', halo, xp[:, 512:512 + H])
            if act_tap:
                act(out_acc, ps, AF.Identity, bias=pp[:, l, bcol:bcol + 1], scale=pp[:, l, wcol + H:wcol + H + 1])
            else:
                ts('dve', out_acc, xp[:, H:H + 512], pp[:, l, wcol + H:wcol + H + 1], pp[:, l, bcol:bcol + 1], ALU.mult, ALU.add)
            for k in range(H):
                stt('dve', out_acc, xp[:, k:k + 512], pp[:, l, wcol + k:wcol + k + 1], out_acc, ALU.mult, ALU.add)

        def layer_tile(li, l):
            PPl = pp[:, l, :]
            if dbg_stop == 10:
                return
            S.tag = 'p1_conformer'
            cp('act', vT[:, :, 0:30], vhalo[:, li, :, :])
            for cb in range(2):
                wa = wget()
                wg = wget()
                for ci in range(4):
                    c = 4 * cb + ci
                    pa = P.next()
                    for kc in range(8):
                        mm(pa[:], wa[:, kc, ci * 128:(ci + 1) * 128], hTb[:, kc, :], kc == 0, kc == 7)
                    pg = P.next()
                    for kc in range(8):
                        mm(pg[:], wg[:, kc, ci * 128:(ci + 1) * 128], hTb[:, kc, :], kc == 0, kc == 7)
                    sig = TA.next()
                    act(sig, pg[:], AF.Sigmoid)
                    tt('dve', vT[:, c, 30:542], pa[:], sig, ALU.mult)
                    cp('act', vhalo[:, li, c, :], vT[:, c, 512:542])

            def conv_unit(c):
                S.tag = 'p1_conv'
                wd_ = wget()
                pc = P.next()
                for k in range(31):
                    mm(pc[:], wd_[:, k, :], vT[:, c, k:k + 512], k == 0, k == 30)
                act(convT[:, c, :], pc[:], AF.Identity, bias=PPl[:, CB + c:CB + c + 1], scale=1.0)
                rb = LNB.next()
                act(rb, pc[:], AF.Identity, bias=PPl[:, CB + c:CB + c + 1], scale=1.0)
                sq = LNB.next()
                act(sq, convT[:, c, :], AF.Square)
                mm(PS_MEAN[:], onesNb[:], rb, c == 0, c == 7)
                mm(PS_EX2[:], onesNb[:], sq, c == 0, c == 7)

            def p1_ln():
                S.tag = 'p1_ln'
                mean_b, rstd_b = ln_finish_stats()
                for c in range(8):
                    t = TC.next()
                    tt('dve', t, convT[:, c, :], mean_b, ALU.subtract)
                    tt('dve', t, t, rstd_b, ALU.mult)
                    act(sT[:, c, :], t, AF.Silu, bias=PPl[:, CBE + c:CBE + c + 1], scale=PPl[:, CG + c:CG + c + 1])

            if dbg_stop == 1:
                return
            S.tag = 'p2a_dtBC'
            dtt = dts[:, 0, :].rearrange("p (j h) -> p j h", j=4)
            aa = dts[:, 1, :].rearrange("p (j h) -> p j h", j=4)
            acs = dts[:, 2, :].rearrange("p (j h) -> p j h", j=4)
            eacs = dts[:, 3, :].rearrange("p (j h) -> p j h", j=4)
            cdec = dts[:, 4, :].rearrange("p (j h) -> p j h", j=4)
            w2 = dts[:, 5, :].rearrange("p (j h) -> p j h", j=4)
            dtmp = dts[:, 6, :].rearrange("p (j h) -> p j h", j=4)
            wdt = wget()
            pdt = P.next()
            for j in range(4):
                for kc in range(8):
                    mm(pdt[:, j * 32:(j + 1) * 32], hTb[:, kc, j * 128:(j + 1) * 128], wdt[:, kc, 0:32], kc == 0, kc == 7)
            tt('dve', dtmp, pdt[:, 0:128].rearrange("p (j h) -> p j h", j=4),
               rowc[:, l, 0, :].unsqueeze(1).to_broadcast([128, 4, 32]), ALU.add)
            act(dtmp, dtmp, AF.Exp, force_self=True)
            act(dtt, dtmp, AF.Ln, bias=1.0, scale=1.0, force_self=True)
            tt('dve', aa, dtt, rowc[:, l, 1, :].unsqueeze(1).to_broadcast([128, 4, 32]), ALU.mult)
            if dbg_stop == 41:
                return
            pcs = P.next()
            mm(pcs[:, 0:128], UINCL, dts[:, 1, :], True, True)
            mm(pcs[:, 128:256], ONES, dts[:, 1, :], True, True)
            pcs_cs = pcs[:, 0:128].rearrange("p (j h) -> p j h", j=4)
            pcs_tot = pcs[:, 128:256].rearrange("p (j h) -> p j h", j=4)
            if dbg_stop == 420:
                return
            cp('act', acs, pcs_cs, force_self=True)
            if dbg_stop == 421:
                return
            act(eacs, pcs_cs, AF.Exp, force_self=True)
            act(cdec, pcs_tot, AF.Exp, force_self=True)
            if dbg_stop == 422:
                return
            cp('act', dtmp, pcs_tot, force_self=True)
            S.op('dve', lambda e: e.tensor_tensor(out=dtmp, in0=dtmp, in1=acs, op=ALU.subtract), reads=[dtmp, acs], writes=[dtmp], force_self=True)
            if dbg_stop == 423:
                return
            act(dtmp, dtmp, AF.Exp, force_self=True)
            if dbg_stop == 424:
                return
            tt('dve', w2, dtt, dtmp, ALU.mult)

            def xbc_chunk(wblk, ci, q, dest, dest_fp32_tmp=False):
                pX = P.next()
                for kc in range(8):
                    mm(pX[:], wblk[:, kc, ci * 128:(ci + 1) * 128], hTb[:, kc, :], kc == 0, kc == 7)
                acc = TB.next()
                conv_from_psum(pX[:], XP, xbch[:, li, q, :], SW + 4 * q, SB + q, 4, l, acc)
                act(dest, acc, AF.Silu)

            if dbg_stop == 42:
                return
            def b_transpose(g):
                ptb = P.next()
                ptb16 = ptb[:].bitcast(BF16)
                for j in range(4):
                    tr(ptb16[:, j * 128:(j + 1) * 128], BT[:, g, j * 128:(j + 1) * 128], identb[:])
                cp('act', Btok[:, :, g * 128:(g + 1) * 128], ptb16[:, 0:512].rearrange("p (j n) -> p j n", j=4))
            wB = wget()
            for g in range(4):
                xbc_chunk(wB, g, 16 + g, BT[:, g, :])
                if g > 0:
                    b_transpose(g - 1)
            wC = wget()
            for g in range(4):
                xbc_chunk(wC, g, 20 + g, CT[:, g, :])
                if g == 0:
                    b_transpose(3)

            if dbg_stop == 4:
                return
            def xs_transpose(xsT, ci):
                ptx = P.next()
                for j in range(4):
                    tr(ptx[:, j * 128:(j + 1) * 128], xsT[:, j * 128:(j + 1) * 128], IDENT)
                cp('act', xs_tok[:, :, ci * 128:(ci + 1) * 128], ptx[:].rearrange("p (j n) -> p j n", j=4))

            def xs_z(g):
                S.tag = 'p3_xs_z'
                wx = wget()
                wz = wget()
                pend = None
                for ci in range(4):
                    xsT = XS.next()
                    xbc_chunk(wx, ci, 4 * g + ci, xsT)
                    if pend is not None:
                        xs_transpose(*pend)
                    pend = (xsT, ci)
                for j in range(4):
                    pz = P.next()
                    for kc in range(8):
                        mm(pz[:], hTb[:, kc, j * 128:(j + 1) * 128], wz[:, kc, :], kc == 0, kc == 7)
                    act(zs[:, g % 2, j, :], pz[:], AF.Silu)
                    if j == 0:
                        xs_transpose(*pend)

            def S1a(g, j, k):
                S.tag = 'p3_s1'
                hs = slice(8 * g, 8 * g + 8)
                jb = slice(j * 128, (j + 1) * 128)
                xs3 = xs_tok[:, j, :].rearrange("p (h d) -> p h d", h=8)
                tt('dve', xsd[:, k, :].rearrange("p (h d) -> p h d", h=8), xs3,
                   dtt[:, j, hs].unsqueeze(2).to_broadcast([128, 8, 64]), ALU.mult)
                tt('dve', xsw[:, k, :].rearrange("p (h d) -> p h d", h=8), xs3,
                   w2[:, j, hs].unsqueeze(2).to_broadcast([128, 8, 64]), ALU.mult)
                tt('dve', t2b[:, k, :].rearrange("p (h d) -> p h d", h=8), xs3,
                   rowc[:, l, 2, hs].unsqueeze(2).to_broadcast([128, 8, 64]), ALU.mult)
                pcb = P.next()
                mm(pcb[:, 0:128], BT[:, g, jb], CT[:, g, jb], True, True)
                tt('dve', cbm[:, k, :], pcb[:, 0:128], MASKT, ALU.mult)
                tt('dve', Apr2[k].rearrange("p (h l) -> p h l", h=8),
                   UINCL.unsqueeze(1).to_broadcast([128, 8, 128]),
                   aa[:, j, hs].unsqueeze(2).to_broadcast([128, 8, 128]), ALU.mult)
                Es = []
                for half in range(2):
                    pseg = P.next()
                    mm(pseg[:], USTRICT, Apr2[k][:, half * 512:(half + 1) * 512], True, True)
                    E = LNB.next()
                    act(E, pseg[:], AF.Exp)
                    Es.append(E)
                return Es

            def S1b(g, j, k, Es):
                S.tag = 'p3_s1'
                for half in range(2):
                    tt('dve', MT2[k][:, 4 * half:4 * half + 4, :], Es[half].rearrange("p (h l) -> p h l", h=4),
                       cbm[:, k, :].unsqueeze(1).to_broadcast([128, 4, 128]), ALU.mult)

            def S2a(g, j, k):
                S.tag = 'p3_s2'
                hs = slice(8 * g, 8 * g + 8)
                jb = slice(j * 128, (j + 1) * 128)
                xd = xsd[:, k, :]
                xw = xsw[:, k, :]
                pyd = P.next()
                for hh in range(8):
                    mm(pyd[:, hh * 64:(hh + 1) * 64], MT2[k][:, hh, :], xd[:, hh * 64:(hh + 1) * 64], True, True)
                pyo = P.next()
                mm(pyo[:], CT[:, g, jb], STb[:, li, g, :], True, True)
                pst = P.next()
                mm(pst[:], Btok[:, j, g * 128:(g + 1) * 128], xw, True, True)
                t1 = TC.next()
                tt('dve', t1.rearrange("p (h d) -> p h d", h=8), pyo[:].rearrange("p (h d) -> p h d", h=8),
                   eacs[:, j, hs].unsqueeze(2).to_broadcast([128, 8, 64]), ALU.mult)
                Sg = ST[:, li, g, :]
                tt('dve', Sg.rearrange("p (h d) -> p h d", h=8), Sg.rearrange("p (h d) -> p h d", h=8),
                   cdec[:, j, hs].unsqueeze(2).to_broadcast([128, 8, 64]), ALU.mult)
                tt('dve', Sg, Sg, pst[:], ALU.add)
                cp('act', STb[:, li, g, :], Sg)
                tt('dve', t1, t1, pyd[:], ALU.add)
                tt('dve', t1, t1, t2b[:, k, :], ALU.add)
                tt('dve', t1, t1, zs[:, g % 2, j, :], ALU.mult)
                act(t2b[:, k, :], t1, AF.Square, accum=sml[:, 0:1])
                act(sml[:, 1:2], sml[:, 0:1], AF.Ln, bias=epsb[:, 1:2], scale=1.0 / 512.0, force_self=True)
                act(sml[:, 2:3], sml[:, 1:2], AF.Exp, scale=-0.5, force_self=True)
                return t1

            def S2b(g, j, k, t1):
                S.tag = 'p3_s2'
                jb = slice(j * 128, (j + 1) * 128)
                yn = YN.next()
                act(yn, t1, AF.Copy, scale=sml[:, 2:3], force_self=True)
                pty = P.next()
                pty16 = pty[:].bitcast(BF16)
                for ci in range(4):
                    tr(pty16[:, ci * 128:(ci + 1) * 128], yn[:, ci * 128:(ci + 1) * 128], identb[:])
                for ci in range(4):
                    act(ynT[:, 4 * g + ci, jb], pty16[:, ci * 128:(ci + 1) * 128], AF.Copy,
                        scale=PPl[:, NW + 4 * g + ci:NW + 4 * g + ci + 1])

            iters = [(g, j) for g in range(4) for j in range(4)]
            xs_z(0)
            Es = S1a(0, 0, 0)
            S1b(0, 0, 0, Es)
            for i, (g, j) in enumerate(iters):
                nxt = None
                if i + 1 < len(iters):
                    g2, j2 = iters[i + 1]
                    if j2 == 0:
                        xs_z(g2)
                    nxt = (g2, j2, (i + 1) % 2, S1a(g2, j2, (i + 1) % 2))
                t1 = S2a(g, j, i % 2)
                if nxt is not None:
                    S1b(*nxt)
                if i < 8:
                    conv_unit(i)
                S2b(g, j, i % 2, t1)
            p1_ln()

            if dbg_stop == 6:
                return
            S.tag = 'p4_out'
            for ob in range(2):
                wga = wget()
                wco = wget()
                for ci in range(4):
                    pga = P.next()
                    for kc in range(8):
                        mm(pga[:], wga[:, kc, ci * 128:(ci + 1) * 128], hTb[:, kc, :], kc == 0, kc == 7)
                    sg = TA.next()
                    act(sg, pga[:], AF.Sigmoid)
                    pya = P.next()
                    for kc in range(8):
                        mm(pya[:], wco[:, kc, ci * 128:(ci + 1) * 128], sT[:, kc, :], kc == 0, kc == 7)
                    tt('dve', mixa[:, ci, :], pya[:], sg, ALU.mult)
                wsa = wget()
                held = [P.hold_next() for _ in range(4)]
                for ci in range(4):
                    for kc in range(8):
                        mm(held[ci][1][:], wsa[:, kc, ci * 128:(ci + 1) * 128], ynT[:, kc, :], kc == 0, False)
                wsb = wget()
                for ci in range(4):
                    for kc in range(8):
                        mm(held[ci][1][:], wsb[:, kc, ci * 128:(ci + 1) * 128], ynT[:, 8 + kc, :], False, kc == 7)
                wgb = wget()
                for ci in range(4):
                    pgb = P.next()
                    for kc in range(8):
                        mm(pgb[:], wgb[:, kc, ci * 128:(ci + 1) * 128], hTb[:, kc, :], kc == 0, kc == 7)
                    sg = TA.next()
                    act(sg, pgb[:], AF.Sigmoid)
                    t = TC.next()
                    tt('dve', t, held[ci][1][:], sg, ALU.mult)
                    tt('dve', mixTb[:, 4 * ob + ci, :], t, mixa[:, ci, :], ALU.add)
                    P.release(held[ci][0])
            for ob in range(2):
                wwo = wget()
                for ci in range(4):
                    oc = 4 * ob + ci
                    po = P.next()
                    for kc in range(8):
                        mm(po[:], wwo[:, kc, ci * 128:(ci + 1) * 128], mixTb[:, kc, :], kc == 0, kc == 7)
                    stt('dve', hT[:, oc, :], hT[:, oc, :], ALPHA, po[:], ALU.mult, ALU.add)
            S.tag = 'ln1'
            ln_hT(l, L1G, L1B)

            if dbg_stop == 7:
                return
            S.tag = 'p5_ffn'
            for b in range(6):
                wgt = wget()
                wvl = wget()
                nci = 4 if b < 5 else 2
                for ci in range(nci):
                    i = 4 * b + ci
                    pg = P.next()
                    for kc in range(8):
                        mm(pg[:], wgt[:, kc, ci * 128:(ci + 1) * 128], hTb[:, kc, :], kc == 0, kc == 7)
                    pv = P.next()
                    for kc in range(8):
                        mm(pv[:], wvl[:, kc, ci * 128:(ci + 1) * 128], hTb[:, kc, :], kc == 0, kc == 7)
                    ag = TB.next()
                    conv_from_psum(pg[:], UP, ffnh[:, li, i, :], FW + 3 * i, FB + i, 3, l, ag)
                    av = TB.next()
                    conv_from_psum(pv[:], UP, ffnh[:, li, 22 + i, :], FW + 3 * (22 + i), FB + 22 + i, 3, l, av)
                    sg = TA.next()
                    act(sg, ag, AF.Silu)
                    tt('dve', actT[:, i, :], sg, av, ALU.mult)
            for ob in range(2):
                held = [P.hold_next() for _ in range(4)]
                for ks in range(3):
                    wd = wget()
                    nk = 8 if ks < 2 else 6
                    for ci in range(4):
                        for kk in range(nk):
                            mm(held[ci][1][:], wd[:, kk, ci * 128:(ci + 1) * 128], actT[:, 8 * ks + kk, :],
                               ks == 0 and kk == 0, ks == 2 and kk == nk - 1)
                for ci in range(4):
                    oc = 4 * ob + ci
                    stt('dve', hT[:, oc, :], hT[:, oc, :], ALPHA, held[ci][1][:], ALU.mult, ALU.add)
                    P.release(held[ci][0])
            S.tag = 'ln2'
            ln_hT(l, L2G, L2B)

        out_toks = []
        groups = [[0, 1], [2, 3], [4, 5], [6, 7]]
        nsteps = NT + 1 if pipe else NT
        for t in range(nsteps):
            S.tag = 'io_in'
            tx = min(t, NT - 1)
            S.dma('sp', lambda e, tx=tx: e.dma_start(out=xio, in_=x_d[tx * TT:(tx + 1) * TT, :].rearrange("(j p) c -> p j c", p=128)),
                  'xld', reads=[x_d], writes=[xio])
            if pipe:
                ts('dve', xio, xio, sel[:, 0:1], None, ALU.mult)
                if t >= 1:
                    rv = recv_d[(t - 1) % 2]
                    S.dma('sp', lambda e, rv=rv: e.dma_start(out=xio2, in_=rv[0:TT, :].rearrange("(j p) c -> p j c", p=128)),
                          'rld', reads=[rv], writes=[xio2])
                    stt('dve', xio, xio2, sel[:, 1:2], xio, ALU.mult, ALU.add)
            for c in range(8):
                ptx = P.next()
                for j in range(4):
                    tr(ptx[:, j * 128:(j + 1) * 128], xio[:, j, c * 128:(c + 1) * 128], IDENT)
                cp('act', hT[:, c, :], ptx[:])
            if apply_ln_in and dbg_stop not in (11, 12):
                ln_hT(layers[0], LIG, LIB, blend=pipe)
            else:
                for c in range(8):
                    cp('act', hTb[:, c, :], hT[:, c, :])
            for li, l in enumerate(layers):
                if dbg_stop in (11, 12, 13):
                    continue
                layer_tile(li, l)
            if pipe and t == 0:
                for buf in (ST, STb, vhalo, xbch, ffnh):
                    flat = buf[:].rearrange("p a b c -> p (a b c)")
                    ts('dve', flat, flat, sel[:, 2:3], None, ALU.mult)
            S.tag = 'io_out'
            for j in range(4):
                for m in range(2):
                    pto = P.next()
                    for cc in range(4):
                        c = 4 * m + cc
                        tr(pto[:, cc * 128:(cc + 1) * 128], hT[:, c, j * 128:(j + 1) * 128], IDENT)
                    cp('act', xio[:, j, m * 512:(m + 1) * 512], pto[:])
            if pipe:
                if t < NT:
                    sd = send_d[t % 2]
                    rv = recv_d[t % 2]
                    S.dma('sp', lambda e, sd=sd: e.dma_start(out=sd.rearrange("(j p) c -> p j c", p=128), in_=xio),
                          'sst', reads=[xio], writes=[sd])
                    S.dma('pool', lambda e, sd=sd, rv=rv: e.collective_compute("AllGather", ALU.bypass, replica_groups=groups, ins=[sd], outs=[rv]),
                          'cc%d' % (t % 2), reads=[sd], writes=[rv], inc=1)
                if t >= 1:
                    ty = t - 1
                    tok = S.dma('sp', lambda e, ty=ty: e.dma_start(out=y_d[ty * TT:(ty + 1) * TT, :].rearrange("(j p) c -> p j c", p=128), in_=xio),
                                'yst', reads=[xio], writes=[('y', ty)])
                    out_toks.append(tok)
            else:
                tok = S.dma('sp', lambda e, t=t: e.dma_start(out=y_d[t * TT:(t + 1) * TT, :].rearrange("(j p) c -> p j c", p=128), in_=xio),
                            'yst', reads=[xio], writes=[('y', t)])
                out_toks.append(tok)
        S.wait_all('sp', out_toks[-1:])
        S.emit()
    return nc, S


def _pack_params(inp):
    pp = np.zeros((128, 2, NPP), np.float32)

    def fm(v, nch):
        return np.ascontiguousarray(v.reshape(nch, 128).T)
    for l in range(2):
        w = inp['conv_dw_w'][l]
        pp[:, l, CW:CW + 248] = w.T.reshape(8, 128, 31).transpose(1, 0, 2).reshape(128, 248)
        pp[:, l, CB:CB + 8] = fm(inp['conv_dw_b'][l], 8)
        pp[:, l, CG:CG + 8] = fm(inp['conv_ln_g'][l], 8)
        pp[:, l, CBE:CBE + 8] = fm(inp['conv_ln_b'][l], 8)
        w = inp['ssm_conv_w'][l]
        pp[:, l, SW:SW + 96] = w.T.reshape(24, 128, 4).transpose(1, 0, 2).reshape(128, 96)
        pp[:, l, SB:SB + 24] = fm(inp['ssm_conv_b'][l], 24)
        pp[:, l, NW:NW + 16] = fm(inp['ssm_norm_w'][l], 16)
        w = inp['ffn_dw_w'][l]
        pp[:, l, FW:FW + 132] = w.T.reshape(44, 128, 3).transpose(1, 0, 2).reshape(128, 132)
        pp[:, l, FB:FB + 44] = fm(inp['ffn_dw_b'][l], 44)
        pp[:, l, L1G:L1G + 8] = fm(inp['ln1_g'][l], 8)
        pp[:, l, L1B:L1B + 8] = fm(inp['ln1_b'][l], 8)
        pp[:, l, L2G:L2G + 8] = fm(inp['ln2_g'][l], 8)
        pp[:, l, L2B:L2B + 8] = fm(inp['ln2_b'][l], 8)
        pp[:, l, LIG:LIG + 8] = fm(inp['ln_in_g'], 8)
        pp[:, l, LIB:LIB + 8] = fm(inp['ln_in_b'], 8)
    rowp = np.zeros((2, 3, 32), np.float32)
    for l in range(2):
        rowp[l, 0] = inp['ssm_dt_bias'][l]
        rowp[l, 1] = inp['ssm_a_log'][l]
        rowp[l, 2] = inp['ssm_d'][l]
    cst = np.zeros((128, 5, 128), np.float32)
    cst[:, 0, :] = np.eye(128)
    cst[:, 1, :] = np.triu(np.ones((128, 128)))
    cst[:, 2, :] = 1.0
    cst[:, 3, :] = np.tril(np.ones((128, 128)), -1)
    cst[:, 4, :] = np.triu(np.ones((128, 128)))
    return pp, rowp, cst


_CACHE = {}


def _get_prog(T, layers, apply_ln_in):
    key = (T, tuple(layers), apply_ln_in)
    if key not in _CACHE:
        _CACHE[key] = build_program(T, list(layers), apply_ln_in)
    return _CACHE[key][0]


def run_layers(xs, inp, layers, apply_ln_in, n_cores):
    T = xs[0].shape[0]
    nc = _get_prog(T, layers, apply_ln_in)
    pp, rowp, cst = _pack_params(inp)
    wts = {k: np.ascontiguousarray(inp[k], dtype=np.float32) for k in
           ('w_in', 'w_conv_out', 'w_ssm_out', 'w_o', 'w_ffn_up', 'w_ffn_down')}
    in_maps = []
    for c in range(n_cores):
        m = dict(wts)
        m['x'] = np.ascontiguousarray(xs[c % len(xs)], dtype=np.float32)
        m['pp'] = pp
        m['rowp'] = rowp
        m['cst'] = cst
        in_maps.append(m)
    res = run_bass_kernel_spmd(nc, in_maps, core_ids=list(range(n_cores)))
    return [np.asarray(res.results[c]['y']) for c in range(len(xs))]


def run_pipe(xs, inp):
    T = xs[0].shape[0]
    key = (T, 'pipe')
    if key not in _CACHE:
        _CACHE[key] = build_program(T, [0], True, pipe=True)
    nc = _CACHE[key][0]
    pp, rowp, cst = _pack_params(inp)
    zeros = np.zeros((T, D), np.float32)
    in_maps = []
    for c in range(8):
        b, lc = c // 2, c % 2
        m = {k: np.ascontiguousarray(inp[k][lc:lc + 1], dtype=np.float32) for k in
             ('w_in', 'w_conv_out', 'w_ssm_out', 'w_o', 'w_ffn_up', 'w_ffn_down')}
        ppc = np.ascontiguousarray(pp[:, lc:lc + 1, :])
        sel = np.zeros((128, 4), np.float32)
        if lc == 0:
            m['x'] = np.ascontiguousarray(xs[b % len(xs)], dtype=np.float32)
            sel[:, 0] = 1.0
            sel[:, 2] = 1.0
        else:
            m['x'] = zeros
            sel[:, 1] = 1.0
            ppc[:, 0, LIG:LIG + 8] = 1.0
            ppc[:, 0, LIB:LIB + 8] = 0.0
        m['pp'] = ppc
        m['rowp'] = np.ascontiguousarray(rowp[lc:lc + 1])
        m['cst'] = cst
        m['sel'] = sel
        in_maps.append(m)
    res = run_bass_kernel_spmd(nc, in_maps, core_ids=list(range(8)))
    return [np.asarray(res.results[2 * b + 1]['y']) for b in range(len(xs))]


FUSED = True
PIPE = True


def kernel(**inputs):
    inp = {k: np.asarray(v) for k, v in inputs.items()}
    x = inp['x'].astype(np.float32)
    B = x.shape[0]
    xs = [x[b] for b in range(B)]
    if PIPE:
        ys = run_pipe(xs, inp)
    elif FUSED:
        ys = run_layers(xs, inp, (0, 1), True, 4)
    else:
        h = run_layers(xs, inp, (0,), True, 8)
        ys = run_layers(h, inp, (1,), False, 8)
    return np.stack(ys, 0).astype(np.float32)
```

```python
import contextlib
import numpy as np
import concourse.bass as bass
import concourse.mybir as mybir
from concourse.bass_utils import run_bass_kernel_spmd

F32 = mybir.dt.float32
BF16 = mybir.dt.bfloat16
ALU = mybir.AluOpType
AF = mybir.ActivationFunctionType

CELL = 512
D = 1024
IN_DIM = 9248
TT = 512
DEPTH = 2
ALPHA = float((2 * DEPTH) ** 0.25)
LN_EPS = 1e-5
RMS_EPS = 1e-5


def _dtsize(dt):
    s = str(dt)
    if '64' in s:
        return 8
    if '32' in s:
        return 4
    if '16' in s:
        return 2
    return 1


class Sched:
    ENGS = ('pe', 'act', 'dve', 'pool', 'sp')

    def __init__(self, nc, same_engine_sync=True):
        self.nc = nc
        self.same_engine_sync = same_engine_sync
        self.streams = {e: [] for e in self.ENGS}
        self.count = {e: 0 for e in self.ENGS}
        self.dma_count = {}
        self.dma_inc = {}
        self.cells = {}
        self.waited = {e: {} for e in self.ENGS}
        self.n_ops = 0
        self.tag = ''
        self.tags = {e: [] for e in self.ENGS}

    def _keys(self, r):
        if isinstance(r, tuple):
            return [r]
        ap = r
        name = ap.tensor.name
        sp_ = str(ap.space).upper()
        if 'DRAM' in sp_ or 'PSUM' in sp_:
            return [(name,)]
        pairs = ap.ap
        pstep = pairs[0][0]
        sz = _dtsize(ap.dtype)
        off = ap.offset % pstep if pstep > 0 else ap.offset
        hull = 0
        for (st, cn) in pairs[1:]:
            hull += abs(st) * (cn - 1)
        lo = off * sz
        hi = (off + hull + 1) * sz
        return [(name, c) for c in range(lo // CELL, (hi - 1) // CELL + 1)]

    def _deps(self, reads, writes):
        deps = {}
        rk = []
        for r in reads:
            rk += self._keys(r)
        wk = []
        for w in writes:
            wk += self._keys(w)
        cells = self.cells
        for k in rk:
            c = cells.get(k)
            if c is not None and c[0] is not None:
                kk, vv = c[0]
                if deps.get(kk, 0) < vv:
                    deps[kk] = vv
        for k in wk:
            c = cells.get(k)
            if c is not None:
                if c[0] is not None:
                    kk, vv = c[0]
                    if deps.get(kk, 0) < vv:
                        deps[kk] = vv
                for kk, vv in c[1].items():
                    if deps.get(kk, 0) < vv:
                        deps[kk] = vv
        return deps, rk, wk

    def _commit(self, rk, wk, tok):
        cells = self.cells
        for k in rk:
            c = cells.get(k)
            if c is None:
                c = [None, {}]
                cells[k] = c
            if c[1].get(tok[0], 0) < tok[1]:
                c[1][tok[0]] = tok[1]
        for k in wk:
            cells[k] = [tok, {}]

    def _filter(self, eng, deps, force_self=False):
        waits = []
        wd = self.waited[eng]
        for k, v in deps.items():
            if k == eng and (eng == 'pe' or not (self.same_engine_sync or force_self)):
                continue
            if wd.get(k, 0) >= v:
                continue
            wd[k] = v
            waits.append((k, v))
        return waits

    def op(self, eng, fn, reads=(), writes=(), force_self=False):
        deps, rk, wk = self._deps(reads, writes)
        waits = self._filter(eng, deps, force_self)
        self.count[eng] += 1
        tok = (eng, self.count[eng])
        self.tags[eng].append(self.tag)
        self.streams[eng].append((waits, fn, tok))
        self._commit(rk, wk, tok)
        self.n_ops += 1
        return tok

    def dma(self, queue, fn, sem, reads=(), writes=(), inc=16):
        deps, rk, wk = self._deps(reads, writes)
        waits = self._filter(queue, deps)
        self.dma_count[sem] = self.dma_count.get(sem, 0) + 1
        self.dma_inc[sem] = inc
        tok = (sem, inc * self.dma_count[sem])
        self.streams[queue].append((waits, fn, tok))
        self._commit(rk, wk, tok)
        self.n_ops += 1
        return tok

    def wait_all(self, eng, toks):
        deps = {}
        for k, v in toks:
            deps[k] = max(deps.get(k, 0), v)
        waits = self._filter(eng, deps)
        if waits:
            self.streams[eng].append((waits, None, None))

    def emit(self):
        nc = self.nc
        semnames = list(self.ENGS) + list(self.dma_count.keys())
        with contextlib.ExitStack() as st:
            sems = {}
            for n in semnames:
                sems[n] = st.enter_context(nc.semaphore("s_" + n))
            block = st.enter_context(nc.Block())
            engmap = {'pe': block.tensor, 'act': block.scalar, 'dve': block.vector,
                      'pool': block.gpsimd, 'sp': block.sync}

            def make(ename):
                stream = self.streams[ename]

                def body(e):
                    for waits, fn, tok in stream:
                        for (k, v) in waits:
                            e.wait_ge(sems[k], v)
                        if fn is None:
                            continue
                        ins = fn(e)
                        if tok[0] == ename:
                            ins.then_inc(sems[tok[0]], 1)
                        else:
                            ins.then_inc(sems[tok[0]], self.dma_inc[tok[0]])
                return body
            for ename in self.ENGS:
                if self.streams[ename]:
                    engmap[ename](make(ename))


CW, CB, CG, CBE = 0, 248, 256, 264
SW, SB, NW = 272, 368, 392
FW, FB = 408, 540
L1G, L1B, L2G, L2B = 584, 592, 600, 608
LIG, LIB = 616, 624
NPP = 632

def _slot_table():
    s = []
    A = lambda name, k0, nk, c0, ncol: s.append((name, k0, nk, c0, ncol))
    A('w_in', 0, 8, 0, 512)
    A('w_in', 0, 8, 1024, 512)
    A('w_in', 0, 8, 512, 512)
    A('w_in', 0, 8, 1536, 512)
    A('w_in', 0, 8, 7168, 32)
    A('w_in', 0, 8, 6144, 512)
    A('w_in', 0, 8, 6656, 512)
    def XZ(g):
        A('w_in', 0, 8, 4096 + 512 * g, 512)
        A('w_in', 0, 8, 2048 + 512 * g, 512)
    XZ(0)
    for c in (0, 1, 2):
        A('diag', c, 8, 0, 496)
    XZ(1)
    for c in (3, 4, 5, 6):
        A('diag', c, 8, 0, 496)
    XZ(2)
    A('diag', 7, 8, 0, 496)
    XZ(3)
    for ob in range(2):
        A('w_in', 0, 8, 7200 + 512 * ob, 512)
        A('w_conv_out', 0, 8, 512 * ob, 512)
        A('w_ssm_out', 0, 8, 512 * ob, 512)
        A('w_ssm_out', 8, 8, 512 * ob, 512)
        A('w_in', 0, 8, 8224 + 512 * ob, 512)
    for ob in range(2):
        A('w_o', 0, 8, 512 * ob, 512)
    for b in range(6):
        ncol = 512 if b < 5 else 256
        A('w_ffn_up', 0, 8, 512 * b, ncol)
        A('w_ffn_up', 0, 8, 2816 + 512 * b, ncol)
    for ob in range(2):
        A('w_ffn_down', 0, 8, 512 * ob, 512)
        A('w_ffn_down', 8, 8, 512 * ob, 512)
        A('w_ffn_down', 16, 6, 512 * ob, 512)
    return s


SLOTS = _slot_table()
NSL = len(SLOTS)
NRING = 4
NPREP = 4


def build_program(T, layers, apply_ln_in, conv_pool_chunks=(), same_engine_sync=False, dbg_stop=0, dbg_var=0, use_pool=False, pipe=False):
    PL = 'pool' if use_pool else 'dve'
    NT = T // TT
    NLW = 1 if (pipe or len(layers) == 1) else 2
    NL = len(layers)
    nc = bass.Bass("TRN2", target_bir_lowering=False, num_devices=8) if pipe else bass.Bass("TRN2", target_bir_lowering=False)
    S = Sched(nc, same_engine_sync=same_engine_sync)

    def din(name, shape):
        return nc.dram_tensor(name, shape, F32, kind="ExternalInput").ap()
    x_d = din("x", [T, D])
    wdr = {
        'w_in': din("w_in", [NLW, D, IN_DIM]),
        'w_conv_out': din("w_conv_out", [NLW, D, D]),
        'w_ssm_out': din("w_ssm_out", [NLW, 2048, D]),
        'w_o': din("w_o", [NLW, D, D]),
        'w_ffn_up': din("w_ffn_up", [NLW, D, 5632]),
        'w_ffn_down': din("w_ffn_down", [NLW, 2816, D]),
    }
    pp_d = din("pp", [128, NLW, NPP])
    rowp_d = din("rowp", [NLW, 3, 32])
    cst_d = din("cst", [128, 5, 128])
    y_d = nc.dram_tensor("y", [T, D], F32, kind="ExternalOutput").ap()
    if pipe:
        sel_d = din("sel", [128, 4])
        send_d = [nc.dram_tensor("send%d" % i, [TT, D], F32, kind="Internal").ap() for i in range(2)]
        recv_d = [nc.dram_tensor("recv%d" % i, [2 * TT, D], F32, kind="Internal").ap() for i in range(2)]
    wscr = nc.dram_tensor("wscr", [NLW, NSL, 128, 4096], BF16, kind="Internal").ap()
    dscr = nc.dram_tensor("dscr", [NLW, 8, 128, 3968], BF16, kind="Internal").ap()

    with contextlib.ExitStack() as st:
        def sb(name, shape, dt=F32):
            return st.enter_context(nc.sbuf_tensor("sb_" + name, shape, dt))

        def psb(name):
            return st.enter_context(nc.psum_tensor(name, [128, 512], F32))
        cst = sb("cst", [128, 5, 128])
        IDENT, UINCL, ONES, USTRICT, MASKT = (cst[:, i, :] for i in range(5))
        identb = sb("identb", [128, 128], BF16)
        ustrb = sb("ustrb", [128, 128], BF16)
        uinclb = sb("uinclb", [128, 128], BF16)
        ahl = sb("ahl", [128, 2, 128], BF16)
        pp = sb("pp", [128, NLW, NPP])
        rowc = sb("rowc", [128, NLW, 3, 32])
        epsb = sb("epsb", [128, 2])
        sel = sb("sel", [128, 4])
        ST = sb("ST", [128, NL, 4, 512])
        STb = sb("STb", [128, NL, 4, 512], BF16)
        vhalo = sb("vhalo", [128, NL, 8, 30], BF16)
        xbch = sb("xbch", [128, NL, 24, 3])
        ffnh = sb("ffnh", [128, NL, 44, 2])
        hT = sb("hT", [128, 8, 512])
        hTb = sb("hTb", [128, 8, 512], BF16)
        wring = sb("wring", [128, NRING, 4096], BF16)
        arena = sb("arena", [128, 9472])
        ynT = sb("ynT", [128, 16, 512], BF16)
        stat = sb("stat", [128, 2, 512])
        sT = sb("sT", [128, 8, 512], BF16)
        BT = sb("BT", [128, 4, 512], BF16)
        CT = sb("CT", [128, 4, 512], BF16)
        Btok = sb("Btok", [128, 4, 512], BF16)
        dts = sb("dts", [128, 7, 128])
        xs_tok = sb("xs_tok", [128, 4, 512])
        zs = sb("zs", [128, 2, 4, 512], BF16)
        cbm = sb("cbm", [128, 4, 128], BF16)
        tA = sb("tA", [128, 2, 512])
        xsT2 = sb("xsT2", [128, 2, 512])
        t2b = sb("t2b", [128, 2, 512])
        tB = sb("tB", [128, 2, 512])
        lnb = sb("lnb", [128, 4, 512], BF16)
        onesNb = sb("onesNb", [128, 128], BF16)
        tC = sb("tC", [128, 2, 512])
        xpre = sb("xpre", [128, 2, 516])
        ynb = sb("ynb", [128, 2, 512], BF16)
        xsd = sb("xsd", [128, 2, 512], BF16)
        xsw = sb("xsw", [128, 2, 512], BF16)
        sml = sb("sml", [128, 8])
        banks = [psb("ps%d" % i) for i in range(8)]

        vT = arena[:, 0:4 * 542].bitcast(BF16).rearrange("p (c t) -> p c t", c=8)
        convT = arena[:, 2176:2176 + 4096].rearrange("p (c t) -> p c t", c=8)
        mixTb = arena[:, 4096:6144].bitcast(BF16).rearrange("p (c t) -> p c t", c=8)
        mixa = arena[:, 6144:8192].rearrange("p (c t) -> p c t", c=4)
        Apr2 = [arena[:, 6272 + 1024 * i:6272 + 1024 * (i + 1)] for i in range(2)]
        MT2 = [arena[:, 8320 + 512 * i:8320 + 512 * (i + 1)].bitcast(BF16).rearrange("p (h l) -> p h l", h=8)
               for i in range(2)]
        actT = arena[:, 0:5632].bitcast(BF16).rearrange("p (c t) -> p c t", c=22)
        upre = arena[:, 5632:5632 + 4 * 516].rearrange("p (c t) -> p c t", c=4)
        xio = arena[:, 4096:8192].rearrange("p (j c) -> p j c", j=4)
        xio2 = arena[:, 0:4096].rearrange("p (j c) -> p j c", j=4)

        class Cyc:
            def __init__(self, items):
                self.items = items
                self.i = 0
                self.held = set()

            def next(self):
                for _ in range(len(self.items) + 1):
                    k = self.i % len(self.items)
                    self.i += 1
                    if k not in self.held:
                        return self.items[k]
                raise RuntimeError("all held")

            def hold_next(self):
                for _ in range(len(self.items) + 1):
                    k = self.i % len(self.items)
                    self.i += 1
                    if k not in self.held:
                        self.held.add(k)
                        return k, self.items[k]
                raise RuntimeError("all held")

            def release(self, k):
                self.held.discard(k)
        P = Cyc(banks[:6])
        PS_MEAN, PS_EX2 = banks[6], banks[7]
        TA = Cyc([tA[:, i, :] for i in range(2)])
        XS = Cyc([xsT2[:, i, :] for i in range(2)])
        TB = Cyc([tB[:, i, :] for i in range(2)])
        LNB = Cyc([lnb[:, i, :] for i in range(4)])
        TC = Cyc([tC[:, i, :] for i in range(2)])
        XP = Cyc([xpre[:, i, :] for i in range(2)])
        UP = Cyc([upre[:, i, :] for i in range(4)])
        YN = Cyc([ynb[:, i, :] for i in range(2)])
        XSD = Cyc([xsd[:, i, :] for i in range(2)])
        XSW = Cyc([xsw[:, i, :] for i in range(2)])

        def mm(out, lhsT, rhs, start, stop):
            S.op('pe', lambda e: e.matmul(out, lhsT=lhsT, rhs=rhs, start=start, stop=stop),
                 reads=[lhsT, rhs], writes=[out])

        def tr(out, in_, ident):
            S.op('pe', lambda e: e.transpose(out, in_, ident), reads=[in_, ident], writes=[out])

        def act(out, in_, func, bias=None, scale=None, accum=None, eng='act', force_self=False):
            kw = {}
            rd = [in_]
            wr = [out]
            if bias is not None:
                kw['bias'] = bias
                if not isinstance(bias, float):
                    rd.append(bias)
            if scale is not None:
                kw['scale'] = scale
                if not isinstance(scale, float):
                    rd.append(scale)
            if accum is not None:
                kw['accum_out'] = accum
                wr.append(accum)
            S.op('act', lambda e: e.activation(out=out, in_=in_, func=func, **kw), reads=rd, writes=wr, force_self=force_self)

        def tt(eng, out, in0, in1, op):
            S.op(eng, lambda e: e.tensor_tensor(out=out, in0=in0, in1=in1, op=op), reads=[in0, in1], writes=[out])

        def ts(eng, out, in0, s1, s2, op0, op1=None):
            rd = [in0] + [s for s in (s1, s2) if s is not None and not isinstance(s, float)]
            if op1 is None:
                S.op(eng, lambda e: e.tensor_scalar(out=out, in0=in0, scalar1=s1, scalar2=None, op0=op0), reads=rd, writes=[out])
            else:
                S.op(eng, lambda e: e.tensor_scalar(out=out, in0=in0, scalar1=s1, scalar2=s2, op0=op0, op1=op1), reads=rd, writes=[out])

        def stt(eng, out, in0, scalar, in1, op0, op1):
            rd = [in0, in1] + ([] if isinstance(scalar, float) else [scalar])
            S.op(eng, lambda e: e.scalar_tensor_tensor(out=out, in0=in0, scalar=scalar, in1=in1, op0=op0, op1=op1),
                 reads=rd, writes=[out])

        def cp(eng, out, in_, force_self=False):
            if eng == 'act':
                S.op('act', lambda e: e.copy(out=out, in_=in_), reads=[in_], writes=[out], force_self=force_self)
            else:
                S.op(eng, lambda e: e.tensor_copy(out=out, in_=in_), reads=[in_], writes=[out], force_self=force_self)

        def memset(eng, ap, v):
            S.op(eng, lambda e: e.memset(ap, v), writes=[ap])

        n_prep = 0
        for l in (layers if dbg_stop not in (11, 13) else []):
            for s, (wn, k0, nk, c0, ncol) in enumerate(SLOTS):
                if wn == 'diag':
                    continue
                src = wdr[wn][l, k0 * 128:(k0 + nk) * 128, c0:c0 + ncol].rearrange("(k p) c -> p k c", p=128)
                dst = wscr[l, s].rearrange("p (k c) -> p k c", k=8)[:, 0:nk, 0:ncol]
                psem = 'prep%d' % (n_prep % NPREP)
                if n_prep >= NPREP:
                    S.wait_all('pool', [(psem, 16 * (n_prep // NPREP))])
                S.dma('pool', lambda e, src=src, dst=dst: e.dma_start(out=dst, in_=src), psem,
                      reads=[], writes=[('wscr', l, s)])
                n_prep += 1

        if n_prep:
            S.wait_all('pool', [('prep%d' % i, 16 * ((n_prep - 1 - i) // NPREP + 1)) for i in range(min(NPREP, n_prep))])
        uses = [(l, s) for t in range(NT + (1 if pipe else 0)) for l in layers for s in range(NSL)]
        wstate = {'issued': 0, 'n': 0}

        def wget():
            n = wstate['n']
            while wstate['issued'] < min(len(uses), max(n + NRING - 1, 2)):
                m = wstate['issued']
                l, s = uses[m]
                wn_, k0_, nk_, _, ncol_ = SLOTS[s]
                if wn_ == 'diag':
                    dst = wring[:, m % NRING, 0:3968]
                    src = dscr[l, k0_]
                    rkey = ('dscr', l, k0_)
                else:
                    dst = wring[:, m % NRING, :].rearrange("p (k c) -> p k c", k=8)[:, 0:nk_, 0:ncol_]
                    src = wscr[l, s].rearrange("p (k c) -> p k c", k=8)[:, 0:nk_, 0:ncol_]
                    rkey = ('wscr', l, s)
                S.dma('sp', lambda e, src=src, dst=dst: e.dma_start(out=dst, in_=src), 'wld%d' % (m % NRING),
                      reads=[rkey], writes=[wring[:, m % NRING, :]])
                wstate['issued'] += 1
            wstate['n'] += 1
            if SLOTS[uses[n][1]][0] == 'diag':
                return wring[:, n % NRING, 0:3968].rearrange("p (k m) -> p k m", k=31)
            return wring[:, n % NRING, :].rearrange("p (k c) -> p k c", k=8)

        S.dma('sp', lambda e: e.dma_start(out=cst[:], in_=cst_d), 'ld0', reads=[cst_d], writes=[cst[:]])
        S.dma('sp', lambda e: e.dma_start(out=pp[:], in_=pp_d), 'ld1', reads=[pp_d], writes=[pp[:]])
        S.dma('sp', lambda e: e.dma_start(out=rowc[:].rearrange("p a b c -> p (a b c)"),
                                          in_=rowp_d.rearrange("a b c -> (a b c)").partition_broadcast(128)),
              'ld2', reads=[rowp_d], writes=[rowc[:]])
        if pipe:
            S.dma('sp', lambda e: e.dma_start(out=sel[:], in_=sel_d), 'ld3', reads=[sel_d], writes=[sel[:]])
        cp('dve', identb[:], IDENT)
        cp('dve', ustrb[:], USTRICT)
        cp('dve', uinclb[:], UINCL)
        ts('dve', onesNb[:], ONES, 1.0 / 1024.0, None, ALU.mult)
        memset('dve', epsb[:, 0:1], LN_EPS)
        memset('dve', epsb[:, 1:2], RMS_EPS)
        memset('dve', ST[:], 0.0)
        memset('dve', STb[:], 0.0)
        memset('dve', vhalo[:], 0.0)
        memset('dve', xbch[:], 0.0)
        memset('dve', ffnh[:], 0.0)
        for l in layers:
            act(rowc[:, l, 1, :], rowc[:, l, 1, :], AF.Exp)
            ts('dve', rowc[:, l, 1, :], rowc[:, l, 1, :], -1.0, None, ALU.mult)

        nbuilt = 0
        for l in layers:
            for c in range(8):
                stg = wring[:, nbuilt % NRING, 0:3968]
                tt('dve', stg.rearrange("p (k m) -> p k m", k=31),
                   IDENT.unsqueeze(1).to_broadcast([128, 31, 128]),
                   pp[:, l, CW + c * 31:CW + c * 31 + 31].unsqueeze(2).to_broadcast([128, 31, 128]), ALU.mult)
                S.dma('sp', lambda e, stg=stg, l=l, c=c: e.dma_start(out=dscr[l, c], in_=stg), 'dst%d' % (nbuilt % NRING),
                      reads=[stg], writes=[('dscr', l, c)])
                nbuilt += 1

        def ln_stats_begin():
            pass

        def ln_finish_stats(blend=False):
            mean_b = stat[:, 0, :]
            rstd_b = stat[:, 1, :]
            cp('act', mean_b, PS_MEAN[:])
            t = TC.next()
            tt('dve', t, mean_b, mean_b, ALU.mult)
            tt('dve', t, PS_EX2[:], t, ALU.subtract)
            act(t, t, AF.Sqrt, bias=epsb[:, 0:1], scale=1.0)
            S.op('dve', lambda e: e.reciprocal(out=rstd_b, in_=t), reads=[t], writes=[rstd_b])
            if blend:
                ts('dve', mean_b, mean_b, sel[:, 0:1], None, ALU.mult)
                ts('dve', rstd_b, rstd_b, sel[:, 0:1], sel[:, 1:2], ALU.mult, ALU.add)
            return mean_b, rstd_b

        def ln_hT(l, gcol, bcol, blend=False):
            for c in range(8):
                sq = LNB.next()
                act(sq, hT[:, c, :], AF.Square)
                rb = LNB.next()
                cp('act', rb, hT[:, c, :])
                mm(PS_MEAN[:], onesNb[:], rb, c == 0, c == 7)
                mm(PS_EX2[:], onesNb[:], sq, c == 0, c == 7)
            mean_b, rstd_b = ln_finish_stats(blend)
            for c in range(8):
                t = TC.next()
                tt('dve', t, hT[:, c, :], mean_b, ALU.subtract)
                tt('dve', t, t, rstd_b, ALU.mult)
                act(hT[:, c, :], t, AF.Identity, bias=pp[:, l, bcol + c:bcol + c + 1], scale=pp[:, l, gcol + c:gcol + c + 1])
                cp('act', hTb[:, c, :], hT[:, c, :])

        def conv_from_psum(ps, pool_cyc, halo, wcol, bcol, K, l, out_acc, act_tap=False):
            xp = pool_cyc.next()
            H = K - 1
            cp('act', xp[:, H:H + 512], ps)
            cp('act', xp[:, 0:H], halo)
            cp('act', halo, xp[:, 512:512 + H])
            if act_tap:
                act(out_acc, ps, AF.Identity, bias=pp[:, l, bcol:bcol + 1], scale=pp[:, l, wcol + H:wcol + H + 1])
            else:
                ts('dve', out_acc, xp[:, H:H + 512], pp[:, l, wcol + H:wcol + H + 1], pp[:, l, bcol:bcol + 1], ALU.mult, ALU.add)
            for k in range(H):
                stt('dve', out_acc, xp[:, k:k + 512], pp[:, l, wcol + k:wcol + k + 1], out_acc, ALU.mult, ALU.add)

        def layer_tile(li, l):
            PPl = pp[:, l, :]
            if dbg_stop == 10:
                return
            S.tag = 'p1_conformer'
            cp('act', vT[:, :, 0:30], vhalo[:, li, :, :])
            for cb in range(2):
                wa = wget()
                wg = wget()
                for ci in range(4):
                    c = 4 * cb + ci
                    pa = P.next()
                    for kc in range(8):
                        mm(pa[:], wa[:, kc, ci * 128:(ci + 1) * 128], hTb[:, kc, :], kc == 0, kc == 7)
                    pg = P.next()
                    for kc in range(8):
                        mm(pg[:], wg[:, kc, ci * 128:(ci + 1) * 128], hTb[:, kc, :], kc == 0, kc == 7)
                    sig = TA.next()
                    act(sig, pg[:], AF.Sigmoid)
                    tt('dve', vT[:, c, 30:542], pa[:], sig, ALU.mult)
                    cp('act', vhalo[:, li, c, :], vT[:, c, 512:542])

            def conv_unit(c):
                S.tag = 'p1_conv'
                wd_ = wget()
                pc = P.next()
                for k in range(31):
                    mm(pc[:], wd_[:, k, :], vT[:, c, k:k + 512], k == 0, k == 30)
                act(convT[:, c, :], pc[:], AF.Identity, bias=PPl[:, CB + c:CB + c + 1], scale=1.0)
                rb = LNB.next()
                act(rb, pc[:], AF.Identity, bias=PPl[:, CB + c:CB + c + 1], scale=1.0)
                sq = LNB.next()
                act(sq, convT[:, c, :], AF.Square)
                mm(PS_MEAN[:], onesNb[:], rb, c == 0, c == 7)
                mm(PS_EX2[:], onesNb[:], sq, c == 0, c == 7)

            def p1_ln():
                S.tag = 'p1_ln'
                mean_b, rstd_b = ln_finish_stats()
                for c in range(8):
                    t = TC.next()
                    tt('dve', t, convT[:, c, :], mean_b, ALU.subtract)
                    tt('dve', t, t, rstd_b, ALU.mult)
                    act(sT[:, c, :], t, AF.Silu, bias=PPl[:, CBE + c:CBE + c + 1], scale=PPl[:, CG + c:CG + c + 1])

            if dbg_stop == 1:
                return
            S.tag = 'p2a_dtBC'
            dtt = dts[:, 0, :].rearrange("p (j h) -> p j h", j=4)
            aa = dts[:, 1, :].rearrange("p (j h) -> p j h", j=4)
            acs = dts[:, 2, :].rearrange("p (j h) -> p j h", j=4)
            eacs = dts[:, 3, :].rearrange("p (j h) -> p j h", j=4)
            cdec = dts[:, 4, :].rearrange("p (j h) -> p j h", j=4)
            w2 = dts[:, 5, :].rearrange("p (j h) -> p j h", j=4)
            dtmp = dts[:, 6, :].rearrange("p (j h) -> p j h", j=4)
            wdt = wget()
            pdt = P.next()
            for j in range(4):
                for kc in range(8):
                    mm(pdt[:, j * 32:(j + 1) * 32], hTb[:, kc, j * 128:(j + 1) * 128], wdt[:, kc, 0:32], kc == 0, kc == 7)
            tt('dve', dtmp, pdt[:, 0:128].rearrange("p (j h) -> p j h", j=4),
               rowc[:, l, 0, :].unsqueeze(1).to_broadcast([128, 4, 32]), ALU.add)
            act(dtmp, dtmp, AF.Exp, force_self=True)
            act(dtt, dtmp, AF.Ln, bias=1.0, scale=1.0, force_self=True)
            tt('dve', aa, dtt, rowc[:, l, 1, :].unsqueeze(1).to_broadcast([128, 4, 32]), ALU.mult)
            if dbg_stop == 41:
                return
            pcs = P.next()
            mm(pcs[:, 0:128], UINCL, dts[:, 1, :], True, True)
            mm(pcs[:, 128:256], ONES, dts[:, 1, :], True, True)
            pcs_cs = pcs[:, 0:128].rearrange("p (j h) -> p j h", j=4)
            pcs_tot = pcs[:, 128:256].rearrange("p (j h) -> p j h", j=4)
            if dbg_stop == 420:
                return
            cp('act', acs, pcs_cs, force_self=True)
            if dbg_stop == 421:
                return
            act(eacs, pcs_cs, AF.Exp, force_self=True)
            act(cdec, pcs_tot, AF.Exp, force_self=True)
            if dbg_stop == 422:
                return
            cp('act', dtmp, pcs_tot, force_self=True)
            S.op('dve', lambda e: e.tensor_tensor(out=dtmp, in0=dtmp, in1=acs, op=ALU.subtract), reads=[dtmp, acs], writes=[dtmp], force_self=True)
            if dbg_stop == 423:
                return
            act(dtmp, dtmp, AF.Exp, force_self=True)
            if dbg_stop == 424:
                return
            tt('dve', w2, dtt, dtmp, ALU.mult)
            a_hi = ahl[:, 0, :].rearrange("p (j h) -> p j h", j=4)
            a_lo = ahl[:, 1, :].rearrange("p (j h) -> p j h", j=4)
            S.op('dve', lambda e: e.tensor_copy(out=a_hi, in_=aa), reads=[aa], writes=[a_hi], force_self=True)
            S.op('dve', lambda e: e.tensor_tensor(out=dtmp, in0=aa, in1=a_hi, op=ALU.subtract), reads=[aa, a_hi], writes=[dtmp], force_self=True)
            S.op('dve', lambda e: e.tensor_copy(out=a_lo, in_=dtmp), reads=[dtmp], writes=[a_lo], force_self=True)

            def xbc_chunk(wblk, ci, q, dest, dest_fp32_tmp=False):
                pX = P.next()
                for kc in range(8):
                    mm(pX[:], wblk[:, kc, ci * 128:(ci + 1) * 128], hTb[:, kc, :], kc == 0, kc == 7)
                acc = TB.next()
                conv_from_psum(pX[:], XP, xbch[:, li, q, :], SW + 4 * q, SB + q, 4, l, acc)
                act(dest, acc, AF.Silu)

            if dbg_stop == 42:
                return
            def b_transpose(g):
                ptb = P.next()
                ptb16 = ptb[:].bitcast(BF16)
                for j in range(4):
                    tr(ptb16[:, j * 128:(j + 1) * 128], BT[:, g, j * 128:(j + 1) * 128], identb[:])
                cp('act', Btok[:, :, g * 128:(g + 1) * 128], ptb16[:, 0:512].rearrange("p (j n) -> p j n", j=4))
            wB = wget()
            for g in range(4):
                xbc_chunk(wB, g, 16 + g, BT[:, g, :])
                if g > 0:
                    b_transpose(g - 1)
            wC = wget()
            for g in range(4):
                xbc_chunk(wC, g, 20 + g, CT[:, g, :])
                if g == 0:
                    b_transpose(3)

            if dbg_stop == 4:
                return
            def xs_transpose(xsT, ci):
                ptx = P.next()
                for j in range(4):
                    tr(ptx[:, j * 128:(j + 1) * 128], xsT[:, j * 128:(j + 1) * 128], IDENT)
                cp('act', xs_tok[:, :, ci * 128:(ci + 1) * 128], ptx[:].rearrange("p (j n) -> p j n", j=4))

            def xs_z(g):
                S.tag = 'p3_xs_z'
                wx = wget()
                wz = wget()
                pend = None
                for ci in range(4):
                    xsT = XS.next()
                    xbc_chunk(wx, ci, 4 * g + ci, xsT)
                    if pend is not None:
                        xs_transpose(*pend)
                    pend = (xsT, ci)
                for j in range(4):
                    pz = P.next()
                    for kc in range(8):
                        mm(pz[:], hTb[:, kc, j * 128:(j + 1) * 128], wz[:, kc, :], kc == 0, kc == 7)
                    act(zs[:, g % 2, j, :], pz[:], AF.Silu)
                    if j == 0:
                        xs_transpose(*pend)
                pcb = P.next()
                for j in range(4):
                    jb = slice(j * 128, (j + 1) * 128)
                    mm(pcb[:, jb], BT[:, g, jb], CT[:, g, jb], True, True)
                tt('dve', cbm[:], pcb[:].rearrange("p (j l) -> p j l", j=4),
                   MASKT.unsqueeze(1).to_broadcast([128, 4, 128]), ALU.mult)

            def S1a(g, j, k):
                S.tag = 'p3_s1'
                hs = slice(8 * g, 8 * g + 8)
                jb = slice(j * 128, (j + 1) * 128)
                xs3 = xs_tok[:, j, :].rearrange("p (h d) -> p h d", h=8)
                tt('dve', xsd[:, k, :].rearrange("p (h d) -> p h d", h=8), xs3,
                   dtt[:, j, hs].unsqueeze(2).to_broadcast([128, 8, 64]), ALU.mult)
                tt('dve', xsw[:, k, :].rearrange("p (h d) -> p h d", h=8), xs3,
                   w2[:, j, hs].unsqueeze(2).to_broadcast([128, 8, 64]), ALU.mult)
                tt('dve', t2b[:, k, :].rearrange("p (h d) -> p h d", h=8), xs3,
                   rowc[:, l, 2, hs].unsqueeze(2).to_broadcast([128, 8, 64]), ALU.mult)
                Ab = Apr2[k].bitcast(BF16)
                for hl in range(2):
                    tt('dve', Ab[:, hl * 1024:(hl + 1) * 1024].rearrange("p (h l) -> p h l", h=8),
                       uinclb[:].unsqueeze(1).to_broadcast([128, 8, 128]),
                       ahl[:, hl, :].rearrange("p (j h) -> p j h", j=4)[:, j, hs].unsqueeze(2).to_broadcast([128, 8, 128]), ALU.mult)
                Es = []
                for half in range(2):
                    pseg = P.next()
                    mm(pseg[:], ustrb[:], Ab[:, half * 512:(half + 1) * 512], True, False)
                    mm(pseg[:], ustrb[:], Ab[:, 1024 + half * 512:1024 + (half + 1) * 512], False, True)
                    E = LNB.next()
                    act(E, pseg[:], AF.Exp)
                    Es.append(E)
                return Es

            def S1b(g, j, k, Es):
                S.tag = 'p3_s1'
                for half in range(2):
                    tt('dve', MT2[k][:, 4 * half:4 * half + 4, :], Es[half].rearrange("p (h l) -> p h l", h=4),
                       cbm[:, j, :].unsqueeze(1).to_broadcast([128, 4, 128]), ALU.mult)

            def S2a(g, j, k):
                S.tag = 'p3_s2'
                hs = slice(8 * g, 8 * g + 8)
                jb = slice(j * 128, (j + 1) * 128)
                xd = xsd[:, k, :]
                xw = xsw[:, k, :]
                pyd = P.next()
                for hh in range(8):
                    mm(pyd[:, hh * 64:(hh + 1) * 64], MT2[k][:, hh, :], xd[:, hh * 64:(hh + 1) * 64], True, True)
                pyo = P.next()
                mm(pyo[:], CT[:, g, jb], STb[:, li, g, :], True, True)
                pst = P.next()
                mm(pst[:], Btok[:, j, g * 128:(g + 1) * 128], xw, True, True)
                t1 = TC.next()
                tt('dve', t1.rearrange("p (h d) -> p h d", h=8), pyo[:].rearrange("p (h d) -> p h d", h=8),
                   eacs[:, j, hs].unsqueeze(2).to_broadcast([128, 8, 64]), ALU.mult)
                Sg = ST[:, li, g, :]
                tt('dve', Sg.rearrange("p (h d) -> p h d", h=8), Sg.rearrange("p (h d) -> p h d", h=8),
                   cdec[:, j, hs].unsqueeze(2).to_broadcast([128, 8, 64]), ALU.mult)
                tt('dve', Sg, Sg, pst[:], ALU.add)
                cp('act', STb[:, li, g, :], Sg)
                tt('dve', t1, t1, pyd[:], ALU.add)
                tt('dve', t1, t1, t2b[:, k, :], ALU.add)
                tt('dve', t1, t1, zs[:, g % 2, j, :], ALU.mult)
                act(t2b[:, k, :], t1, AF.Square, accum=sml[:, 0:1])
                act(sml[:, 1:2], sml[:, 0:1], AF.Ln, bias=epsb[:, 1:2], scale=1.0 / 512.0, force_self=True)
                act(sml[:, 2:3], sml[:, 1:2], AF.Exp, scale=-0.5, force_self=True)
                yn = YN.next()
                act(yn, t1, AF.Copy, scale=sml[:, 2:3], force_self=True)
                return yn

            def S2b(g, j, k, yn):
                S.tag = 'p3_s2'
                jb = slice(j * 128, (j + 1) * 128)
                pty = P.next()
                pty16 = pty[:].bitcast(BF16)
                for ci in range(4):
                    tr(pty16[:, ci * 128:(ci + 1) * 128], yn[:, ci * 128:(ci + 1) * 128], identb[:])
                for ci in range(4):
                    act(ynT[:, 4 * g + ci, jb], pty16[:, ci * 128:(ci + 1) * 128], AF.Copy,
                        scale=PPl[:, NW + 4 * g + ci:NW + 4 * g + ci + 1])

            iters = [(g, j) for g in range(4) for j in range(4)]
            pend2 = None
            xs_z(0)
            Es = S1a(0, 0, 0)
            S1b(0, 0, 0, Es)
            for i, (g, j) in enumerate(iters):
                nxt = None
                if i + 1 < len(iters):
                    g2, j2 = iters[i + 1]
                    if j2 == 0:
                        xs_z(g2)
                    nxt = (g2, j2, (i + 1) % 2, S1a(g2, j2, (i + 1) % 2))
                if pend2 is not None:
                    S2b(*pend2)
                yn = S2a(g, j, i % 2)
                pend2 = (g, j, i % 2, yn)
                if nxt is not None:
                    S1b(*nxt)
                if i < 8:
                    conv_unit(i)
            S2b(*pend2)
            p1_ln()

            if dbg_stop == 6:
                return
            S.tag = 'p4_out'
            for ob in range(2):
                wga = wget()
                wco = wget()
                for ci in range(4):
                    pga = P.next()
                    for kc in range(8):
                        mm(pga[:], wga[:, kc, ci * 128:(ci + 1) * 128], hTb[:, kc, :], kc == 0, kc == 7)
                    sg = TA.next()
                    act(sg, pga[:], AF.Sigmoid)
                    pya = P.next()
                    for kc in range(8):
                        mm(pya[:], wco[:, kc, ci * 128:(ci + 1) * 128], sT[:, kc, :], kc == 0, kc == 7)
                    tt('dve', mixa[:, ci, :], pya[:], sg, ALU.mult)
                wsa = wget()
                held = [P.hold_next() for _ in range(4)]
                for ci in range(4):
                    for kc in range(8):
                        mm(held[ci][1][:], wsa[:, kc, ci * 128:(ci + 1) * 128], ynT[:, kc, :], kc == 0, False)
                wsb = wget()
                for ci in range(4):
                    for kc in range(8):
                        mm(held[ci][1][:], wsb[:, kc, ci * 128:(ci + 1) * 128], ynT[:, 8 + kc, :], False, kc == 7)
                wgb = wget()
                for ci in range(4):
                    pgb = P.next()
                    for kc in range(8):
                        mm(pgb[:], wgb[:, kc, ci * 128:(ci + 1) * 128], hTb[:, kc, :], kc == 0, kc == 7)
                    sg = TA.next()
                    act(sg, pgb[:], AF.Sigmoid)
                    t = TC.next()
                    tt('dve', t, held[ci][1][:], sg, ALU.mult)
                    tt('dve', mixTb[:, 4 * ob + ci, :], t, mixa[:, ci, :], ALU.add)
                    P.release(held[ci][0])
            for ob in range(2):
                wwo = wget()
                for ci in range(4):
                    oc = 4 * ob + ci
                    po = P.next()
                    for kc in range(8):
                        mm(po[:], wwo[:, kc, ci * 128:(ci + 1) * 128], mixTb[:, kc, :], kc == 0, kc == 7)
                    stt('dve', hT[:, oc, :], hT[:, oc, :], ALPHA, po[:], ALU.mult, ALU.add)
            S.tag = 'ln1'
            ln_hT(l, L1G, L1B)

            if dbg_stop == 7:
                return
            S.tag = 'p5_ffn'
            for b in range(6):
                wgt = wget()
                wvl = wget()
                nci = 4 if b < 5 else 2
                for ci in range(nci):
                    i = 4 * b + ci
                    pg = P.next()
                    for kc in range(8):
                        mm(pg[:], wgt[:, kc, ci * 128:(ci + 1) * 128], hTb[:, kc, :], kc == 0, kc == 7)
                    pv = P.next()
                    for kc in range(8):
                        mm(pv[:], wvl[:, kc, ci * 128:(ci + 1) * 128], hTb[:, kc, :], kc == 0, kc == 7)
                    ag = TB.next()
                    conv_from_psum(pg[:], UP, ffnh[:, li, i, :], FW + 3 * i, FB + i, 3, l, ag)
                    av = TB.next()
                    conv_from_psum(pv[:], UP, ffnh[:, li, 22 + i, :], FW + 3 * (22 + i), FB + 22 + i, 3, l, av)
                    sg = TA.next()
                    act(sg, ag, AF.Silu)
                    tt('dve', actT[:, i, :], sg, av, ALU.mult)
            for ob in range(2):
                held = [P.hold_next() for _ in range(4)]
                for ks in range(3):
                    wd = wget()
                    nk = 8 if ks < 2 else 6
                    for ci in range(4):
                        for kk in range(nk):
                            mm(held[ci][1][:], wd[:, kk, ci * 128:(ci + 1) * 128], actT[:, 8 * ks + kk, :],
                               ks == 0 and kk == 0, ks == 2 and kk == nk - 1)
                for ci in range(4):
                    oc = 4 * ob + ci
                    stt('dve', hT[:, oc, :], hT[:, oc, :], ALPHA, held[ci][1][:], ALU.mult, ALU.add)
                    P.release(held[ci][0])
            S.tag = 'ln2'
            ln_hT(l, L2G, L2B)

        out_toks = []
        groups = [[0, 1], [2, 3], [4, 5], [6, 7]]
        nsteps = NT + 1 if pipe else NT
        for t in range(nsteps):
            S.tag = 'io_in'
            tx = min(t, NT - 1)
            S.dma('sp', lambda e, tx=tx: e.dma_start(out=xio, in_=x_d[tx * TT:(tx + 1) * TT, :].rearrange("(j p) c -> p j c", p=128)),
                  'xld', reads=[x_d], writes=[xio])
            if pipe:
                ts('dve', xio, xio, sel[:, 0:1], None, ALU.mult)
                if t >= 1:
                    rv = recv_d[(t - 1) % 2]
                    S.dma('sp', lambda e, rv=rv: e.dma_start(out=xio2, in_=rv[0:TT, :].rearrange("(j p) c -> p j c", p=128)),
                          'rld', reads=[rv], writes=[xio2])
                    stt('dve', xio, xio2, sel[:, 1:2], xio, ALU.mult, ALU.add)
            for c in range(8):
                ptx = P.next()
                for j in range(4):
                    tr(ptx[:, j * 128:(j + 1) * 128], xio[:, j, c * 128:(c + 1) * 128], IDENT)
                cp('act', hT[:, c, :], ptx[:])
            if apply_ln_in and dbg_stop not in (11, 12):
                ln_hT(layers[0], LIG, LIB, blend=pipe)
            else:
                for c in range(8):
                    cp('act', hTb[:, c, :], hT[:, c, :])
            for li, l in enumerate(layers):
                if dbg_stop in (11, 12, 13):
                    continue
                layer_tile(li, l)
            if pipe and t == 0:
                for buf in (ST, STb, vhalo, xbch, ffnh):
                    flat = buf[:].rearrange("p a b c -> p (a b c)")
                    ts('dve', flat, flat, sel[:, 2:3], None, ALU.mult)
            S.tag = 'io_out'
            for j in range(4):
                for m in range(2):
                    pto = P.next()
                    for cc in range(4):
                        c = 4 * m + cc
                        tr(pto[:, cc * 128:(cc + 1) * 128], hT[:, c, j * 128:(j + 1) * 128], IDENT)
                    cp('act', xio[:, j, m * 512:(m + 1) * 512], pto[:])
            if pipe:
                if t < NT:
                    sd = send_d[t % 2]
                    rv = recv_d[t % 2]
                    S.dma('sp', lambda e, sd=sd: e.dma_start(out=sd.rearrange("(j p) c -> p j c", p=128), in_=xio),
                          'sst', reads=[xio], writes=[sd])
                    S.dma('pool', lambda e, sd=sd, rv=rv: e.collective_compute("AllGather", ALU.bypass, replica_groups=groups, ins=[sd], outs=[rv]),
                          'cc%d' % (t % 2), reads=[sd], writes=[rv], inc=1)
                if t >= 1:
                    ty = t - 1
                    tok = S.dma('sp', lambda e, ty=ty: e.dma_start(out=y_d[ty * TT:(ty + 1) * TT, :].rearrange("(j p) c -> p j c", p=128), in_=xio),
                                'yst', reads=[xio], writes=[('y', ty)])
                    out_toks.append(tok)
            else:
                tok = S.dma('sp', lambda e, t=t: e.dma_start(out=y_d[t * TT:(t + 1) * TT, :].rearrange("(j p) c -> p j c", p=128), in_=xio),
                            'yst', reads=[xio], writes=[('y', t)])
                out_toks.append(tok)
        S.wait_all('sp', out_toks[-1:])
        S.emit()
    return nc, S


def _pack_params(inp):
    pp = np.zeros((128, 2, NPP), np.float32)

    def fm(v, nch):
        return np.ascontiguousarray(v.reshape(nch, 128).T)
    for l in range(2):
        w = inp['conv_dw_w'][l]
        pp[:, l, CW:CW + 248] = w.T.reshape(8, 128, 31).transpose(1, 0, 2).reshape(128, 248)
        pp[:, l, CB:CB + 8] = fm(inp['conv_dw_b'][l], 8)
        pp[:, l, CG:CG + 8] = fm(inp['conv_ln_g'][l], 8)
        pp[:, l, CBE:CBE + 8] = fm(inp['conv_ln_b'][l], 8)
        w = inp['ssm_conv_w'][l]
        pp[:, l, SW:SW + 96] = w.T.reshape(24, 128, 4).transpose(1, 0, 2).reshape(128, 96)
        pp[:, l, SB:SB + 24] = fm(inp['ssm_conv_b'][l], 24)
        pp[:, l, NW:NW + 16] = fm(inp['ssm_norm_w'][l], 16)
        w = inp['ffn_dw_w'][l]
        pp[:, l, FW:FW + 132] = w.T.reshape(44, 128, 3).transpose(1, 0, 2).reshape(128, 132)
        pp[:, l, FB:FB + 44] = fm(inp['ffn_dw_b'][l], 44)
        pp[:, l, L1G:L1G + 8] = fm(inp['ln1_g'][l], 8)
        pp[:, l, L1B:L1B + 8] = fm(inp['ln1_b'][l], 8)
        pp[:, l, L2G:L2G + 8] = fm(inp['ln2_g'][l], 8)
        pp[:, l, L2B:L2B + 8] = fm(inp['ln2_b'][l], 8)
        pp[:, l, LIG:LIG + 8] = fm(inp['ln_in_g'], 8)
        pp[:, l, LIB:LIB + 8] = fm(inp['ln_in_b'], 8)
    rowp = np.zeros((2, 3, 32), np.float32)
    for l in range(2):
        rowp[l, 0] = inp['ssm_dt_bias'][l]
        rowp[l, 1] = inp['ssm_a_log'][l]
        rowp[l, 2] = inp['ssm_d'][l]
    cst = np.zeros((128, 5, 128), np.float32)
    cst[:, 0, :] = np.eye(128)
    cst[:, 1, :] = np.triu(np.ones((128, 128)))
    cst[:, 2, :] = 1.0
    cst[:, 3, :] = np.tril(np.ones((128, 128)), -1)
    cst[:, 4, :] = np.triu(np.ones((128, 128)))
    return pp, rowp, cst


_CACHE = {}


def _get_prog(T, layers, apply_ln_in):
    key = (T, tuple(layers), apply_ln_in)
    if key not in _CACHE:
        _CACHE[key] = build_program(T, list(layers), apply_ln_in)
    return _CACHE[key][0]


def run_layers(xs, inp, layers, apply_ln_in, n_cores):
    T = xs[0].shape[0]
    nc = _get_prog(T, layers, apply_ln_in)
    pp, rowp, cst = _pack_params(inp)
    wts = {k: np.ascontiguousarray(inp[k], dtype=np.float32) for k in
           ('w_in', 'w_conv_out', 'w_ssm_out', 'w_o', 'w_ffn_up', 'w_ffn_down')}
    in_maps = []
    for c in range(n_cores):
        m = dict(wts)
        m['x'] = np.ascontiguousarray(xs[c % len(xs)], dtype=np.float32)
        m['pp'] = pp
        m['rowp'] = rowp
        m['cst'] = cst
        in_maps.append(m)
    res = run_bass_kernel_spmd(nc, in_maps, core_ids=list(range(n_cores)))
    return [np.asarray(res.results[c]['y']) for c in range(len(xs))]


def run_pipe(xs, inp):
    T = xs[0].shape[0]
    key = (T, 'pipe')
    if key not in _CACHE:
        _CACHE[key] = build_program(T, [0], True, pipe=True)
    nc = _CACHE[key][0]
    pp, rowp, cst = _pack_params(inp)
    zeros = np.zeros((T, D), np.float32)
    in_maps = []
    for c in range(8):
        b, lc = c // 2, c % 2
        m = {k: np.ascontiguousarray(inp[k][lc:lc + 1], dtype=np.float32) for k in
             ('w_in', 'w_conv_out', 'w_ssm_out', 'w_o', 'w_ffn_up', 'w_ffn_down')}
        ppc = np.ascontiguousarray(pp[:, lc:lc + 1, :])
        sel = np.zeros((128, 4), np.float32)
        if lc == 0:
            m['x'] = np.ascontiguousarray(xs[b % len(xs)], dtype=np.float32)
            sel[:, 0] = 1.0
            sel[:, 2] = 1.0
        else:
            m['x'] = zeros
            sel[:, 1] = 1.0
            ppc[:, 0, LIG:LIG + 8] = 1.0
            ppc[:, 0, LIB:LIB + 8] = 0.0
        m['pp'] = ppc
        m['rowp'] = np.ascontiguousarray(rowp[lc:lc + 1])
        m['cst'] = cst
        m['sel'] = sel
        in_maps.append(m)
    res = run_bass_kernel_spmd(nc, in_maps, core_ids=list(range(8)))
    return [np.asarray(res.results[2 * b + 1]['y']) for b in range(len(xs))]


FUSED = True
PIPE = True


def kernel(**inputs):
    inp = {k: np.asarray(v) for k, v in inputs.items()}
    x = inp['x'].astype(np.float32)
    B = x.shape[0]
    xs = [x[b] for b in range(B)]
    if PIPE:
        ys = run_pipe(xs, inp)
    elif FUSED:
        ys = run_layers(xs, inp, (0, 1), True, 4)
    else:
        h = run_layers(xs, inp, (0,), True, 8)
        ys = run_layers(h, inp, (1,), False, 8)
    return np.stack(ys, 0).astype(np.float32)
```

```python
import contextlib
import numpy as np
import concourse.bass as bass
import concourse.mybir as mybir
from concourse.bass_utils import run_bass_kernel_spmd

F32 = mybir.dt.float32
BF16 = mybir.dt.bfloat16
ALU = mybir.AluOpType
AF = mybir.ActivationFunctionType

CELL = 512
D = 1024
IN_DIM = 9248
TT = 512
DEPTH = 2
ALPHA = float((2 * DEPTH) ** 0.25)
LN_EPS = 1e-5
RMS_EPS = 1e-5


def _dtsize(dt):
    s = str(dt)
    if '64' in s:
        return 8
    if '32' in s:
        return 4
    if '16' in s:
        return 2
    return 1


class Sched:
    ENGS = ('pe', 'act', 'dve', 'pool', 'sp')

    def __init__(self, nc, same_engine_sync=True):
        self.nc = nc
        self.same_engine_sync = same_engine_sync
        self.streams = {e: [] for e in self.ENGS}
        self.count = {e: 0 for e in self.ENGS}
        self.dma_count = {}
        self.dma_inc = {}
        self.cells = {}
        self.waited = {e: {} for e in self.ENGS}
        self.n_ops = 0
        self.tag = ''
        self.tags = {e: [] for e in self.ENGS}

    def _keys(self, r):
        if isinstance(r, tuple):
            return [r]
        ap = r
        name = ap.tensor.name
        sp_ = str(ap.space).upper()
        if 'DRAM' in sp_ or 'PSUM' in sp_:
            return [(name,)]
        pairs = ap.ap
        pstep = pairs[0][0]
        sz = _dtsize(ap.dtype)
        off = ap.offset % pstep if pstep > 0 else ap.offset
        hull = 0
        for (st, cn) in pairs[1:]:
            hull += abs(st) * (cn - 1)
        lo = off * sz
        hi = (off + hull + 1) * sz
        return [(name, c) for c in range(lo // CELL, (hi - 1) // CELL + 1)]

    def _deps(self, reads, writes):
        deps = {}
        rk = []
        for r in reads:
            rk += self._keys(r)
        wk = []
        for w in writes:
            wk += self._keys(w)
        cells = self.cells
        for k in rk:
            c = cells.get(k)
            if c is not None and c[0] is not None:
                kk, vv = c[0]
                if deps.get(kk, 0) < vv:
                    deps[kk] = vv
        for k in wk:
            c = cells.get(k)
            if c is not None:
                if c[0] is not None:
                    kk, vv = c[0]
                    if deps.get(kk, 0) < vv:
                        deps[kk] = vv
                for kk, vv in c[1].items():
                    if deps.get(kk, 0) < vv:
                        deps[kk] = vv
        return deps, rk, wk

    def _commit(self, rk, wk, tok):
        cells = self.cells
        for k in rk:
            c = cells.get(k)
            if c is None:
                c = [None, {}]
                cells[k] = c
            if c[1].get(tok[0], 0) < tok[1]:
                c[1][tok[0]] = tok[1]
        for k in wk:
            cells[k] = [tok, {}]

    def _filter(self, eng, deps, force_self=False):
        waits = []
        wd = self.waited[eng]
        for k, v in deps.items():
            if k == eng and (eng == 'pe' or not (self.same_engine_sync or force_self)):
                continue
            if wd.get(k, 0) >= v:
                continue
            wd[k] = v
            waits.append((k, v))
        return waits

    def op(self, eng, fn, reads=(), writes=(), force_self=False):
        deps, rk, wk = self._deps(reads, writes)
        waits = self._filter(eng, deps, force_self)
        self.count[eng] += 1
        tok = (eng, self.count[eng])
        self.tags[eng].append(self.tag)
        self.streams[eng].append((waits, fn, tok))
        self._commit(rk, wk, tok)
        self.n_ops += 1
        return tok

    def dma(self, queue, fn, sem, reads=(), writes=(), inc=16):
        deps, rk, wk = self._deps(reads, writes)
        waits = self._filter(queue, deps)
        self.dma_count[sem] = self.dma_count.get(sem, 0) + 1
        self.dma_inc[sem] = inc
        tok = (sem, inc * self.dma_count[sem])
        self.streams[queue].append((waits, fn, tok))
        self._commit(rk, wk, tok)
        self.n_ops += 1
        return tok

    def wait_all(self, eng, toks):
        deps = {}
        for k, v in toks:
            deps[k] = max(deps.get(k, 0), v)
        waits = self._filter(eng, deps)
        if waits:
            self.streams[eng].append((waits, None, None))

    def emit(self):
        nc = self.nc
        semnames = list(self.ENGS) + list(self.dma_count.keys())
        with contextlib.ExitStack() as st:
            sems = {}
            for n in semnames:
                sems[n] = st.enter_context(nc.semaphore("s_" + n))
            block = st.enter_context(nc.Block())
            engmap = {'pe': block.tensor, 'act': block.scalar, 'dve': block.vector,
                      'pool': block.gpsimd, 'sp': block.sync}

            def make(ename):
                stream = self.streams[ename]

                def body(e):
                    for waits, fn, tok in stream:
                        for (k, v) in waits:
                            e.wait_ge(sems[k], v)
                        if fn is None:
                            continue
                        ins = fn(e)
                        if tok[0] == ename:
                            ins.then_inc(sems[tok[0]], 1)
                        else:
                            ins.then_inc(sems[tok[0]], self.dma_inc[tok[0]])
                return body
            for ename in self.ENGS:
                if self.streams[ename]:
                    engmap[ename](make(ename))


CW, CB, CG, CBE = 0, 248, 256, 264
SW, SB, NW = 272, 368, 392
FW, FB = 408, 540
L1G, L1B, L2G, L2B = 584, 592, 600, 608
LIG, LIB = 616, 624
NPP = 632

def _slot_table():
    s = []
    A = lambda name, k0, nk, c0, ncol: s.append((name, k0, nk, c0, ncol))
    A('w_in', 0, 8, 0, 512)
    A('w_in', 0, 8, 1024, 512)
    A('w_in', 0, 8, 512, 512)
    A('w_in', 0, 8, 1536, 512)
    A('w_in', 0, 8, 7168, 32)
    A('w_in', 0, 8, 6144, 512)
    A('diag2', 0, 8, 0, 256)
    A('w_in', 0, 8, 6656, 512)
    A('diag2', 1, 8, 0, 256)
    def XZ(g):
        A('w_in', 0, 8, 4096 + 512 * g, 512)
        A('diag2', 2 + g, 8, 0, 256)
        A('w_in', 0, 8, 2048 + 512 * g, 512)
    XZ(0)
    for c in (0, 1, 2):
        A('diag', c, 8, 0, 496)
    XZ(1)
    for c in (3, 4, 5, 6):
        A('diag', c, 8, 0, 496)
    XZ(2)
    A('diag', 7, 8, 0, 496)
    XZ(3)
    for ob in range(2):
        A('w_in', 0, 8, 7200 + 512 * ob, 512)
        A('w_conv_out', 0, 8, 512 * ob, 512)
        A('w_ssm_out', 0, 8, 512 * ob, 512)
        A('w_ssm_out', 8, 8, 512 * ob, 512)
        A('w_in', 0, 8, 8224 + 512 * ob, 512)
    for ob in range(2):
        A('w_o', 0, 8, 512 * ob, 512)
    for b in range(6):
        ncol = 512 if b < 5 else 256
        A('w_ffn_up', 0, 8, 512 * b, ncol)
        A('w_ffn_up', 0, 8, 2816 + 512 * b, ncol)
    for ob in range(2):
        A('w_ffn_down', 0, 8, 512 * ob, 512)
        A('w_ffn_down', 8, 8, 512 * ob, 512)
        A('w_ffn_down', 16, 6, 512 * ob, 512)
    return s


SLOTS = _slot_table()
NSL = len(SLOTS)
NRING = 4
NPREP = 4


def build_program(T, layers, apply_ln_in, conv_pool_chunks=(), same_engine_sync=False, dbg_stop=0, dbg_var=0, use_pool=False, pipe=False):
    PL = 'pool' if use_pool else 'dve'
    NT = T // TT
    NLW = 1 if (pipe or len(layers) == 1) else 2
    NL = len(layers)
    nc = bass.Bass("TRN2", target_bir_lowering=False, num_devices=8) if pipe else bass.Bass("TRN2", target_bir_lowering=False)
    S = Sched(nc, same_engine_sync=same_engine_sync)

    def din(name, shape):
        return nc.dram_tensor(name, shape, F32, kind="ExternalInput").ap()
    x_d = din("x", [T, D])
    wdr = {
        'w_in': din("w_in", [NLW, D, IN_DIM]),
        'w_conv_out': din("w_conv_out", [NLW, D, D]),
        'w_ssm_out': din("w_ssm_out", [NLW, 2048, D]),
        'w_o': din("w_o", [NLW, D, D]),
        'w_ffn_up': din("w_ffn_up", [NLW, D, 5632]),
        'w_ffn_down': din("w_ffn_down", [NLW, 2816, D]),
    }
    pp_d = din("pp", [128, NLW, NPP])
    rowp_d = din("rowp", [NLW, 3, 32])
    cst_d = din("cst", [128, 5, 128])
    y_d = nc.dram_tensor("y", [T, D], F32, kind="ExternalOutput").ap()
    if pipe:
        sel_d = din("sel", [128, 4])
        send_d = [nc.dram_tensor("send%d" % i, [TT, D], F32, kind="Internal").ap() for i in range(2)]
        recv_d = [nc.dram_tensor("recv%d" % i, [2 * TT, D], F32, kind="Internal").ap() for i in range(2)]
    wscr = nc.dram_tensor("wscr", [NLW, NSL, 128, 4096], BF16, kind="Internal").ap()
    dscr = nc.dram_tensor("dscr", [NLW, 8, 128, 3968], BF16, kind="Internal").ap()
    dscr2 = nc.dram_tensor("dscr2", [NLW, 6, 128, 2048], BF16, kind="Internal").ap()

    with contextlib.ExitStack() as st:
        def sb(name, shape, dt=F32):
            return st.enter_context(nc.sbuf_tensor("sb_" + name, shape, dt))

        def psb(name):
            return st.enter_context(nc.psum_tensor(name, [128, 512], F32))
        cst = sb("cst", [128, 5, 128])
        IDENT, UINCL, ONES, USTRICT, MASKT = (cst[:, i, :] for i in range(5))
        identb = sb("identb", [128, 128], BF16)
        ustrb = sb("ustrb", [128, 128], BF16)
        uinclb = sb("uinclb", [128, 128], BF16)
        ahl = sb("ahl", [128, 2, 128], BF16)
        pp = sb("pp", [128, NLW, NPP])
        rowc = sb("rowc", [128, NLW, 3, 32])
        epsb = sb("epsb", [128, 2])
        sel = sb("sel", [128, 4])
        ST = sb("ST", [128, NL, 4, 512])
        STb = sb("STb", [128, NL, 4, 512], BF16)
        vhalo = sb("vhalo", [128, NL, 8, 30], BF16)
        xbch = sb("xbch", [128, NL, 24, 3], BF16)
        ffnh = sb("ffnh", [128, NL, 44, 2])
        hT = sb("hT", [128, 8, 512])
        hTb = sb("hTb", [128, 8, 512], BF16)
        wring = sb("wring", [128, NRING, 4096], BF16)
        arena = sb("arena", [128, 9472])
        ynT = sb("ynT", [128, 16, 512], BF16)
        stat = sb("stat", [128, 2, 512])
        sT = sb("sT", [128, 8, 512], BF16)
        BT = sb("BT", [128, 4, 512], BF16)
        CT = sb("CT", [128, 4, 512], BF16)
        Btok = sb("Btok", [128, 4, 512], BF16)
        dts = sb("dts", [128, 7, 128])
        xs_tok = sb("xs_tok", [128, 4, 512])
        zs = sb("zs", [128, 2, 4, 512], BF16)
        cbm = sb("cbm", [128, 4, 128], BF16)
        tA = sb("tA", [128, 2, 512])
        xsT2 = sb("xsT2", [128, 2, 512])
        t2b = sb("t2b", [128, 2, 512])
        tB = sb("tB", [128, 2, 512])
        lnb = sb("lnb", [128, 4, 512], BF16)
        onesNb = sb("onesNb", [128, 128], BF16)
        tC = sb("tC", [128, 2, 512])
        xpre = sb("xpre", [128, 2, 516])
        ynb = sb("ynb", [128, 2, 512], BF16)
        xsd = sb("xsd", [128, 2, 512], BF16)
        xsw = sb("xsw", [128, 2, 512], BF16)
        sml = sb("sml", [128, 8])
        banks = [psb("ps%d" % i) for i in range(8)]

        vT = arena[:, 0:4 * 542].bitcast(BF16).rearrange("p (c t) -> p c t", c=8)
        convT = arena[:, 2176:2176 + 4096].rearrange("p (c t) -> p c t", c=8)
        mixTb = arena[:, 4096:6144].bitcast(BF16).rearrange("p (c t) -> p c t", c=8)
        mixa = arena[:, 6144:8192].rearrange("p (c t) -> p c t", c=4)
        Apr2 = [arena[:, 6272 + 1024 * i:6272 + 1024 * (i + 1)] for i in range(2)]
        MT2 = [arena[:, 8320 + 512 * i:8320 + 512 * (i + 1)].bitcast(BF16).rearrange("p (h l) -> p h l", h=8)
               for i in range(2)]
        actT = arena[:, 0:5632].bitcast(BF16).rearrange("p (c t) -> p c t", c=22)
        upre = arena[:, 5632:5632 + 4 * 516].rearrange("p (c t) -> p c t", c=4)
        xio = arena[:, 4096:8192].rearrange("p (j c) -> p j c", j=4)
        xio2 = arena[:, 0:4096].rearrange("p (j c) -> p j c", j=4)

        class Cyc:
            def __init__(self, items):
                self.items = items
                self.i = 0
                self.held = set()

            def next(self):
                for _ in range(len(self.items) + 1):
                    k = self.i % len(self.items)
                    self.i += 1
                    if k not in self.held:
                        return self.items[k]
                raise RuntimeError("all held")

            def hold_next(self):
                for _ in range(len(self.items) + 1):
                    k = self.i % len(self.items)
                    self.i += 1
                    if k not in self.held:
                        self.held.add(k)
                        return k, self.items[k]
                raise RuntimeError("all held")

            def release(self, k):
                self.held.discard(k)
        P = Cyc(banks[:6])
        PS_MEAN, PS_EX2 = banks[6], banks[7]
        TA = Cyc([tA[:, i, :] for i in range(2)])
        XS = Cyc([xsT2[:, i, :] for i in range(2)])
        TB = Cyc([tB[:, i, :] for i in range(2)])
        LNB = Cyc([lnb[:, i, :] for i in range(4)])
        TC = Cyc([tC[:, i, :] for i in range(2)])
        XP = Cyc([xpre[:, i, :] for i in range(2)])
        _xpb = xpre[:].rearrange("p a b -> p (a b)").bitcast(BF16)
        XPB = Cyc([_xpb[:, i * 516:(i + 1) * 516] for i in range(4)])
        UP = Cyc([upre[:, i, :] for i in range(4)])
        YN = Cyc([ynb[:, i, :] for i in range(2)])
        XSD = Cyc([xsd[:, i, :] for i in range(2)])
        XSW = Cyc([xsw[:, i, :] for i in range(2)])

        def mm(out, lhsT, rhs, start, stop):
            S.op('pe', lambda e: e.matmul(out, lhsT=lhsT, rhs=rhs, start=start, stop=stop),
                 reads=[lhsT, rhs], writes=[out])

        def tr(out, in_, ident):
            S.op('pe', lambda e: e.transpose(out, in_, ident), reads=[in_, ident], writes=[out])

        def act(out, in_, func, bias=None, scale=None, accum=None, eng='act', force_self=False):
            kw = {}
            rd = [in_]
            wr = [out]
            if bias is not None:
                kw['bias'] = bias
                if not isinstance(bias, float):
                    rd.append(bias)
            if scale is not None:
                kw['scale'] = scale
                if not isinstance(scale, float):
                    rd.append(scale)
            if accum is not None:
                kw['accum_out'] = accum
                wr.append(accum)
            S.op('act', lambda e: e.activation(out=out, in_=in_, func=func, **kw), reads=rd, writes=wr, force_self=force_self)

        def tt(eng, out, in0, in1, op):
            S.op(eng, lambda e: e.tensor_tensor(out=out, in0=in0, in1=in1, op=op), reads=[in0, in1], writes=[out])

        def ts(eng, out, in0, s1, s2, op0, op1=None):
            rd = [in0] + [s for s in (s1, s2) if s is not None and not isinstance(s, float)]
            if op1 is None:
                S.op(eng, lambda e: e.tensor_scalar(out=out, in0=in0, scalar1=s1, scalar2=None, op0=op0), reads=rd, writes=[out])
            else:
                S.op(eng, lambda e: e.tensor_scalar(out=out, in0=in0, scalar1=s1, scalar2=s2, op0=op0, op1=op1), reads=rd, writes=[out])

        def stt(eng, out, in0, scalar, in1, op0, op1):
            rd = [in0, in1] + ([] if isinstance(scalar, float) else [scalar])
            S.op(eng, lambda e: e.scalar_tensor_tensor(out=out, in0=in0, scalar=scalar, in1=in1, op0=op0, op1=op1),
                 reads=rd, writes=[out])

        def cp(eng, out, in_, force_self=False):
            if eng == 'act':
                S.op('act', lambda e: e.copy(out=out, in_=in_), reads=[in_], writes=[out], force_self=force_self)
            else:
                S.op(eng, lambda e: e.tensor_copy(out=out, in_=in_), reads=[in_], writes=[out], force_self=force_self)

        def memset(eng, ap, v):
            S.op(eng, lambda e: e.memset(ap, v), writes=[ap])

        n_prep = 0
        for l in (layers if dbg_stop not in (11, 13) else []):
            for s, (wn, k0, nk, c0, ncol) in enumerate(SLOTS):
                if wn in ('diag', 'diag2'):
                    continue
                src = wdr[wn][l, k0 * 128:(k0 + nk) * 128, c0:c0 + ncol].rearrange("(k p) c -> p k c", p=128)
                dst = wscr[l, s].rearrange("p (k c) -> p k c", k=8)[:, 0:nk, 0:ncol]
                psem = 'prep%d' % (n_prep % NPREP)
                if n_prep >= NPREP:
                    S.wait_all('pool', [(psem, 16 * (n_prep // NPREP))])
                S.dma('pool', lambda e, src=src, dst=dst: e.dma_start(out=dst, in_=src), psem,
                      reads=[], writes=[('wscr', l, s)])
                n_prep += 1

        if n_prep:
            S.wait_all('pool', [('prep%d' % i, 16 * ((n_prep - 1 - i) // NPREP + 1)) for i in range(min(NPREP, n_prep))])
        uses = [(l, s) for t in range(NT + (1 if pipe else 0)) for l in layers for s in range(NSL)]
        wstate = {'issued': 0, 'n': 0}

        def wget():
            n = wstate['n']
            while wstate['issued'] < min(len(uses), max(n + NRING - 1, 2)):
                m = wstate['issued']
                l, s = uses[m]
                wn_, k0_, nk_, _, ncol_ = SLOTS[s]
                if wn_ == 'diag':
                    dst = wring[:, m % NRING, 0:3968]
                    src = dscr[l, k0_]
                    rkey = ('dscr', l, k0_)
                elif wn_ == 'diag2':
                    dst = wring[:, m % NRING, 0:2048]
                    src = dscr2[l, k0_]
                    rkey = ('dscr2', l, k0_)
                else:
                    dst = wring[:, m % NRING, :].rearrange("p (k c) -> p k c", k=8)[:, 0:nk_, 0:ncol_]
                    src = wscr[l, s].rearrange("p (k c) -> p k c", k=8)[:, 0:nk_, 0:ncol_]
                    rkey = ('wscr', l, s)
                S.dma('sp', lambda e, src=src, dst=dst: e.dma_start(out=dst, in_=src), 'wld%d' % (m % NRING),
                      reads=[rkey], writes=[wring[:, m % NRING, :]])
                wstate['issued'] += 1
            wstate['n'] += 1
            if SLOTS[uses[n][1]][0] == 'diag':
                return wring[:, n % NRING, 0:3968].rearrange("p (k m) -> p k m", k=31)
            if SLOTS[uses[n][1]][0] == 'diag2':
                return wring[:, n % NRING, 0:2048].rearrange("p (k m) -> p k m", k=16)
            return wring[:, n % NRING, :].rearrange("p (k c) -> p k c", k=8)

        S.dma('sp', lambda e: e.dma_start(out=cst[:], in_=cst_d), 'ld0', reads=[cst_d], writes=[cst[:]])
        S.dma('sp', lambda e: e.dma_start(out=pp[:], in_=pp_d), 'ld1', reads=[pp_d], writes=[pp[:]])
        S.dma('sp', lambda e: e.dma_start(out=rowc[:].rearrange("p a b c -> p (a b c)"),
                                          in_=rowp_d.rearrange("a b c -> (a b c)").partition_broadcast(128)),
              'ld2', reads=[rowp_d], writes=[rowc[:]])
        if pipe:
            S.dma('sp', lambda e: e.dma_start(out=sel[:], in_=sel_d), 'ld3', reads=[sel_d], writes=[sel[:]])
        cp('dve', identb[:], IDENT)
        cp('dve', ustrb[:], USTRICT)
        cp('dve', uinclb[:], UINCL)
        ts('dve', onesNb[:], ONES, 1.0 / 1024.0, None, ALU.mult)
        memset('dve', epsb[:, 0:1], LN_EPS)
        memset('dve', epsb[:, 1:2], RMS_EPS)
        memset('dve', ST[:], 0.0)
        memset('dve', STb[:], 0.0)
        memset('dve', vhalo[:], 0.0)
        memset('dve', xbch[:], 0.0)
        memset('dve', ffnh[:], 0.0)
        for l in layers:
            act(rowc[:, l, 1, :], rowc[:, l, 1, :], AF.Exp)
            ts('dve', rowc[:, l, 1, :], rowc[:, l, 1, :], -1.0, None, ALU.mult)

        nbuilt = 0
        for l in layers:
            for c in range(8):
                stg = wring[:, nbuilt % NRING, 0:3968]
                tt('dve', stg.rearrange("p (k m) -> p k m", k=31),
                   IDENT.unsqueeze(1).to_broadcast([128, 31, 128]),
                   pp[:, l, CW + c * 31:CW + c * 31 + 31].unsqueeze(2).to_broadcast([128, 31, 128]), ALU.mult)
                S.dma('sp', lambda e, stg=stg, l=l, c=c: e.dma_start(out=dscr[l, c], in_=stg), 'dst%d' % (nbuilt % NRING),
                      reads=[stg], writes=[('dscr', l, c)])
                nbuilt += 1

        for l in layers:
            for b6 in range(6):
                q0 = (16, 20, 0, 4, 8, 12)[b6]
                stg = wring[:, nbuilt % NRING, 0:2048]
                tt('dve', stg.rearrange("p (k m) -> p k m", k=16),
                   IDENT.unsqueeze(1).to_broadcast([128, 16, 128]),
                   pp[:, l, SW + 4 * q0:SW + 4 * q0 + 16].unsqueeze(2).to_broadcast([128, 16, 128]), ALU.mult)
                S.dma('sp', lambda e, stg=stg, l=l, b6=b6: e.dma_start(out=dscr2[l, b6], in_=stg), 'dst%d' % (nbuilt % NRING),
                      reads=[stg], writes=[('dscr2', l, b6)])
                nbuilt += 1

        def ln_stats_begin():
            pass

        def ln_finish_stats(blend=False):
            mean_b = stat[:, 0, :]
            rstd_b = stat[:, 1, :]
            cp('act', mean_b, PS_MEAN[:])
            t = TC.next()
            tt('dve', t, mean_b, mean_b, ALU.mult)
            tt('dve', t, PS_EX2[:], t, ALU.subtract)
            act(t, t, AF.Sqrt, bias=epsb[:, 0:1], scale=1.0)
            S.op('dve', lambda e: e.reciprocal(out=rstd_b, in_=t), reads=[t], writes=[rstd_b])
            if blend:
                ts('dve', mean_b, mean_b, sel[:, 0:1], None, ALU.mult)
                ts('dve', rstd_b, rstd_b, sel[:, 0:1], sel[:, 1:2], ALU.mult, ALU.add)
            return mean_b, rstd_b

        def ln_hT(l, gcol, bcol, blend=False):
            for c in range(8):
                sq = LNB.next()
                act(sq, hT[:, c, :], AF.Square)
                rb = LNB.next()
                cp('act', rb, hT[:, c, :])
                mm(PS_MEAN[:], onesNb[:], rb, c == 0, c == 7)
                mm(PS_EX2[:], onesNb[:], sq, c == 0, c == 7)
            mean_b, rstd_b = ln_finish_stats(blend)
            for c in range(8):
                t = TC.next()
                tt('dve', t, hT[:, c, :], mean_b, ALU.subtract)
                tt('dve', t, t, rstd_b, ALU.mult)
                act(hT[:, c, :], t, AF.Identity, bias=pp[:, l, bcol + c:bcol + c + 1], scale=pp[:, l, gcol + c:gcol + c + 1])
                cp('act', hTb[:, c, :], hT[:, c, :])

        def conv_from_psum(ps, pool_cyc, halo, wcol, bcol, K, l, out_acc, act_tap=False):
            xp = pool_cyc.next()
            H = K - 1
            cp('act', xp[:, H:H + 512], ps)
            cp('act', xp[:, 0:H], halo)
            cp('act', halo, xp[:, 512:512 + H])
            if act_tap:
                act(out_acc, ps, AF.Identity, bias=pp[:, l, bcol:bcol + 1], scale=pp[:, l, wcol + H:wcol + H + 1])
            else:
                ts('dve', out_acc, xp[:, H:H + 512], pp[:, l, wcol + H:wcol + H + 1], pp[:, l, bcol:bcol + 1], ALU.mult, ALU.add)
            for k in range(H):
                stt('dve', out_acc, xp[:, k:k + 512], pp[:, l, wcol + k:wcol + k + 1], out_acc, ALU.mult, ALU.add)

        def layer_tile(li, l):
            PPl = pp[:, l, :]
            if dbg_stop == 10:
                return
            S.tag = 'p1_conformer'
            cp('act', vT[:, :, 0:30], vhalo[:, li, :, :])
            for cb in range(2):
                wa = wget()
                wg = wget()
                for ci in range(4):
                    c = 4 * cb + ci
                    pa = P.next()
                    for kc in range(8):
                        mm(pa[:], wa[:, kc, ci * 128:(ci + 1) * 128], hTb[:, kc, :], kc == 0, kc == 7)
                    pg = P.next()
                    for kc in range(8):
                        mm(pg[:], wg[:, kc, ci * 128:(ci + 1) * 128], hTb[:, kc, :], kc == 0, kc == 7)
                    sig = TA.next()
                    act(sig, pg[:], AF.Sigmoid)
                    tt('dve', vT[:, c, 30:542], pa[:], sig, ALU.mult)
                    cp('act', vhalo[:, li, c, :], vT[:, c, 512:542])

            def conv_unit(c):
                S.tag = 'p1_conv'
                wd_ = wget()
                pc = P.next()
                for k in range(31):
                    mm(pc[:], wd_[:, k, :], vT[:, c, k:k + 512], k == 0, k == 30)
                act(convT[:, c, :], pc[:], AF.Identity, bias=PPl[:, CB + c:CB + c + 1], scale=1.0)
                rb = LNB.next()
                act(rb, pc[:], AF.Identity, bias=PPl[:, CB + c:CB + c + 1], scale=1.0)
                sq = LNB.next()
                act(sq, convT[:, c, :], AF.Square)
                mm(PS_MEAN[:], onesNb[:], rb, c == 0, c == 7)
                mm(PS_EX2[:], onesNb[:], sq, c == 0, c == 7)

            p1s = {}

            def p1_ln_part(part):
                S.tag = 'p1_ln'
                if part == 0:
                    p1s['m'], p1s['r'] = ln_finish_stats()
                    return
                for c in (2 * (part - 1), 2 * (part - 1) + 1):
                    t = TC.next()
                    tt('dve', t, convT[:, c, :], p1s['m'], ALU.subtract)
                    tt('dve', t, t, p1s['r'], ALU.mult)
                    act(sT[:, c, :], t, AF.Silu, bias=PPl[:, CBE + c:CBE + c + 1], scale=PPl[:, CG + c:CG + c + 1])

            if dbg_stop == 1:
                return
            S.tag = 'p2a_dtBC'
            dtt = dts[:, 0, :].rearrange("p (j h) -> p j h", j=4)
            aa = dts[:, 1, :].rearrange("p (j h) -> p j h", j=4)
            acs = dts[:, 2, :].rearrange("p (j h) -> p j h", j=4)
            eacs = dts[:, 3, :].rearrange("p (j h) -> p j h", j=4)
            cdec = dts[:, 4, :].rearrange("p (j h) -> p j h", j=4)
            w2 = dts[:, 5, :].rearrange("p (j h) -> p j h", j=4)
            dtmp = dts[:, 6, :].rearrange("p (j h) -> p j h", j=4)
            wdt = wget()
            pdt = P.next()
            for j in range(4):
                for kc in range(8):
                    mm(pdt[:, j * 32:(j + 1) * 32], hTb[:, kc, j * 128:(j + 1) * 128], wdt[:, kc, 0:32], kc == 0, kc == 7)
            tt('dve', dtmp, pdt[:, 0:128].rearrange("p (j h) -> p j h", j=4),
               rowc[:, l, 0, :].unsqueeze(1).to_broadcast([128, 4, 32]), ALU.add)
            act(dtmp, dtmp, AF.Exp, force_self=True)
            act(dtt, dtmp, AF.Ln, bias=1.0, scale=1.0, force_self=True)
            tt('dve', aa, dtt, rowc[:, l, 1, :].unsqueeze(1).to_broadcast([128, 4, 32]), ALU.mult)
            if dbg_stop == 41:
                return
            pcs = P.next()
            mm(pcs[:, 0:128], UINCL, dts[:, 1, :], True, True)
            mm(pcs[:, 128:256], ONES, dts[:, 1, :], True, True)
            pcs_cs = pcs[:, 0:128].rearrange("p (j h) -> p j h", j=4)
            pcs_tot = pcs[:, 128:256].rearrange("p (j h) -> p j h", j=4)
            if dbg_stop == 420:
                return
            cp('act', acs, pcs_cs, force_self=True)
            if dbg_stop == 421:
                return
            act(eacs, pcs_cs, AF.Exp, force_self=True)
            act(cdec, pcs_tot, AF.Exp, force_self=True)
            if dbg_stop == 422:
                return
            cp('act', dtmp, pcs_tot, force_self=True)
            S.op('dve', lambda e: e.tensor_tensor(out=dtmp, in0=dtmp, in1=acs, op=ALU.subtract), reads=[dtmp, acs], writes=[dtmp], force_self=True)
            if dbg_stop == 423:
                return
            act(dtmp, dtmp, AF.Exp, force_self=True)
            if dbg_stop == 424:
                return
            tt('dve', w2, dtt, dtmp, ALU.mult)
            a_hi = ahl[:, 0, :].rearrange("p (j h) -> p j h", j=4)
            a_lo = ahl[:, 1, :].rearrange("p (j h) -> p j h", j=4)
            S.op('dve', lambda e: e.tensor_copy(out=a_hi, in_=aa), reads=[aa], writes=[a_hi], force_self=True)
            S.op('dve', lambda e: e.tensor_tensor(out=dtmp, in0=aa, in1=a_hi, op=ALU.subtract), reads=[aa, a_hi], writes=[dtmp], force_self=True)
            S.op('dve', lambda e: e.tensor_copy(out=a_lo, in_=dtmp), reads=[dtmp], writes=[a_lo], force_self=True)

            def xbc_chunk(wblk, wdg, ci, q, dest):
                pX = P.next()
                for kc in range(8):
                    mm(pX[:], wblk[:, kc, ci * 128:(ci + 1) * 128], hTb[:, kc, :], kc == 0, kc == 7)
                xp = XPB.next()
                cp('act', xp[:, 3:515], pX[:])
                cp('act', xp[:, 0:3], xbch[:, li, q, :])
                cp('act', xbch[:, li, q, :], xp[:, 512:515])
                pc = P.next()
                for k in range(4):
                    mm(pc[:], wdg[:, ci * 4 + k, :], xp[:, k:k + 512], k == 0, k == 3)
                act(dest, pc[:], AF.Silu, bias=pp[:, l, SB + q:SB + q + 1], scale=1.0)

            if dbg_stop == 42:
                return
            def b_transpose(g):
                ptb = P.next()
                ptb16 = ptb[:].bitcast(BF16)
                for j in range(4):
                    tr(ptb16[:, j * 128:(j + 1) * 128], BT[:, g, j * 128:(j + 1) * 128], identb[:])
                cp('act', Btok[:, :, g * 128:(g + 1) * 128], ptb16[:, 0:512].rearrange("p (j n) -> p j n", j=4))
            wB = wget()
            dgB = wget()
            for g in range(4):
                xbc_chunk(wB, dgB, g, 16 + g, BT[:, g, :])
                if g > 0:
                    b_transpose(g - 1)
            wC = wget()
            dgC = wget()
            for g in range(4):
                xbc_chunk(wC, dgC, g, 20 + g, CT[:, g, :])
                if g == 0:
                    b_transpose(3)

            if dbg_stop == 4:
                return
            def xs_transpose(xsT, ci):
                ptx = P.next()
                for j in range(4):
                    tr(ptx[:, j * 128:(j + 1) * 128], xsT[:, j * 128:(j + 1) * 128], IDENT)
                cp('act', xs_tok[:, :, ci * 128:(ci + 1) * 128], ptx[:].rearrange("p (j n) -> p j n", j=4))

            def xs_z(g):
                S.tag = 'p3_xs_z'
                wx = wget()
                dgx = wget()
                pend = None
                for ci in range(4):
                    xsT = XS.next()
                    xbc_chunk(wx, dgx, ci, 4 * g + ci, xsT)
                    if pend is not None:
                        xs_transpose(*pend)
                    pend = (xsT, ci)
                wz = wget()
                for j in range(4):
                    pz = P.next()
                    for kc in range(8):
                        mm(pz[:], hTb[:, kc, j * 128:(j + 1) * 128], wz[:, kc, :], kc == 0, kc == 7)
                    act(zs[:, g % 2, j, :], pz[:], AF.Silu)
                    if j == 0:
                        xs_transpose(*pend)
                pcb = P.next()
                for j in range(4):
                    jb = slice(j * 128, (j + 1) * 128)
                    mm(pcb[:, jb], BT[:, g, jb], CT[:, g, jb], True, True)
                tt('dve', cbm[:], pcb[:].rearrange("p (j l) -> p j l", j=4),
                   MASKT.unsqueeze(1).to_broadcast([128, 4, 128]), ALU.mult)

            def S1a(g, j, k):
                S.tag = 'p3_s1'
                hs = slice(8 * g, 8 * g + 8)
                jb = slice(j * 128, (j + 1) * 128)
                xs3 = xs_tok[:, j, :].rearrange("p (h d) -> p h d", h=8)
                tt('dve', xsd[:, k, :].rearrange("p (h d) -> p h d", h=8), xs3,
                   dtt[:, j, hs].unsqueeze(2).to_broadcast([128, 8, 64]), ALU.mult)
                tt('dve', xsw[:, k, :].rearrange("p (h d) -> p h d", h=8), xs3,
                   w2[:, j, hs].unsqueeze(2).to_broadcast([128, 8, 64]), ALU.mult)
                tt('dve', t2b[:, k, :].rearrange("p (h d) -> p h d", h=8), xs3,
                   rowc[:, l, 2, hs].unsqueeze(2).to_broadcast([128, 8, 64]), ALU.mult)
                Ab = Apr2[k].bitcast(BF16)
                for hl in range(2):
                    tt('dve', Ab[:, hl * 1024:(hl + 1) * 1024].rearrange("p (h l) -> p h l", h=8),
                       uinclb[:].unsqueeze(1).to_broadcast([128, 8, 128]),
                       ahl[:, hl, :].rearrange("p (j h) -> p j h", j=4)[:, j, hs].unsqueeze(2).to_broadcast([128, 8, 128]), ALU.mult)
                Es = []
                for half in range(2):
                    pseg = P.next()
                    mm(pseg[:], ustrb[:], Ab[:, half * 512:(half + 1) * 512], True, False)
                    mm(pseg[:], ustrb[:], Ab[:, 1024 + half * 512:1024 + (half + 1) * 512], False, True)
                    E = LNB.next()
                    act(E, pseg[:], AF.Exp)
                    Es.append(E)
                return Es

            def S1b(g, j, k, Es):
                S.tag = 'p3_s1'
                for half in range(2):
                    tt('dve', MT2[k][:, 4 * half:4 * half + 4, :], Es[half].rearrange("p (h l) -> p h l", h=4),
                       cbm[:, j, :].unsqueeze(1).to_broadcast([128, 4, 128]), ALU.mult)

            def S2a(g, j, k):
                S.tag = 'p3_s2'
                hs = slice(8 * g, 8 * g + 8)
                jb = slice(j * 128, (j + 1) * 128)
                xd = xsd[:, k, :]
                xw = xsw[:, k, :]
                pyd = P.next()
                for hh in range(8):
                    mm(pyd[:, hh * 64:(hh + 1) * 64], MT2[k][:, hh, :], xd[:, hh * 64:(hh + 1) * 64], True, True)
                pyo = P.next()
                mm(pyo[:], CT[:, g, jb], STb[:, li, g, :], True, True)
                pst = P.next()
                mm(pst[:], Btok[:, j, g * 128:(g + 1) * 128], xw, True, True)
                t1 = TC.next()
                tt('dve', t1.rearrange("p (h d) -> p h d", h=8), pyo[:].rearrange("p (h d) -> p h d", h=8),
                   eacs[:, j, hs].unsqueeze(2).to_broadcast([128, 8, 64]), ALU.mult)
                Sg = ST[:, li, g, :]
                tt('dve', Sg.rearrange("p (h d) -> p h d", h=8), Sg.rearrange("p (h d) -> p h d", h=8),
                   cdec[:, j, hs].unsqueeze(2).to_broadcast([128, 8, 64]), ALU.mult)
                tt('dve', Sg, Sg, pst[:], ALU.add)
                cp('act', STb[:, li, g, :], Sg)
                tt('dve', t1, t1, pyd[:], ALU.add)
                tt('dve', t1, t1, t2b[:, k, :], ALU.add)
                tt('dve', t1, t1, zs[:, g % 2, j, :], ALU.mult)
                act(t2b[:, k, :], t1, AF.Square, accum=sml[:, 0:1])
                act(sml[:, 1:2], sml[:, 0:1], AF.Ln, bias=epsb[:, 1:2], scale=1.0 / 512.0, force_self=True)
                act(sml[:, 2:3], sml[:, 1:2], AF.Exp, scale=-0.5, force_self=True)
                yn = YN.next()
                act(yn, t1, AF.Copy, scale=sml[:, 2:3], force_self=True)
                return yn

            def S2b(g, j, k, yn):
                S.tag = 'p3_s2'
                jb = slice(j * 128, (j + 1) * 128)
                pty = P.next()
                pty16 = pty[:].bitcast(BF16)
                for ci in range(4):
                    tr(pty16[:, ci * 128:(ci + 1) * 128], yn[:, ci * 128:(ci + 1) * 128], identb[:])
                for ci in range(4):
                    act(ynT[:, 4 * g + ci, jb], pty16[:, ci * 128:(ci + 1) * 128], AF.Copy,
                        scale=PPl[:, NW + 4 * g + ci:NW + 4 * g + ci + 1])

            iters = [(g, j) for g in range(4) for j in range(4)]
            pend2 = None
            xs_z(0)
            Es = S1a(0, 0, 0)
            S1b(0, 0, 0, Es)
            for i, (g, j) in enumerate(iters):
                nxt = None
                if i + 1 < len(iters):
                    g2, j2 = iters[i + 1]
                    if j2 == 0:
                        xs_z(g2)
                    nxt = (g2, j2, (i + 1) % 2, S1a(g2, j2, (i + 1) % 2))
                if pend2 is not None:
                    S2b(*pend2)
                yn = S2a(g, j, i % 2)
                pend2 = (g, j, i % 2, yn)
                if nxt is not None:
                    S1b(*nxt)
                if i < 8:
                    conv_unit(i)
                elif i < 13:
                    p1_ln_part(i - 8)
            S2b(*pend2)

            if dbg_stop == 6:
                return
            S.tag = 'p4_out'
            for ob in range(2):
                wga = wget()
                wco = wget()
                for ci in range(4):
                    pga = P.next()
                    for kc in range(8):
                        mm(pga[:], wga[:, kc, ci * 128:(ci + 1) * 128], hTb[:, kc, :], kc == 0, kc == 7)
                    sg = TA.next()
                    act(sg, pga[:], AF.Sigmoid)
                    pya = P.next()
                    for kc in range(8):
                        mm(pya[:], wco[:, kc, ci * 128:(ci + 1) * 128], sT[:, kc, :], kc == 0, kc == 7)
                    tt('dve', mixa[:, ci, :], pya[:], sg, ALU.mult)
                wsa = wget()
                held = [P.hold_next() for _ in range(4)]
                for ci in range(4):
                    for kc in range(8):
                        mm(held[ci][1][:], wsa[:, kc, ci * 128:(ci + 1) * 128], ynT[:, kc, :], kc == 0, False)
                wsb = wget()
                for ci in range(4):
                    for kc in range(8):
                        mm(held[ci][1][:], wsb[:, kc, ci * 128:(ci + 1) * 128], ynT[:, 8 + kc, :], False, kc == 7)
                wgb = wget()
                for ci in range(4):
                    pgb = P.next()
                    for kc in range(8):
                        mm(pgb[:], wgb[:, kc, ci * 128:(ci + 1) * 128], hTb[:, kc, :], kc == 0, kc == 7)
                    sg = TA.next()
                    act(sg, pgb[:], AF.Sigmoid)
                    t = TC.next()
                    tt('dve', t, held[ci][1][:], sg, ALU.mult)
                    tt('dve', mixTb[:, 4 * ob + ci, :], t, mixa[:, ci, :], ALU.add)
                    P.release(held[ci][0])
            for ob in range(2):
                wwo = wget()
                for ci in range(4):
                    oc = 4 * ob + ci
                    po = P.next()
                    for kc in range(8):
                        mm(po[:], wwo[:, kc, ci * 128:(ci + 1) * 128], mixTb[:, kc, :], kc == 0, kc == 7)
                    stt('dve', hT[:, oc, :], hT[:, oc, :], ALPHA, po[:], ALU.mult, ALU.add)
            S.tag = 'ln1'
            ln_hT(l, L1G, L1B)

            if dbg_stop == 7:
                return
            S.tag = 'p5_ffn'
            for b in range(6):
                wgt = wget()
                wvl = wget()
                nci = 4 if b < 5 else 2
                for ci in range(nci):
                    i = 4 * b + ci
                    pg = P.next()
                    for kc in range(8):
                        mm(pg[:], wgt[:, kc, ci * 128:(ci + 1) * 128], hTb[:, kc, :], kc == 0, kc == 7)
                    pv = P.next()
                    for kc in range(8):
                        mm(pv[:], wvl[:, kc, ci * 128:(ci + 1) * 128], hTb[:, kc, :], kc == 0, kc == 7)
                    ag = TB.next()
                    conv_from_psum(pg[:], UP, ffnh[:, li, i, :], FW + 3 * i, FB + i, 3, l, ag)
                    av = TB.next()
                    conv_from_psum(pv[:], UP, ffnh[:, li, 22 + i, :], FW + 3 * (22 + i), FB + 22 + i, 3, l, av)
                    sg = TA.next()
                    act(sg, ag, AF.Silu)
                    tt('dve', actT[:, i, :], sg, av, ALU.mult)
            for ob in range(2):
                held = [P.hold_next() for _ in range(4)]
                for ks in range(3):
                    wd = wget()
                    nk = 8 if ks < 2 else 6
                    for ci in range(4):
                        for kk in range(nk):
                            mm(held[ci][1][:], wd[:, kk, ci * 128:(ci + 1) * 128], actT[:, 8 * ks + kk, :],
                               ks == 0 and kk == 0, ks == 2 and kk == nk - 1)
                for ci in range(4):
                    oc = 4 * ob + ci
                    stt('dve', hT[:, oc, :], hT[:, oc, :], ALPHA, held[ci][1][:], ALU.mult, ALU.add)
                    P.release(held[ci][0])
            S.tag = 'ln2'
            ln_hT(l, L2G, L2B)

        out_toks = []
        groups = [[0, 1], [2, 3], [4, 5], [6, 7]]
        nsteps = NT + 1 if pipe else NT
        for t in range(nsteps):
            S.tag = 'io_in'
            tx = min(t, NT - 1)
            S.dma('sp', lambda e, tx=tx: e.dma_start(out=xio, in_=x_d[tx * TT:(tx + 1) * TT, :].rearrange("(j p) c -> p j c", p=128)),
                  'xld', reads=[x_d], writes=[xio])
            if pipe:
                ts('dve', xio, xio, sel[:, 0:1], None, ALU.mult)
                if t >= 1:
                    rv = recv_d[(t - 1) % 2]
                    S.dma('sp', lambda e, rv=rv: e.dma_start(out=xio2, in_=rv[0:TT, :].rearrange("(j p) c -> p j c", p=128)),
                          'rld', reads=[rv], writes=[xio2])
                    stt('dve', xio, xio2, sel[:, 1:2], xio, ALU.mult, ALU.add)
            for c in range(8):
                ptx = P.next()
                for j in range(4):
                    tr(ptx[:, j * 128:(j + 1) * 128], xio[:, j, c * 128:(c + 1) * 128], IDENT)
                cp('act', hT[:, c, :], ptx[:])
            if apply_ln_in and dbg_stop not in (11, 12):
                ln_hT(layers[0], LIG, LIB, blend=pipe)
            else:
                for c in range(8):
                    cp('act', hTb[:, c, :], hT[:, c, :])
            for li, l in enumerate(layers):
                if dbg_stop in (11, 12, 13):
                    continue
                layer_tile(li, l)
            if pipe and t == 0:
                for buf in (ST, STb, vhalo, xbch, ffnh):
                    flat = buf[:].rearrange("p a b c -> p (a b c)")
                    ts('dve', flat, flat, sel[:, 2:3], None, ALU.mult)
            S.tag = 'io_out'
            for j in range(4):
                for m in range(2):
                    pto = P.next()
                    for cc in range(4):
                        c = 4 * m + cc
                        tr(pto[:, cc * 128:(cc + 1) * 128], hT[:, c, j * 128:(j + 1) * 128], IDENT)
                    cp('act', xio[:, j, m * 512:(m + 1) * 512], pto[:])
            if pipe:
                if t < NT:
                    sd = send_d[t % 2]
                    rv = recv_d[t % 2]
                    S.dma('sp', lambda e, sd=sd: e.dma_start(out=sd.rearrange("(j p) c -> p j c", p=128), in_=xio),
                          'sst', reads=[xio], writes=[sd])
                    S.dma('pool', lambda e, sd=sd, rv=rv: e.collective_compute("AllGather", ALU.bypass, replica_groups=groups, ins=[sd], outs=[rv]),
                          'cc%d' % (t % 2), reads=[sd], writes=[rv], inc=1)
                if t >= 1:
                    ty = t - 1
                    tok = S.dma('sp', lambda e, ty=ty: e.dma_start(out=y_d[ty * TT:(ty + 1) * TT, :].rearrange("(j p) c -> p j c", p=128), in_=xio),
                                'yst', reads=[xio], writes=[('y', ty)])
                    out_toks.append(tok)
            else:
                tok = S.dma('sp', lambda e, t=t: e.dma_start(out=y_d[t * TT:(t + 1) * TT, :].rearrange("(j p) c -> p j c", p=128), in_=xio),
                            'yst', reads=[xio], writes=[('y', t)])
                out_toks.append(tok)
        S.wait_all('sp', out_toks[-1:])
        S.emit()
    return nc, S


def _pack_params(inp):
    pp = np.zeros((128, 2, NPP), np.float32)

    def fm(v, nch):
        return np.ascontiguousarray(v.reshape(nch, 128).T)
    for l in range(2):
        w = inp['conv_dw_w'][l]
        pp[:, l, CW:CW + 248] = w.T.reshape(8, 128, 31).transpose(1, 0, 2).reshape(128, 248)
        pp[:, l, CB:CB + 8] = fm(inp['conv_dw_b'][l], 8)
        pp[:, l, CG:CG + 8] = fm(inp['conv_ln_g'][l], 8)
        pp[:, l, CBE:CBE + 8] = fm(inp['conv_ln_b'][l], 8)
        w = inp['ssm_conv_w'][l]
        pp[:, l, SW:SW + 96] = w.T.reshape(24, 128, 4).transpose(1, 0, 2).reshape(128, 96)
        pp[:, l, SB:SB + 24] = fm(inp['ssm_conv_b'][l], 24)
        pp[:, l, NW:NW + 16] = fm(inp['ssm_norm_w'][l], 16)
        w = inp['ffn_dw_w'][l]
        pp[:, l, FW:FW + 132] = w.T.reshape(44, 128, 3).transpose(1, 0, 2).reshape(128, 132)
        pp[:, l, FB:FB + 44] = fm(inp['ffn_dw_b'][l], 44)
        pp[:, l, L1G:L1G + 8] = fm(inp['ln1_g'][l], 8)
        pp[:, l, L1B:L1B + 8] = fm(inp['ln1_b'][l], 8)
        pp[:, l, L2G:L2G + 8] = fm(inp['ln2_g'][l], 8)
        pp[:, l, L2B:L2B + 8] = fm(inp['ln2_b'][l], 8)
        pp[:, l, LIG:LIG + 8] = fm(inp['ln_in_g'], 8)
        pp[:, l, LIB:LIB + 8] = fm(inp['ln_in_b'], 8)
    rowp = np.zeros((2, 3, 32), np.float32)
    for l in range(2):
        rowp[l, 0] = inp['ssm_dt_bias'][l]
        rowp[l, 1] = inp['ssm_a_log'][l]
        rowp[l, 2] = inp['ssm_d'][l]
    cst = np.zeros((128, 5, 128), np.float32)
    cst[:, 0, :] = np.eye(128)
    cst[:, 1, :] = np.triu(np.ones((128, 128)))
    cst[:, 2, :] = 1.0
    cst[:, 3, :] = np.tril(np.ones((128, 128)), -1)
    cst[:, 4, :] = np.triu(np.ones((128, 128)))
    return pp, rowp, cst


_CACHE = {}


def _get_prog(T, layers, apply_ln_in):
    key = (T, tuple(layers), apply_ln_in)
    if key not in _CACHE:
        _CACHE[key] = build_program(T, list(layers), apply_ln_in)
    return _CACHE[key][0]


def run_layers(xs, inp, layers, apply_ln_in, n_cores):
    T = xs[0].shape[0]
    nc = _get_prog(T, layers, apply_ln_in)
    pp, rowp, cst = _pack_params(inp)
    wts = {k: np.ascontiguousarray(inp[k], dtype=np.float32) for k in
           ('w_in', 'w_conv_out', 'w_ssm_out', 'w_o', 'w_ffn_up', 'w_ffn_down')}
    in_maps = []
    for c in range(n_cores):
        m = dict(wts)
        m['x'] = np.ascontiguousarray(xs[c % len(xs)], dtype=np.float32)
        m['pp'] = pp
        m['rowp'] = rowp
        m['cst'] = cst
        in_maps.append(m)
    res = run_bass_kernel_spmd(nc, in_maps, core_ids=list(range(n_cores)))
    return [np.asarray(res.results[c]['y']) for c in range(len(xs))]


def run_pipe(xs, inp):
    T = xs[0].shape[0]
    key = (T, 'pipe')
    if key not in _CACHE:
        _CACHE[key] = build_program(T, [0], True, pipe=True)
    nc = _CACHE[key][0]
    pp, rowp, cst = _pack_params(inp)
    zeros = np.zeros((T, D), np.float32)
    in_maps = []
    for c in range(8):
        b, lc = c // 2, c % 2
        m = {k: np.ascontiguousarray(inp[k][lc:lc + 1], dtype=np.float32) for k in
             ('w_in', 'w_conv_out', 'w_ssm_out', 'w_o', 'w_ffn_up', 'w_ffn_down')}
        ppc = np.ascontiguousarray(pp[:, lc:lc + 1, :])
        sel = np.zeros((128, 4), np.float32)
        if lc == 0:
            m['x'] = np.ascontiguousarray(xs[b % len(xs)], dtype=np.float32)
            sel[:, 0] = 1.0
            sel[:, 2] = 1.0
        else:
            m['x'] = zeros
            sel[:, 1] = 1.0
            ppc[:, 0, LIG:LIG + 8] = 1.0
            ppc[:, 0, LIB:LIB + 8] = 0.0
        m['pp'] = ppc
        m['rowp'] = np.ascontiguousarray(rowp[lc:lc + 1])
        m['cst'] = cst
        m['sel'] = sel
        in_maps.append(m)
    res = run_bass_kernel_spmd(nc, in_maps, core_ids=list(range(8)))
    return [np.asarray(res.results[2 * b + 1]['y']) for b in range(len(xs))]


FUSED = True
PIPE = True


def kernel(**inputs):
    inp = {k: np.asarray(v) for k, v in inputs.items()}
    x = inp['x'].astype(np.float32)
    B = x.shape[0]
    xs = [x[b] for b in range(B)]
    if PIPE:
        ys = run_pipe(xs, inp)
    elif FUSED:
        ys = run_layers(xs, inp, (0, 1), True, 4)
    else:
        h = run_layers(xs, inp, (0,), True, 8)
        ys = run_layers(h, inp, (1,), False, 8)
    return np.stack(ys, 0).astype(np.float32)
```
